# Optimizing a Trainium2 kernel written in Bass

```python
import math
import jax, jax.numpy as jnp
from jax import lax
import numpy as np

D_MODEL = 2048
BATCH = 4
SEQ = 8192
DEPTH = 4
DEC_BATCH = 8
DEC_SEQ = 64
PAST_LEN = 1024

CHUNK = 64
Q_BLOCK = 128
N_AB = (DEPTH + 1) // 2
N_C = DEPTH // 2
MLA_V = 128
MLA_HEADS = (D_MODEL // 2) // MLA_V
MLA_NOPE = 128
MLA_ROPE = 64
Q_LORA = 512
KV_LORA = 256
MLA_SCALE = (MLA_NOPE + MLA_ROPE) ** -0.5
S5_WIDTH = D_MODEL // 2
S5_GROUP = 16
S5_GROUPS = S5_WIDTH // S5_GROUP
S5_STATE = 64
DT_MIN = 1e-3
DT_MAX = 1e-1
IN_AB = Q_LORA + KV_LORA + MLA_ROPE + S5_WIDTH
MIX_AB = MLA_HEADS * MLA_V + S5_WIDTH
RET_HEADS = 8
RET_DK = D_MODEL // RET_HEADS
RET_DV = 2 * RET_DK
IN_C = 2 * RET_HEADS * RET_DK + 2 * RET_HEADS * RET_DV
D_FF = 4 * D_MODEL
ROPE_THETA = 10000.0
EPS = 1e-6
GN_EPS = 1e-5

kernel_name = 'streaming_mla_s5_retention_step'


def _normal(k, shape, scale):
    return jax.random.normal(k, shape, jnp.float32) * scale


def _rmsnorm(x, g):
    xf = x.astype(jnp.float32)
    y = xf * lax.rsqrt(jnp.mean(xf * xf, axis=-1, keepdims=True) + EPS)
    return (y * g.astype(jnp.float32)).astype(x.dtype)


def _rope(x, pos):
    half = x.shape[-1] // 2
    inv = ROPE_THETA ** (-jnp.arange(half, dtype=jnp.float32) / half)
    ang = pos.astype(jnp.float32)[:, None] * inv[None, :]
    ang = ang.reshape((ang.shape[0],) + (1,) * (x.ndim - 3) + (half,))
    cos, sin = jnp.cos(ang), jnp.sin(ang)
    xf = x.astype(jnp.float32)
    x1, x2 = xf[..., :half], xf[..., half:]
    return jnp.concatenate([x1 * cos - x2 * sin, x1 * sin + x2 * cos], axis=-1).astype(x.dtype)


def _mlp(h, w_up, w_down):
    return jnp.square(jax.nn.relu(h @ w_up)) @ w_down


def _mla_attention(q_nope, q_rope, k_nope, k_rope, v, q_pos):
    Bn, Sq = q_nope.shape[:2]
    k_chunk = jnp.arange(k_nope.shape[1]) // CHUNK

    def attend(args):
        qn, qr, qp = args
        s = (jnp.einsum('bqhd,bkhd->bhqk', qn, k_nope)
             + jnp.einsum('bqhr,bkr->bhqk', qr, k_rope)).astype(jnp.float32) * MLA_SCALE
        allowed = k_chunk[None, :] <= (qp[:, None] // CHUNK)
        s = jnp.where(allowed, s, -jnp.inf)
        p = jax.nn.softmax(s, axis=-1).astype(v.dtype)
        return jnp.einsum('bhqk,bkhd->bqhd', p, v)

    if Sq <= Q_BLOCK:
        return attend((q_nope, q_rope, q_pos))
    nb = Sq // Q_BLOCK

    def to_blocks(t):
        return jnp.moveaxis(t.reshape((Bn, nb, Q_BLOCK) + t.shape[2:]), 1, 0)

    out = lax.map(attend, (to_blocks(q_nope), to_blocks(q_rope), q_pos.reshape(nb, Q_BLOCK)))
    return jnp.moveaxis(out, 0, 1).reshape((Bn, Sq) + out.shape[3:])


def _complex_affine_combine(left, right):
    a1r, a1i, b1r, b1i = left
    a2r, a2i, b2r, b2i = right
    return (a2r * a1r - a2i * a1i,
            a2r * a1i + a2i * a1r,
            a2r * b1r - a2i * b1i + b2r,
            a2r * b1i + a2i * b1r + b2i)


def _s5_scan(u, h_re, h_im, lam_re, lam_im, log_dt, b_re, b_im, c_re, c_im, d_skip):
    f32 = jnp.float32
    Bn, S, _ = u.shape
    ug = u.reshape(Bn, S, S5_GROUPS, S5_GROUP).astype(f32)
    lr = jnp.minimum(lam_re.astype(f32), -1e-4)
    li = lam_im.astype(f32)
    dt = jnp.exp(log_dt.astype(f32))[:, None]
    mag = jnp.exp(lr * dt)
    a_re = mag * jnp.cos(li * dt)
    a_im = mag * jnp.sin(li * dt)
    den = lr * lr + li * li
    f_re = ((a_re - 1.0) * lr + a_im * li) / den
    f_im = (a_im * lr - (a_re - 1.0) * li) / den
    br = b_re.astype(f32)
    bi = b_im.astype(f32)
    bb_re = f_re[..., None] * br - f_im[..., None] * bi
    bb_im = f_re[..., None] * bi + f_im[..., None] * br
    bu_re = jnp.einsum('gnp,bsgp->bsgn', bb_re, ug)
    bu_im = jnp.einsum('gnp,bsgp->bsgn', bb_im, ug)
    hr = h_re.astype(f32)
    hi = h_im.astype(f32)
    bu_re = bu_re.at[:, 0].add(a_re * hr - a_im * hi)
    bu_im = bu_im.at[:, 0].add(a_re * hi + a_im * hr)
    A_re = jnp.broadcast_to(a_re, (1, S) + a_re.shape)
    A_im = jnp.broadcast_to(a_im, (1, S) + a_im.shape)
    _, _, s_re, s_im = lax.associative_scan(_complex_affine_combine, (A_re, A_im, bu_re, bu_im), axis=1)
    y = (jnp.einsum('gpn,bsgn->bsgp', c_re.astype(f32), s_re)
         - jnp.einsum('gpn,bsgn->bsgp', c_im.astype(f32), s_im)
         + d_skip.astype(f32) * ug)
    return y.reshape(Bn, S, S5_WIDTH).astype(u.dtype), s_re[:, -1], s_im[:, -1]


def _mixer_ab(h, pos, past_ckv, past_krope, s5_re, s5_im, w_in, q_a_norm, kv_a_norm, w_q_b, w_kv_b,
              lam_re, lam_im, log_dt, b_re, b_im, c_re, c_im, d_skip, w_glu, b_glu, w_out):
    Bn, S, _ = h.shape
    c_q, c_kv, k_rope, u = jnp.split(h @ w_in, [Q_LORA, Q_LORA + KV_LORA, Q_LORA + KV_LORA + MLA_ROPE], axis=-1)
    q = (_rmsnorm(c_q, q_a_norm) @ w_q_b).reshape(Bn, S, MLA_HEADS, MLA_NOPE + MLA_ROPE)
    q_nope = q[..., :MLA_NOPE]
    q_rope = _rope(q[..., MLA_NOPE:], pos)
    c_kv = _rmsnorm(c_kv, kv_a_norm)
    k_rope = _rope(k_rope, pos)
    all_ckv = jnp.concatenate([past_ckv.astype(c_kv.dtype), c_kv], axis=1)
    all_krope = jnp.concatenate([past_krope.astype(k_rope.dtype), k_rope], axis=1)
    kv = (all_ckv @ w_kv_b).reshape(Bn, all_ckv.shape[1], MLA_HEADS, MLA_NOPE + MLA_V)
    attn = _mla_attention(q_nope, q_rope, kv[..., :MLA_NOPE], all_krope, kv[..., MLA_NOPE:], pos)
    y, s_re, s_im = _s5_scan(u, s5_re, s5_im, lam_re, lam_im, log_dt, b_re, b_im, c_re, c_im, d_skip)
    z = jax.nn.gelu(y)
    ssm = z * jax.nn.sigmoid(z @ w_glu + b_glu)
    out = jnp.concatenate([attn.reshape(Bn, S, MLA_HEADS * MLA_V), ssm], axis=-1) @ w_out
    return out, c_kv, k_rope, s_re, s_im


def _retention_chunk(q, k, v, state, log_g):
    L = q.shape[2]
    dt = q.dtype
    idx = jnp.arange(L, dtype=jnp.float32)
    diff = idx[:, None] - idx[None, :]
    decay = jnp.where(diff >= 0, jnp.exp(log_g[:, None, None] * jnp.maximum(diff, 0.0)), 0.0).astype(dt)
    inner = jnp.einsum('bhnm,bhmv->bhnv', jnp.einsum('bhnd,bhmd->bhnm', q, k) * decay, v)
    q_dec = jnp.exp(log_g[:, None] * (idx + 1.0)).astype(dt)[None, :, :, None]
    cross = jnp.einsum('bhnd,bhdv->bhnv', q * q_dec, state)
    k_dec = jnp.exp(log_g[:, None] * (L - 1.0 - idx)).astype(dt)[None, :, :, None]
    new_state = (jnp.exp(log_g * L).astype(dt)[None, :, None, None] * state
                 + jnp.einsum('bhmd,bhmv->bhdv', k * k_dec, v))
    return inner + cross, new_state


def _mixer_c(h, pos, state, w_in, gn_gain, w_out):
    Bn, S, _ = h.shape
    qk = RET_HEADS * RET_DK
    q, k, v, g = jnp.split(h @ w_in, [qk, 2 * qk, 2 * qk + RET_HEADS * RET_DV], axis=-1)
    q = _rope(q.reshape(Bn, S, RET_HEADS, RET_DK), pos)
    k = _rope(k.reshape(Bn, S, RET_HEADS, RET_DK), pos) * RET_DK ** -0.5
    v = v.reshape(Bn, S, RET_HEADS, RET_DV)
    q, k, v = (jnp.swapaxes(t, 1, 2) for t in (q, k, v))
    log_g = jnp.log(1.0 - 2.0 ** (-5.0 - jnp.arange(RET_HEADS, dtype=jnp.float32)))
    state = state.astype(q.dtype)
    if S <= CHUNK:
        o, new_state = _retention_chunk(q, k, v, state, log_g)
    else:
        nc = S // CHUNK

        def to_chunks(t):
            return jnp.moveaxis(t.reshape(Bn, RET_HEADS, nc, CHUNK, t.shape[-1]), 2, 0)

        def step(st, qkv):
            o_c, st = _retention_chunk(qkv[0], qkv[1], qkv[2], st, log_g)
            return st, o_c

        new_state, o = lax.scan(step, state, (to_chunks(q), to_chunks(k), to_chunks(v)))
        o = jnp.moveaxis(o, 0, 2).reshape(Bn, RET_HEADS, S, RET_DV)
    of = jnp.swapaxes(o, 1, 2).astype(jnp.float32)
    mu = jnp.mean(of, axis=-1, keepdims=True)
    var = jnp.mean(jnp.square(of - mu), axis=-1, keepdims=True)
    on = (of - mu) * lax.rsqrt(var + GN_EPS) * gn_gain.astype(jnp.float32).reshape(RET_HEADS, RET_DV)
    out = (on.astype(h.dtype).reshape(Bn, S, RET_HEADS * RET_DV) * jax.nn.silu(g)) @ w_out
    return out, new_state


def _trunk(x, pos, past_ckv, past_krope, s5_re, s5_im, ret_state,
           norm_mix, norm_mlp, norm_final, w_in_ab, q_a_norm, kv_a_norm, w_q_b, w_kv_b,
           s5_lam_re, s5_lam_im, s5_log_dt, s5_b_re, s5_b_im, s5_c_re, s5_c_im, s5_d, w_glu, b_glu,
           w_out_ab, w_in_c, ret_gn, w_out_c, w_up, w_down):
    ckv_rows, krope_rows, s5_re_new, s5_im_new, ret_new = [], [], [], [], []
    for layer in range(DEPTH):
        i = layer // 2
        h = _rmsnorm(x, norm_mix[layer])
        if layer % 2 == 0:
            mix, ckv, kr, sr, si = _mixer_ab(h, pos, past_ckv[i], past_krope[i], s5_re[i], s5_im[i],
                                             w_in_ab[i], q_a_norm[i], kv_a_norm[i], w_q_b[i], w_kv_b[i],
                                             s5_lam_re[i], s5_lam_im[i], s5_log_dt[i], s5_b_re[i], s5_b_im[i],
                                             s5_c_re[i], s5_c_im[i], s5_d[i], w_glu[i], b_glu[i], w_out_ab[i])
            ckv_rows.append(ckv)
            krope_rows.append(kr)
            s5_re_new.append(sr)
            s5_im_new.append(si)
        else:
            mix, rs = _mixer_c(h, pos, ret_state[i], w_in_c[i], ret_gn[i], w_out_c[i])
            ret_new.append(rs)
        x = x + mix
        x = x + _mlp(_rmsnorm(x, norm_mlp[layer]), w_up[layer], w_down[layer])
    return (_rmsnorm(x, norm_final), jnp.stack(ckv_rows), jnp.stack(krope_rows),
            jnp.stack(s5_re_new), jnp.stack(s5_im_new), jnp.stack(ret_new))


def setup_inputs(seed: int = 0) -> dict:
    key = jax.random.key(seed)
    k = jax.random.split(key, 31)
    f32 = jnp.float32
    G, N, P = S5_GROUPS, S5_STATE, S5_GROUP
    return {
        'x_prompt': _normal(k[0], (BATCH, SEQ, D_MODEL), 1.0),
        'x_sample': _normal(k[1], (DEC_BATCH, DEC_SEQ, D_MODEL), 1.0),
        'cache_mla_ckv': _normal(k[2], (N_AB, DEC_BATCH, PAST_LEN, KV_LORA), 1.0),
        'cache_mla_krope': _normal(k[3], (N_AB, DEC_BATCH, PAST_LEN, MLA_ROPE), 1.0),
        'state_s5_re': _normal(k[4], (N_AB, DEC_BATCH, G, N), 0.1),
        'state_s5_im': _normal(k[5], (N_AB, DEC_BATCH, G, N), 0.1),
        'state_ret': _normal(k[6], (N_C, DEC_BATCH, RET_HEADS, RET_DK, RET_DV), 1.0),
        'norm_mix': 1.0 + _normal(k[7], (DEPTH, D_MODEL), 0.02),
        'norm_mlp': 1.0 + _normal(k[8], (DEPTH, D_MODEL), 0.02),
        'norm_final': 1.0 + _normal(k[9], (D_MODEL,), 0.02),
        'w_in_ab': _normal(k[10], (N_AB, D_MODEL, IN_AB), D_MODEL ** -0.5),
        'q_a_norm': 1.0 + _normal(k[11], (N_AB, Q_LORA), 0.02),
        'kv_a_norm': 1.0 + _normal(k[12], (N_AB, KV_LORA), 0.02),
        'w_q_b': _normal(k[13], (N_AB, Q_LORA, MLA_HEADS * (MLA_NOPE + MLA_ROPE)), Q_LORA ** -0.5),
        'w_kv_b': _normal(k[14], (N_AB, KV_LORA, MLA_HEADS * (MLA_NOPE + MLA_V)), KV_LORA ** -0.5),
        's5_lam_re': -0.5 + _normal(k[15], (N_AB, G, N), 0.01),
        's5_lam_im': math.pi * jnp.arange(N, dtype=f32) + _normal(k[16], (N_AB, G, N), 0.01),
        's5_log_dt': jax.random.uniform(k[17], (N_AB, G), f32, math.log(DT_MIN), math.log(DT_MAX)),
        's5_b_re': _normal(k[18], (N_AB, G, N, P), (2 * P) ** -0.5),
        's5_b_im': _normal(k[19], (N_AB, G, N, P), (2 * P) ** -0.5),
        's5_c_re': _normal(k[20], (N_AB, G, P, N), N ** -0.5),
        's5_c_im': _normal(k[21], (N_AB, G, P, N), N ** -0.5),
        's5_d': _normal(k[22], (N_AB, G, P), 1.0),
        'w_glu': _normal(k[23], (N_AB, S5_WIDTH, S5_WIDTH), S5_WIDTH ** -0.5),
        'b_glu': _normal(k[24], (N_AB, S5_WIDTH), 0.01),
        'w_out_ab': _normal(k[25], (N_AB, MIX_AB, D_MODEL), MIX_AB ** -0.5),
        'w_in_c': _normal(k[26], (N_C, D_MODEL, IN_C), D_MODEL ** -0.5),
        'ret_gn': 1.0 + _normal(k[27], (N_C, RET_HEADS * RET_DV), 0.02),
        'w_out_c': _normal(k[28], (N_C, RET_HEADS * RET_DV, D_MODEL), (RET_HEADS * RET_DV) ** -0.5),
        'w_up': _normal(k[29], (DEPTH, D_MODEL, D_FF), D_MODEL ** -0.5),
        'w_down': _normal(k[30], (DEPTH, D_FF, D_MODEL), D_FF ** -0.5),
    }


def reference(x_prompt, x_sample, cache_mla_ckv, cache_mla_krope, state_s5_re, state_s5_im, state_ret,
              norm_mix, norm_mlp, norm_final, w_in_ab, q_a_norm, kv_a_norm, w_q_b, w_kv_b,
              s5_lam_re, s5_lam_im, s5_log_dt, s5_b_re, s5_b_im, s5_c_re, s5_c_im, s5_d, w_glu, b_glu,
              w_out_ab, w_in_c, ret_gn, w_out_c, w_up, w_down):
    weights = (norm_mix, norm_mlp, norm_final, w_in_ab, q_a_norm, kv_a_norm, w_q_b, w_kv_b,
               s5_lam_re, s5_lam_im, s5_log_dt, s5_b_re, s5_b_im, s5_c_re, s5_c_im, s5_d, w_glu, b_glu,
               w_out_ab, w_in_c, ret_gn, w_out_c, w_up, w_down)
    bp, sp, _ = x_prompt.shape
    dt = x_prompt.dtype
    empty_ckv = jnp.zeros((N_AB, bp, 0, KV_LORA), dt)
    empty_krope = jnp.zeros((N_AB, bp, 0, MLA_ROPE), dt)
    zero_s5 = jnp.zeros((N_AB, bp, S5_GROUPS, S5_STATE), jnp.float32)
    zero_ret = jnp.zeros((N_C, bp, RET_HEADS, RET_DK, RET_DV), dt)
    pos_p = jnp.arange(sp)
    y_prompt, ckv_p, krope_p, s5_re_p, s5_im_p, ret_p = _trunk(
        x_prompt, pos_p, empty_ckv, empty_krope, zero_s5, zero_s5, zero_ret, *weights)
    past = cache_mla_ckv.shape[2]
    pos_s = past + jnp.arange(x_sample.shape[1])
    y_sample, ckv_s, krope_s, s5_re_s, s5_im_s, ret_s = _trunk(
        x_sample, pos_s, cache_mla_ckv, cache_mla_krope, state_s5_re, state_s5_im, state_ret, *weights)
    return (y_prompt, y_sample, ckv_p, krope_p, s5_re_p, s5_im_p, ret_p,
            ckv_s, krope_s, s5_re_s, s5_im_s, ret_s)
```

```python
import math
from contextlib import ExitStack
import numpy as np
import ml_dtypes
import concourse.bass as bass
import concourse.mybir as mybir
from concourse.bass_utils import run_bass_kernel_spmd

F32 = mybir.dt.float32
BF16 = mybir.dt.bfloat16
AF = mybir.ActivationFunctionType
ALU = mybir.AluOpType

D = 2048
KC = 16
EPS = 1e-6
GN_EPS = 1e-5
Q_LORA, KV_LORA, ROPE = 512, 256, 64
NH = 8
S5W = 1024
IN_AB = 1856
DFF = 8192
RH, RDK, RDV = 8, 256, 512
MLA_SCALE = (128 + 64) ** -0.5
SEM_LIMIT = 30000


class Buf:
    __slots__ = ("w", "r")

    def __init__(self):
        self.w = {}
        self.r = {}


def bufs(n):
    return [Buf() for _ in range(n)]


class Eng:
    def __init__(self, nc, e, name, same_sync):
        self.e = e
        self.name = name
        self.sem = nc.alloc_semaphore("s_" + name)
        self.cnt = 0
        self.seen = {}
        self.same_sync = same_sync


class K:
    def __init__(self, nc):
        self.nc = nc
        self.pe = Eng(nc, nc.tensor, "pe", False)
        self.act = Eng(nc, nc.scalar, "act", True)
        self.dve = Eng(nc, nc.vector, "dve", True)
        self.pool = Eng(nc, nc.gpsimd, "pool", True)
        self.sp = Eng(nc, nc.sync, "sp", False)
        self.dma_sems = {}
        self.ninstr = 0

    def _wait(self, E, ev):
        sem, val = ev
        if sem is E.sem and not E.same_sync:
            return
        key = id(sem)
        if E.seen.get(key, 0) >= val:
            return
        E.e.wait_ge(sem, val)
        E.seen[key] = val

    def _deps(self, E, reads, writes):
        for b in reads:
            for ev in b.w.values():
                self._wait(E, ev)
        for b in writes:
            for ev in b.r.values():
                self._wait(E, ev)
            for ev in b.w.values():
                self._wait(E, ev)

    def _mark(self, ev, reads, writes):
        key = id(ev[0])
        for b in reads:
            b.r[key] = ev
        for b in writes:
            b.w[key] = ev
            b.r = {}

    def op(self, E, fn, R=(), W=()):
        if E.cnt >= SEM_LIMIT:
            E.sem = self.nc.alloc_semaphore(f"s_{E.name}_{self.ninstr}")
            E.cnt = 0
        self._deps(E, R, W)
        ins = fn(E.e)
        E.cnt += 1
        ins.then_inc(E.sem, 1)
        self._mark((E.sem, E.cnt), R, W)
        self.ninstr += 1
        return ins

    def dma(self, out, in_, R=(), W=(), Q=None, nsem=12, **kw):
        Q = Q or self.sp
        pool = self.dma_sems.setdefault(Q.name, {"sems": [], "vals": [], "i": 0})
        if len(pool["sems"]) < nsem:
            pool["sems"].append(self.nc.alloc_semaphore(f"d_{Q.name}{len(pool['sems'])}"))
            pool["vals"].append(0)
        i = pool["i"] % len(pool["sems"])
        pool["i"] += 1
        sem = pool["sems"][i]
        if pool["vals"][i] > 0:
            self._wait(Q, (sem, pool["vals"][i]))
        if pool["vals"][i] >= SEM_LIMIT:
            sem = pool["sems"][i] = self.nc.alloc_semaphore(f"d_{Q.name}{i}_{self.ninstr}")
            pool["vals"][i] = 0
        self._deps(Q, R, W)
        ins = Q.e.dma_start(out=out, in_=in_, **kw)
        pool["vals"][i] += 16
        ins.then_inc(sem, 16)
        self._mark((sem, pool["vals"][i]), R, W)
        self.ninstr += 1
        return ins

    def barrier(self):
        engs = [self.pe, self.act, self.dve, self.pool, self.sp]
        evs = [(E.sem, E.cnt) for E in engs if E.cnt > 0]
        for pl in self.dma_sems.values():
            for sem, val in zip(pl["sems"], pl["vals"]):
                if val > 0:
                    evs.append((sem, val))
        for E in engs:
            for ev in evs:
                if ev[0] is not E.sem:
                    self._wait(E, ev)

    def finish(self, bl):
        for b in bl:
            for ev in b.w.values():
                self._wait(self.sp, ev)

    def mm(self, out, lhsT, rhs, start, stop, R, W):
        return self.op(self.pe, lambda e: e.matmul(out, lhsT=lhsT, rhs=rhs, start=start, stop=stop), R, W)

    def tr(self, out, in_, ident, R, W):
        return self.op(self.pe, lambda e: e.transpose(out, in_, ident), R, W)

    def actf(self, out, in_, func, R, W, **kw):
        return self.op(self.act, lambda e: e.activation(out=out, in_=in_, func=func, **kw), R, W)

    def cp(self, E, out, in_, R, W):
        if E is self.act:
            return self.op(E, lambda e: e.copy(out=out, in_=in_), R, W)
        return self.op(E, lambda e: e.tensor_copy(out=out, in_=in_), R, W)

    def tt(self, E, out, in0, in1, op, R, W):
        return self.op(E, lambda e: e.tensor_tensor(out=out, in0=in0, in1=in1, op=op), R, W)

    def ts(self, E, out, in0, s1, s2, op0, op1, R, W):
        if op1 is None:
            return self.op(E, lambda e: e.tensor_scalar(out=out, in0=in0, scalar1=s1, scalar2=None, op0=op0), R, W)
        return self.op(E, lambda e: e.tensor_scalar(out=out, in0=in0, scalar1=s1, scalar2=s2, op0=op0, op1=op1), R, W)

    def stt(self, E, out, in0, scalar, in1, op0, op1, R, W):
        return self.op(E, lambda e: e.scalar_tensor_tensor(out=out, in0=in0, scalar=scalar, in1=in1, op0=op0, op1=op1), R, W)

    def memset(self, E, ap, val, R, W):
        return self.op(E, lambda e: e.memset(ap, val), R, W)


class Slab:
    __slots__ = ("ap", "buf", "nk", "mw")

    def __init__(self, ap, nk, mw):
        self.ap = ap
        self.buf = Buf()
        self.nk = nk
        self.mw = mw


def host_consts(maxpos):
    half = 32
    inv = 10000.0 ** (-np.arange(half, dtype=np.float64) / half)
    pos = np.arange(maxpos, dtype=np.float64)
    ang = pos[:, None] * inv[None, :]
    mla_cs = np.concatenate([np.cos(ang), np.sin(ang)], axis=1).astype(np.float32)
    half = 128
    inv = 10000.0 ** (-np.arange(half, dtype=np.float64) / half)
    ang = inv[:, None] * pos[None, :]
    ret_cos = np.cos(ang).astype(np.float32)
    ret_sin = np.sin(ang).astype(np.float32)
    logg = np.log(1.0 - 2.0 ** (-5.0 - np.arange(RH, dtype=np.float64)))
    L = 128
    idx = np.arange(L, dtype=np.float64)
    diff = idx[None, :] - idx[:, None]
    DT = np.where(diff >= 0, np.exp(logg[:, None, None] * np.maximum(diff, 0.0)), 0.0) * RDK ** -0.5
    DTt = np.ascontiguousarray(DT.transpose(1, 0, 2)).astype(np.float32)
    dq = np.exp(logg[:, None] * (idx[None, :] + 1.0))
    dq_rep = np.broadcast_to(dq[None], (128, RH, L)).astype(np.float32).copy()
    kdec = {}
    for LL in (128, 64):
        ii = np.arange(LL, dtype=np.float64)
        kd = np.exp(logg[None, :] * (LL - 1.0 - ii[:, None])) * RDK ** -0.5
        full = np.zeros((128, RH), np.float32)
        full[:LL] = kd
        kdec[LL] = full
    gL = {LL: [float(np.exp(logg[h] * LL)) for h in range(RH)] for LL in (128, 64)}
    return dict(mla_cs=mla_cs, ret_cos=ret_cos, ret_sin=ret_sin, DTt=DTt, dq_rep=dq_rep,
                kdec128=kdec[128], kdec64=kdec[64], gL=gL,
                ident_f=np.eye(128, dtype=np.float32), ident_b=np.eye(128, dtype=np.float32).astype(ml_dtypes.bfloat16))


class Seq:
    pass


def build(cfg):
    SEQ, DS, PAST, DEPTH = cfg["SEQ"], cfg["DS"], cfg["PAST"], cfg["DEPTH"]
    types = cfg["types"]
    NAB = sum(1 for t in types if t == "ab")
    NC_ = sum(1 for t in types if t == "c")
    NABd, NCd = max(NAB, 1), max(NC_, 1)
    DO_MLP = cfg.get("mlp", True)
    S5L = cfg.get("s5l", 4)
    TP = cfg.get("T", 512)
    maxpos = max(SEQ, PAST + DS)
    nc = bass.Bass("TRN2", target_bir_lowering=False)
    k = K(nc)
    k.in_shapes = {}
    gLtab = cfg["gL"]
    HALFPI = math.pi / 2

    def din(name, shape, dt=F32, used=True):
        if not used:
            shape = [1] * len(shape)
        k.in_shapes[name] = tuple(shape)
        return nc.dram_tensor(name, list(shape), dt, kind="ExternalInput").ap()

    def dout(name, shape, dt=F32):
        return nc.dram_tensor(name, list(shape), dt, kind="ExternalOutput").ap()

    def dscr(name, shape, dt):
        return nc.dram_tensor(name, list(shape), dt, kind="Internal").ap()

    uab, uc = NAB > 0, NC_ > 0
    xp = din("xp", [SEQ, D]); xs = din("xs", [DS, D])
    c_ckv = din("c_ckv", [NABd, PAST, KV_LORA], used=uab); c_kr = din("c_kr", [NABd, PAST, ROPE], used=uab)
    s5re_in = din("s5re_in", [NABd, 64, 64], used=uab); s5im_in = din("s5im_in", [NABd, 64, 64], used=uab)
    ret_in = din("ret_in", [NCd, RH, RDK, RDV], used=uc)
    norm_mix = din("norm_mix", [DEPTH, D]); norm_mlp = din("norm_mlp", [DEPTH, D]); norm_final = din("norm_final", [D])
    w_in_ab = din("w_in_ab", [NABd, D, IN_AB], used=uab); q_a_norm = din("q_a_norm", [NABd, Q_LORA], used=uab)
    kv_a_norm = din("kv_a_norm", [NABd, KV_LORA], used=uab)
    w_q_b = din("w_q_b", [NABd, Q_LORA, NH * 192], used=uab); w_kv_b = din("w_kv_b", [NABd, KV_LORA, NH * 256], used=uab)
    lam_re = din("s5_lam_re", [NABd, 64, 64], used=uab); lam_im = din("s5_lam_im", [NABd, 64, 64], used=uab)
    log_dt = din("s5_log_dt", [NABd, 64], used=uab)
    b_re = din("s5_b_re", [NABd, 64, 64, 16], used=uab); b_im = din("s5_b_im", [NABd, 64, 64, 16], used=uab)
    c_re = din("s5_c_re", [NABd, 64, 16, 64], used=uab); c_im = din("s5_c_im", [NABd, 64, 16, 64], used=uab)
    s5_d = din("s5_d", [NABd, 64, 16], used=uab)
    w_glu = din("w_glu", [NABd, S5W, S5W], used=uab); b_glu = din("b_glu", [NABd, S5W], used=uab)
    w_out_ab = din("w_out_ab", [NABd, D, D], used=uab)
    w_in_c = din("w_in_c", [NCd, D, 12288], used=uc); ret_gn = din("ret_gn", [NCd, RH * RDV], used=uc)
    w_out_c = din("w_out_c", [NCd, RH * RDV, D], used=uc)
    w_up = din("w_up", [DEPTH, D, DFF], used=DO_MLP); w_down = din("w_down", [DEPTH, DFF, D], used=DO_MLP)
    h_mla_cs = din("h_mla_cs", [maxpos, 64]); h_ret_cos = din("h_ret_cos", [128, maxpos]); h_ret_sin = din("h_ret_sin", [128, maxpos])
    h_DTt = din("h_DTt", [128, RH, 128]); h_dq = din("h_dq", [128, RH, 128])
    h_kdec128 = din("h_kdec128", [128, RH]); h_kdec64 = din("h_kdec64", [128, RH])
    h_ident_f = din("h_ident_f", [128, 128]); h_ident_b = din("h_ident_b", [128, 128], BF16)

    y_p = dout("y_p", [SEQ, D]); y_s = dout("y_s", [DS, D])
    ckv_p = dout("ckv_p", [NABd, SEQ, KV_LORA]); kr_p = dout("kr_p", [NABd, SEQ, ROPE])
    s5re_p = dout("s5re_p", [NABd, 64, 64]); s5im_p = dout("s5im_p", [NABd, 64, 64]); ret_p = dout("ret_p", [NCd, RH, RDK, RDV])
    ckv_s = dout("ckv_s", [NABd, DS, KV_LORA]); kr_s = dout("kr_s", [NABd, DS, ROPE])
    s5re_s = dout("s5re_s", [NABd, 64, 64]); s5im_s = dout("s5im_s", [NABd, 64, 64]); ret_s = dout("ret_s", [NCd, RH, RDK, RDV])
    out_bufs = []

    def obuf():
        b = Buf(); out_bufs.append(b)
        return b

    sp_ = Seq(); sp_.name = "p"; sp_.x = xp; sp_.y = y_p; sp_.T = TP; sp_.nt = SEQ // TP; sp_.pos0 = 0; sp_.past = 0
    sp_.ckv_o, sp_.kr_o, sp_.s5re_o, sp_.s5im_o, sp_.ret_o = ckv_p, kr_p, s5re_p, s5im_p, ret_p
    ss_ = Seq(); ss_.name = "s"; ss_.x = xs; ss_.y = y_s; ss_.T = DS; ss_.nt = 1; ss_.pos0 = PAST; ss_.past = PAST
    ss_.ckv_o, ss_.kr_o, ss_.s5re_o, ss_.s5im_o, ss_.ret_o = ckv_s, kr_s, s5re_s, s5im_s, ret_s
    seqs = [sp_, ss_]
    for s in seqs:
        s.nkeys = s.past + s.nt * s.T
        s.xscr = dscr(f"xscr_{s.name}", [s.nt, 128, KC, s.T], F32)
        s.xscr_b = bufs(s.nt)
        s.kvT = [dscr(f"kvT_{s.name}{i}", [128, 2, s.nkeys], BF16) for i in range(NAB)]
        s.krT = [dscr(f"krT_{s.name}{i}", [64, s.nkeys], BF16) for i in range(NAB)]
        s.kvtok = [dscr(f"kvtok_{s.name}{i}", [s.nkeys, KV_LORA], BF16) for i in range(NAB)]
        nsb = (s.nkeys + 511) // 512
        s.kv_b = [[Buf() for _ in range(nsb)] for _ in range(NAB)]
    s5blk = [dscr(f"s5blk{i}", [32, 128, 4, 128], BF16) for i in range(NAB)]
    s5tab = [dscr(f"s5tab{i}", [32, 128, 3, 512], F32) for i in range(NAB)]
    s5c_b = [[Buf() for _ in range(32)] for _ in range(NAB)]

    def sb(name, shape, dt):
        return nc.alloc_sbuf_tensor(name, list(shape), dt)

    uid = [0]

    def sbs(es, shape, dt):
        uid[0] += 1
        return es.enter_context(nc.sbuf_tensor(f"t{uid[0]}", list(shape), dt))

    xt = sb("xt", [128, KC, TP], F32); b_xt = bufs(KC)
    xn = sb("xn", [128, KC, TP], BF16); b_xn = bufs(KC)
    WS = 2
    wsl = [sb(f"wsl{i}", [128, 16 * 512], BF16) for i in range(WS)]; b_wsl = bufs(WS)
    ident_f = sb("ident_f", [128, 128], F32); ident_b = sb("ident_b", [128, 128], BF16); ones_b = sb("ones_b", [128, 128], BF16)
    b_const = Buf()
    gmix = sb("gmix", [128, DEPTH, KC], F32); gmlp = sb("gmlp", [128, DEPTH, KC], F32)
    rstd = sb("rstd", [128, TP], F32); b_rstd = Buf()
    sqb = [sb(f"sqb{i}", [128, 4, TP], BF16) for i in range(2)]; b_sqb = bufs(2)
    eps_t = sb("eps_t", [128, 4], F32)
    NPS = 8
    ps = [nc.alloc_psum_tensor(f"ps{i}", [128, 512], F32) for i in range(NPS)]; b_ps = bufs(NPS)
    st = {"ps": 0, "ws": 0, "ce": 0}

    st["pa"] = 0

    def nps():
        i = 4 + st["ps"] % 4
        st["ps"] += 1
        return ps[i], b_ps[i]

    def npa(n):
        r = []
        for j in range(n):
            i = (st["pa"] + j) % 4
            r.append((ps[i], b_ps[i]))
        st["pa"] += n
        return r

    def psb(p):
        return p[:].bitcast(BF16)

    k.dma(ident_f[:], h_ident_f, W=[b_const])
    k.dma(ident_b[:], h_ident_b, W=[b_const])
    k.memset(k.pool, ones_b[:], 1.0, [], [b_const])
    k.dma(gmix[:], norm_mix.rearrange("l (c p) -> p l c", p=128), W=[b_const], allow_slow_non_contiguous=True)
    k.dma(gmlp[:], norm_mlp.rearrange("l (c p) -> p l c", p=128), W=[b_const], allow_slow_non_contiguous=True)
    k.memset(k.pool, eps_t[:, 0:1], EPS, [], [b_const])
    k.memset(k.pool, eps_t[:, 1:2], GN_EPS, [], [b_const])
    k.memset(k.pool, eps_t[:, 2:3], HALFPI, [], [b_const])
    k.memset(k.pool, eps_t[:, 3:4], 0.0, [], [b_const])

    slab_id = [0]

    def mk_slab(nk, mw):
        slab_id[0] += 1
        return Slab(dscr(f"slab{slab_id[0]}", [128, nk, mw], BF16), nk, mw)

    cast_engs = [k.pool, k.act, k.dve]

    def precast_all(jobs):
        with nc.sbuf_tensor("stg0", [128, 8192], F32) as s0, nc.sbuf_tensor("stg1", [128, 8192], F32) as s1:
            stg = [s0, s1]; b_stg = bufs(2)
            for n, job in enumerate(jobs):
                src3, slab = job[0], job[1]
                s = n % 2
                nk, mw = slab.nk, slab.mw
                sv = stg[s][:, :nk * mw].rearrange("p (c m) -> p c m", c=nk)
                if len(job) > 2:
                    sv = sv.rearrange(job[2], **job[3])
                    for c_ in range(nk):
                        k.dma(sv[:, c_], src3[:, c_], W=[b_stg[s]])
                else:
                    k.dma(sv, src3, W=[b_stg[s]])
                w = st["ws"] % WS; st["ws"] += 1
                wv = wsl[w][:, :nk * mw]
                E = cast_engs[n % 3]
                k.cp(E, wv, stg[s][:, :nk * mw], [b_stg[s]], [b_wsl[w]])
                k.dma(slab.ap, wv.rearrange("p (c m) -> p c m", c=nk), R=[b_wsl[w]], W=[slab.buf])
        k.barrier()

    def slabs_2d(W2, K_, M_, kgrp=16, mgrp=512):
        jobs, grid = [], []
        for m0 in range(0, M_, mgrp):
            mw = min(mgrp, M_ - m0)
            row = []
            for k0 in range(0, K_ // 128, kgrp):
                nk = min(kgrp, K_ // 128 - k0)
                sl = mk_slab(nk, mw)
                jobs.append((W2[k0 * 128:(k0 + nk) * 128, m0:m0 + mw].rearrange("(c p) m -> p c m", p=128), sl))
                row.append(sl)
            grid.append(row)
        return jobs, grid

    jobs = []
    LW = []
    iab = ic = 0
    for l in range(DEPTH):
        lw = {}
        if types[l] == "ab":
            i = iab; iab += 1
            lw["i"] = i
            j, lw["in_cq"] = slabs_2d(w_in_ab[i][:, 0:512], D, 512); jobs += j
            j, lw["in_kv"] = slabs_2d(w_in_ab[i][:, 512:832], D, 320); jobs += j
            j, lw["in_u"] = slabs_2d(w_in_ab[i][:, 832:1856], D, 1024); jobs += j
            wq = w_q_b[i].rearrange("(c p) (h e) -> p c h e", p=128, e=192)
            sl = mk_slab(4, 1024); jobs.append((wq[:, :, :, 0:128], sl, "p c (h e) -> p c h e", dict(h=8))); lw["qb_nope"] = sl
            sl = mk_slab(4, 512); jobs.append((wq[:, :, :, 128:192], sl, "p c (h e) -> p c h e", dict(h=8))); lw["qb_rope"] = sl
            sl = mk_slab(2, 2048); jobs.append((w_kv_b[i].rearrange("(c p) m -> p c m", p=128), sl)); lw["kvb"] = sl
            j, lw["glu"] = slabs_2d(w_glu[i], S5W, S5W); jobs += j
            j, lw["out"] = slabs_2d(w_out_ab[i], D, D); jobs += j
        elif types[l] == "c":
            i = ic; ic += 1
            lw["i"] = i
            lw["qk"], lw["v"], lw["g"], lw["out"] = [], [], [], []
            for h in range(RH):
                sl = mk_slab(16, 512)
                src = w_in_c[i][:, 0:4096].rearrange("(c p) (two h e) -> p c two h e", p=128, two=2, e=256)[:, :, :, h, :]
                jobs.append((src, sl, "p c (t e) -> p c t e", dict(t=2))); lw["qk"].append(sl)
                sl = mk_slab(16, 512); jobs.append((w_in_c[i][:, 4096 + h * 512:4096 + (h + 1) * 512].rearrange("(c p) m -> p c m", p=128), sl)); lw["v"].append(sl)
                sl = mk_slab(16, 512); jobs.append((w_in_c[i][:, 8192 + h * 512:8192 + (h + 1) * 512].rearrange("(c p) m -> p c m", p=128), sl)); lw["g"].append(sl)
                j, g = slabs_2d(w_out_c[i][h * 512:(h + 1) * 512, :], 512, D); jobs += j; lw["out"].append(g)
        if DO_MLP:
            j, lw["up"] = slabs_2d(w_up[l], D, DFF); jobs += j
            j, lw["down"] = slabs_2d(w_down[l], DFF, D); jobs += j
        LW.append(lw)
    precast_all(jobs)

    def load_slab(sl):
        w = st["ws"] % WS; st["ws"] += 1
        v = wsl[w][:, :sl.nk * sl.mw].rearrange("p (c m) -> p c m", c=sl.nk)
        k.dma(v, sl.ap, R=[sl.buf], W=[b_wsl[w]])
        return v, b_wsl[w]

    def proj_fm(grid, rhs_fn, rhs_bufs, Tt, evac, mt_w=128):
        mt = 0
        for row in grid:
            mw = row[0].mw
            nmt = mw // mt_w
            pss = npa(nmt)
            nkt = sum(sl.nk for sl in row)
            kc0 = 0
            for sl in row:
                v, bw = load_slab(sl)
                for j in range(nmt):
                    p, bp = pss[j]
                    for c in range(sl.nk):
                        kc = kc0 + c
                        k.mm(p[:mt_w, :Tt], v[:, c, j * mt_w:(j + 1) * mt_w], rhs_fn(kc), kc == 0, kc == nkt - 1,
                             [bw] + rhs_bufs(kc), [bp])
                kc0 += sl.nk
            for j in range(nmt):
                evac(mt, pss[j][0], pss[j][1])
                mt += 1

    def proj_tm(row, lhs_fn, lhs_bufs, TS, nsub, ncols, evac):
        pss = npa(nsub)
        nkt = sum(sl.nk for sl in row)
        kc0 = 0
        for sl in row:
            v, bw = load_slab(sl)
            for sub in range(nsub):
                p, bp = pss[sub]
                for c in range(sl.nk):
                    kc = kc0 + c
                    k.mm(p[:TS, :ncols], lhs_fn(kc, sub), v[:, c, :ncols], kc == 0, kc == nkt - 1, [bw] + lhs_bufs(kc), [bp])
            kc0 += sl.nk
        for sub in range(nsub):
            evac(sub, pss[sub][0], pss[sub][1])

    def rmsnorm_fm(src, b_src, nchunk, gain, Tt, dst, b_dst):
        p, bp = nps()
        for g0 in range(0, nchunk, 4):
            s = st["ce"] % 2; st["ce"] += 1
            k.actf(sqb[s][:, :, :Tt], src[:, g0:g0 + 4, :Tt], AF.Square, b_src[g0:g0 + 4], [b_sqb[s]])
            for c in range(4):
                k.mm(p[:, :Tt], ones_b[:], sqb[s][:, c, :Tt], g0 + c == 0, g0 + c == nchunk - 1, [b_sqb[s], b_const], [bp])
        k.actf(rstd[:, :Tt], p[:, :Tt], AF.Sqrt, [bp], [b_rstd], bias=eps_t[:, 0:1], scale=1.0 / (nchunk * 128))
        k.op(k.dve, lambda e: e.reciprocal(out=rstd[:, :Tt], in_=rstd[:, :Tt]), [b_rstd], [b_rstd])
        for c in range(nchunk):
            k.stt(k.dve, dst[:, c, :Tt], src[:, c, :Tt], gain(c), rstd[:, :Tt], ALU.mult, ALU.mult, [b_src[c], b_rstd, b_const], [b_dst[c]])

    def add_to_x(Tt):
        def ev(mt, p, bp):
            k.tt(k.dve, xt[:, mt, :Tt], xt[:, mt, :Tt], p[:, :Tt], ALU.add, [bp, b_xt[mt]], [b_xt[mt]])
        return ev

    def load_x0(s, ti):
        Tt = s.T; TS = min(128, Tt); nsub = Tt // TS
        with ExitStack() as es:
            k.barrier()
            xtok = sbs(es, [128, D], F32); b_xtok = Buf()
            for sub in range(nsub):
                t0 = ti * Tt + sub * TS
                k.dma(xtok[:TS, :], s.x[t0:t0 + TS, :], W=[b_xtok])
                for c4 in range(4):
                    p, bp = nps()
                    for j in range(4):
                        c = c4 * 4 + j
                        k.tr(p[:, j * TS:(j + 1) * TS], xtok[:TS, c * 128:(c + 1) * 128], ident_f[:TS, :TS], [b_xtok, b_const], [bp])
                    E = k.act if c4 % 2 == 0 else k.dve
                    k.cp(E, xt[:, c4 * 4:c4 * 4 + 4, sub * TS:(sub + 1) * TS], p[:, :4 * TS].rearrange("p (a b) -> p a b", a=4),
                         [bp], b_xt[c4 * 4:c4 * 4 + 4])
            k.barrier()

    def final_out(s, ti):
        Tt = s.T; TS = min(128, Tt); nsub = Tt // TS
        with ExitStack() as es:
            k.barrier()
            xtok = sbs(es, [128, D], F32); gfin = sbs(es, [128, D], F32); junk = sbs(es, [128, D], BF16); ssum = sbs(es, [128, 4], F32)
            b_xtok = Buf(); b_fin = Buf(); b_g = Buf()
            k.dma(gfin[:], norm_final.partition_broadcast(128), W=[b_g])
            for sub in range(nsub):
                t0 = ti * Tt + sub * TS
                for c4 in range(4):
                    p, bp = nps()
                    for j in range(4):
                        c = c4 * 4 + j
                        k.tr(p[:TS, j * 128:(j + 1) * 128], xt[:, c, sub * TS:(sub + 1) * TS], ident_f[:, :], [b_xt[c], b_const], [bp])
                    E = k.act if c4 % 2 == 0 else k.dve
                    k.cp(E, xtok[:TS, c4 * 512:(c4 + 1) * 512], p[:TS, :], [bp], [b_xtok])
                k.actf(junk[:TS, :], xtok[:TS, :], AF.Square, [b_xtok], [b_fin], accum_out=ssum[:TS, 0:1])
                k.actf(ssum[:TS, 1:2], ssum[:TS, 0:1], AF.Sqrt, [b_fin], [b_fin], bias=eps_t[:TS, 0:1], scale=1.0 / D)
                k.op(k.dve, lambda e: e.reciprocal(out=ssum[:TS, 2:3], in_=ssum[:TS, 1:2]), [b_fin], [b_fin])
                k.stt(k.dve, xtok[:TS, :], xtok[:TS, :], ssum[:TS, 2:3], gfin[:TS, :], ALU.mult, ALU.mult, [b_xtok, b_fin, b_g], [b_xtok])
                k.dma(s.y[t0:t0 + TS, :], xtok[:TS, :], R=[b_xtok], W=[obuf()])
            k.barrier()

    def mlp(l, Tt):
        lw = LW[l]
        with ExitStack() as es:
            k.barrier()
            hT = sbs(es, [128, 16, TP], BF16); tmp = [sbs(es, [128, TP], F32) for _ in range(2)]
            b_hT = bufs(16); b_tmp = bufs(2)
            rmsnorm_fm(xt, b_xt, KC, lambda c: gmlp[:, l, c:c + 1], Tt, xn, b_xn)
            for j in range(DFF // 2048):
                def evac_up(mt, p, bp):
                    s = mt % 2
                    k.actf(tmp[s][:, :Tt], p[:, :Tt], AF.Relu, [bp], [b_tmp[s]])
                    E = k.dve if mt % 2 == 0 else k.pool
                    k.tt(E, hT[:, mt, :Tt], tmp[s][:, :Tt], tmp[s][:, :Tt], ALU.mult, [b_tmp[s]], [b_hT[mt]])
                proj_fm([[lw["up"][j * 4 + q][0]] for q in range(4)], lambda kc: xn[:, kc, :Tt], lambda kc: [b_xn[kc]], Tt, evac_up)
                proj_fm([[lw["down"][q][j]] for q in range(4)], lambda kc: hT[:, kc, :Tt], lambda kc: [b_hT[kc]], Tt, add_to_x(Tt))
            k.barrier()

    def rope_fm(p0, bp0, p1, bp1, rc, rs_, b_rope, ta, b_ta, dst, b_dst, Tt):
        k.tt(k.dve, ta[0][:, :Tt], p0[:, :Tt], rc[:, :Tt], ALU.mult, [bp0, b_rope], [b_ta[0]])
        k.tt(k.dve, ta[1][:, :Tt], p1[:, :Tt], rs_[:, :Tt], ALU.mult, [bp1, b_rope], [b_ta[1]])
        k.tt(k.pool, dst[:, 0, :Tt], ta[0][:, :Tt], ta[1][:, :Tt], ALU.subtract, [b_ta[0], b_ta[1]], [b_dst[0]])
        k.tt(k.dve, ta[2][:, :Tt], p0[:, :Tt], rs_[:, :Tt], ALU.mult, [bp0, b_rope], [b_ta[2]])
        k.tt(k.dve, ta[3][:, :Tt], p1[:, :Tt], rc[:, :Tt], ALU.mult, [bp1, b_rope], [b_ta[3]])
        k.tt(k.pool, dst[:, 1, :Tt], ta[2][:, :Tt], ta[3][:, :Tt], ALU.add, [b_ta[2], b_ta[3]], [b_dst[1]])

    def layer_c(l):
        lw = LW[l]; i = lw["i"]
        with ExitStack() as les:
            k.barrier()
            Sst = sbs(les, [128, RH, 2, 512], F32); b_S = bufs(RH)
            DTt = sbs(les, [128, RH, 128], F32); dq = sbs(les, [128, RH, 128], F32)
            kd128 = sbs(les, [128, RH], F32); kd64 = sbs(les, [128, RH], F32)
            b_lc = Buf()
            k.dma(DTt[:], h_DTt, W=[b_lc]); k.dma(dq[:], h_dq, W=[b_lc])
            k.dma(kd128[:], h_kdec128, W=[b_lc]); k.dma(kd64[:], h_kdec64, W=[b_lc])
            for s in seqs:
                if s.past == 0:
                    for h in range(RH):
                        k.memset(k.pool, Sst[:, h, :, :], 0.0, [], [b_S[h]])
                else:
                    for h in range(RH):
                        k.dma(Sst[:, h, :, :], ret_in[i, h].rearrange("(c p) v -> p c v", p=128), W=[b_S[h]])
                for ti in range(s.nt):
                    tile_begin(l, s, ti)
                    mixer_c(l, s, ti, Sst, b_S, DTt, dq, kd128 if s.T >= 128 else kd64, b_lc)
                    tile_end(l, s, ti)
                for h in range(RH):
                    k.dma(s.ret_o[i, h].rearrange("(c p) v -> p c v", p=128), Sst[:, h, :, :], R=[b_S[h]], W=[obuf()])
            k.barrier()

    def mixer_c(l, s, ti, Sst, b_S, DTt, dq, kdec, b_lc):
        lw = LW[l]; i = lw["i"]; Tt = s.T; L = min(128, Tt); nch = Tt // L
        pos_lo = s.pos0 + ti * Tt
        gLs = gLtab[L]
        rmsnorm_fm(xt, b_xt, KC, lambda c: gmix[:, l, c:c + 1], Tt, xn, b_xn)
        with ExitStack() as es:
            k.barrier()
            rc = sbs(es, [128, TP], F32); rs_ = sbs(es, [128, TP], F32); b_rope = Buf()
            k.dma(rc[:, :Tt], h_ret_cos[:, pos_lo:pos_lo + Tt], W=[b_rope])
            k.dma(rs_[:, :Tt], h_ret_sin[:, pos_lo:pos_lo + Tt], W=[b_rope])
            qr = sbs(es, [128, 2, TP], BF16); kr_ = sbs(es, [128, 2, TP], BF16); qt = sbs(es, [128, 2, TP], BF16)
            b_qr = bufs(2); b_kr = bufs(2); b_qt = bufs(2)
            ta = [sbs(es, [128, TP], F32) for _ in range(4)]; b_ta = bufs(4)
            vt = sbs(es, [128, 4, 512], BF16); gt = sbs(es, [128, 4, 512], BF16); ktk = sbs(es, [128, 4, 256], BF16)
            b_vt = bufs(4); b_gt = bufs(4); b_ktk = bufs(4)
            PT = [sbs(es, [128, 128], BF16) for _ in range(2)]; b_PT = bufs(2)
            onf = [sbs(es, [128, 512], F32) for _ in range(2)]; b_onf = bufs(2)
            onb = [sbs(es, [128, 512], BF16) for _ in range(2)]; b_onb = bufs(2)
            onT = sbs(es, [128, 4, TP], BF16); b_onT = bufs(4)
            Sbf = sbs(es, [128, 2, 512], BF16); b_Sbf = bufs(2)
            gnr = [sbs(es, [128, 512], F32) for _ in range(2)]; b_gnr = bufs(2)
            stats = [sbs(es, [128, 16], F32) for _ in range(2)]; b_stats = bufs(2)
            cnt = 0
            for h in range(RH):
                g_ = h % 2
                k.dma(gnr[g_][:], ret_gn[i, h * 512:(h + 1) * 512].partition_broadcast(128), W=[b_gnr[g_]])
                hold = []

                def evac_qk(mt, p, bp):
                    hold.append((p, bp))
                    if mt == 1:
                        rope_fm(hold[0][0], hold[0][1], hold[1][0], hold[1][1], rc, rs_, b_rope, ta, b_ta, qr, b_qr, Tt)
                    if mt == 3:
                        rope_fm(hold[2][0], hold[2][1], hold[3][0], hold[3][1], rc, rs_, b_rope, ta, b_ta, kr_, b_kr, Tt)
                proj_fm([[lw["qk"][h]]], lambda kc: xn[:, kc, :Tt], lambda kc: [b_xn[kc]], Tt, evac_qk)
                for i2 in range(2):
                    k.tt(k.pool, qt[:, i2, :Tt].rearrange("p (c n) -> p c n", n=L), qr[:, i2, :Tt].rearrange("p (c n) -> p c n", n=L),
                         dq[:, h:h + 1, :L].broadcast_to([128, nch, L]), ALU.mult, [b_qr[i2], b_lc], [b_qt[i2]])

                def evac_v(sub, p, bp):
                    k.cp(k.act, vt[:L, sub, :], p[:L, :512], [bp], [b_vt[sub]])

                def evac_g(sub, p, bp):
                    k.actf(gt[:L, sub, :], p[:L, :512], AF.Silu, [bp], [b_gt[sub]])
                lhs = lambda kc, sub: xn[:, kc, sub * L:(sub + 1) * L]
                proj_tm([lw["v"][h]], lhs, lambda kc: [b_xn[kc]], L, nch, 512, evac_v)
                proj_tm([lw["g"][h]], lhs, lambda kc: [b_xn[kc]], L, nch, 512, evac_g)
                for sub in range(nch):
                    p, bp = nps(); pb = psb(p)
                    for i2 in range(2):
                        k.tr(pb[:L, i2 * 128:(i2 + 1) * 128], kr_[:, i2, sub * L:(sub + 1) * L], ident_b[:, :], [b_kr[i2], b_const], [bp])
                    k.ts(k.dve, ktk[:L, sub, :], pb[:L, :256], kdec[:L, h:h + 1], None, ALU.mult, None, [bp, b_lc], [b_ktk[sub]])
                for i2 in range(2):
                    k.cp(k.act, Sbf[:, i2, :], Sst[:, h, i2, :], [b_S[h]], [b_Sbf[i2]])
                for ci in range(nch):
                    c0 = ci * L
                    x = cnt % 2; cnt += 1
                    p_s, bp_s = nps()
                    for i2 in range(2):
                        k.mm(p_s[:L, :L], kr_[:, i2, c0:c0 + L], qr[:, i2, c0:c0 + L], i2 == 0, i2 == 1, [b_kr[i2], b_qr[i2]], [bp_s])
                    k.tt(k.dve, PT[x][:L, :L], p_s[:L, :L], DTt[:L, h, :L], ALU.mult, [bp_s, b_lc], [b_PT[x]])
                    p_o, bp_o = nps()
                    k.mm(p_o[:L, :512], PT[x][:L, :L], vt[:L, ci, :], True, False, [b_PT[x], b_vt[ci]], [bp_o])
                    for i2 in range(2):
                        k.mm(p_o[:L, :512], qt[:, i2, c0:c0 + L], Sbf[:, i2, :], False, i2 == 1, [b_qt[i2], b_Sbf[i2]], [bp_o])
                    sx = stats[x]; bsx = b_stats[x]
                    k.op(k.dve, lambda e: e.bn_stats(out=sx[:L, 0:6], in_=p_o[:L, :512]), [bp_o], [bsx])
                    k.op(k.dve, lambda e: e.bn_aggr(out=sx[:L, 6:8], in_=sx[:L, 0:6]), [bsx], [bsx])
                    k.actf(sx[:L, 8:9], sx[:L, 7:8], AF.Sqrt, [bsx], [bsx], bias=eps_t[:L, 1:2], scale=1.0)
                    k.op(k.dve, lambda e: e.reciprocal(out=sx[:L, 9:10], in_=sx[:L, 8:9]), [bsx], [bsx])
                    k.ts(k.dve, onf[x][:L, :], p_o[:L, :512], sx[:L, 6:7], sx[:L, 9:10], ALU.subtract, ALU.mult, [bp_o, bsx], [b_onf[x]])
                    k.tt(k.pool, onf[x][:L, :], onf[x][:L, :], gnr[g_][:L, :], ALU.mult, [b_onf[x], b_gnr[g_]], [b_onf[x]])
                    k.tt(k.pool, onb[x][:L, :], onf[x][:L, :], gt[:L, ci, :], ALU.mult, [b_onf[x], b_gt[ci]], [b_onb[x]])
                    p_t, bp_t = nps(); ptb = psb(p_t)
                    for j in range(4):
                        k.tr(ptb[:, j * L:(j + 1) * L], onb[x][:L, j * 128:(j + 1) * 128], ident_b[:L, :L], [b_onb[x], b_const], [bp_t])
                    k.cp(k.act, onT[:, :, c0:c0 + L], ptb[:, :4 * L].rearrange("p (a b) -> p a b", a=4), [bp_t], b_onT)
                    for i2 in range(2):
                        p_d, bp_d = nps()
                        k.mm(p_d[:, :512], ktk[:L, ci, i2 * 128:(i2 + 1) * 128], vt[:L, ci, :], True, True, [b_ktk[ci], b_vt[ci]], [bp_d])
                        k.stt(k.dve, Sst[:, h, i2, :], Sst[:, h, i2, :], gLs[h], p_d[:, :512], ALU.mult, ALU.add, [bp_d, b_S[h]], [b_S[h]])
                        k.cp(k.act, Sbf[:, i2, :], Sst[:, h, i2, :], [b_S[h]], [b_Sbf[i2]])
                proj_fm(lw["out"][h], lambda kc: onT[:, kc, :Tt], lambda kc: [b_onT[kc]], Tt, add_to_x(Tt))
            k.barrier()

    def s5_setup(i, es0):
        with ExitStack() as es:
            k.barrier()
            def t32(n=32):
                return sbs(es, [128, n], F32)
            lr = t32(); li = t32(); dtt = t32(); mag = t32(); th = t32(); cc = t32(); sn = t32()
            c2 = t32(); s2 = t32(); cs_ = t32(); ar = t32(); ai = t32(); den = t32(); am1 = t32(); fre = t32(); fim = t32(); u1 = t32(); u2 = t32()
            B = Buf()
            k.dma(lr[:], lam_re[i].rearrange("(gp g2) n -> (g2 n) gp", g2=2), W=[B], allow_slow_non_contiguous=True)
            k.dma(li[:], lam_im[i].rearrange("(gp g2) n -> (g2 n) gp", g2=2), W=[B], allow_slow_non_contiguous=True)
            for g2 in range(2):
                src = bass.AP(tensor=log_dt.tensor, offset=i * 64 + g2, ap=[[0, 64], [2, 32]])
                k.dma(dtt[g2 * 64:(g2 + 1) * 64, :], src, W=[B], allow_slow_non_contiguous=True)
            R_, W_ = [B], [B]
            k.ts(k.dve, lr[:], lr[:], -1e-4, None, ALU.min, None, R_, W_)
            k.actf(dtt[:], dtt[:], AF.Exp, R_, W_)
            k.tt(k.dve, u1[:], lr[:], dtt[:], ALU.mult, R_, W_)
            k.actf(mag[:], u1[:], AF.Exp, R_, W_)
            k.tt(k.dve, th[:], li[:], dtt[:], ALU.mult, R_, W_)
            k.actf(cc[:], th[:], AF.Sin, R_, W_, bias=eps_t[:, 2:3], scale=1.0 / 16)
            k.actf(sn[:], th[:], AF.Sin, R_, W_, bias=eps_t[:, 3:4], scale=1.0 / 16)
            for _ in range(4):
                k.tt(k.dve, c2[:], cc[:], cc[:], ALU.mult, R_, W_)
                k.tt(k.dve, s2[:], sn[:], sn[:], ALU.mult, R_, W_)
                k.tt(k.dve, cs_[:], cc[:], sn[:], ALU.mult, R_, W_)
                k.tt(k.dve, cc[:], c2[:], s2[:], ALU.subtract, R_, W_)
                k.ts(k.dve, sn[:], cs_[:], 2.0, None, ALU.mult, None, R_, W_)
            k.tt(k.dve, ar[:], mag[:], cc[:], ALU.mult, R_, W_)
            k.tt(k.dve, ai[:], mag[:], sn[:], ALU.mult, R_, W_)
            k.tt(k.dve, c2[:], lr[:], lr[:], ALU.mult, R_, W_)
            k.tt(k.dve, s2[:], li[:], li[:], ALU.mult, R_, W_)
            k.tt(k.dve, den[:], c2[:], s2[:], ALU.add, R_, W_)
            k.op(k.dve, lambda e: e.reciprocal(out=den[:], in_=den[:]), R_, W_)
            k.ts(k.dve, am1[:], ar[:], -1.0, None, ALU.add, None, R_, W_)
            k.tt(k.dve, u1[:], am1[:], lr[:], ALU.mult, R_, W_)
            k.tt(k.dve, u2[:], ai[:], li[:], ALU.mult, R_, W_)
            k.tt(k.dve, u1[:], u1[:], u2[:], ALU.add, R_, W_)
            k.tt(k.dve, fre[:], u1[:], den[:], ALU.mult, R_, W_)
            k.tt(k.dve, u1[:], ai[:], lr[:], ALU.mult, R_, W_)
            k.tt(k.dve, u2[:], am1[:], li[:], ALU.mult, R_, W_)
            k.tt(k.dve, u1[:], u1[:], u2[:], ALU.subtract, R_, W_)
            k.tt(k.dve, fim[:], u1[:], den[:], ALU.mult, R_, W_)
            bre = sbs(es, [128, 32, 16], F32); bim = sbs(es, [128, 32, 16], F32)
            cre = sbs(es, [128, 32, 16], F32); cim = sbs(es, [128, 32, 16], F32)
            w1 = sbs(es, [128, 32, 16], F32); w2 = sbs(es, [128, 32, 16], F32)
            bbr = sbs(es, [128, 32, 16], F32); bbi = sbs(es, [128, 32, 16], F32)
            k.dma(bre[:], b_re[i].rearrange("(gp g2) n p -> (g2 n) gp p", g2=2), W=[B])
            k.dma(bim[:], b_im[i].rearrange("(gp g2) n p -> (g2 n) gp p", g2=2), W=[B])
            for g2 in range(2):
                for gp_ in range(32):
                    k.dma(cre[g2 * 64:(g2 + 1) * 64, gp_, :], c_re[i][2 * gp_ + g2].rearrange("q n -> n q"), W=[B], allow_slow_non_contiguous=True)
                    k.dma(cim[g2 * 64:(g2 + 1) * 64, gp_, :], c_im[i][2 * gp_ + g2].rearrange("q n -> n q"), W=[B], allow_slow_non_contiguous=True)
            freb = fre[:, :].rearrange("p (g o) -> p g o", o=1).broadcast_to([128, 32, 16])
            fimb = fim[:, :].rearrange("p (g o) -> p g o", o=1).broadcast_to([128, 32, 16])
            k.tt(k.dve, w1[:], bre[:], freb, ALU.mult, R_, W_)
            k.tt(k.dve, w2[:], bim[:], fimb, ALU.mult, R_, W_)
            k.tt(k.dve, bbr[:], w1[:], w2[:], ALU.subtract, R_, W_)
            k.tt(k.dve, w1[:], bim[:], freb, ALU.mult, R_, W_)
            k.tt(k.dve, w2[:], bre[:], fimb, ALU.mult, R_, W_)
            k.tt(k.dve, bbi[:], w1[:], w2[:], ALU.add, R_, W_)
            k.ts(k.dve, cim[:], cim[:], -1.0, None, ALU.mult, None, R_, W_)
            natR = [sbs(es, [128, 128], BF16) for _ in range(4)]; natI = [sbs(es, [128, 128], BF16) for _ in range(4)]
            blkq = [sbs(es, [128, 4, 128], BF16) for _ in range(4)]; b_q = bufs(4)
            for q in range(4):
                k.memset(k.pool, natR[q][:], 0.0, [], [b_q[q]]); k.memset(k.pool, natI[q][:], 0.0, [], [b_q[q]])
                k.memset(k.pool, blkq[q][:], 0.0, [], [b_q[q]])
            tab = [sbs(es, [128, 3, 512], F32) for _ in range(2)]; b_tab = bufs(2)
            tt_ = [sbs(es, [128, 256], F32) for _ in range(2)]
            for gp in range(32):
                q = gp % 4
                bq = b_q[q]
                for g2 in range(2):
                    rows = slice(g2 * 64, (g2 + 1) * 64)
                    cols = slice(q * 32 + g2 * 16, q * 32 + g2 * 16 + 16)
                    k.cp(k.dve, natR[q][rows, cols], bbr[rows, gp, :], [B, bq], [bq])
                    k.cp(k.dve, natI[q][rows, cols], bbi[rows, gp, :], [B, bq], [bq])
                    k.cp(k.pool, blkq[q][rows, 2, cols], cre[rows, gp, :], [B, bq], [bq])
                    k.cp(k.pool, blkq[q][rows, 3, cols], cim[rows, gp, :], [B, bq], [bq])
                p, bp = nps(); pb = psb(p)
                k.tr(pb[:, 0:128], natR[q][:, :], ident_b[:, :], [bq, b_const], [bp])
                k.tr(pb[:, 128:256], natI[q][:, :], ident_b[:, :], [bq, b_const], [bp])
                k.cp(k.act, blkq[q][:, 0:2, :], pb[:, 0:256].rearrange("p (a b) -> p a b", a=2), [bp, bq], [bq])
                k.dma(s5blk[i][gp], blkq[q][:], R=[bq], W=[s5c_b[i][gp]])
                x = gp % 2
                T_ = tab[x]; bt = b_tab[x]
                k.cp(k.dve, T_[:, 0, 0:1], cc[:, gp:gp + 1], [B, bt], [bt])
                k.cp(k.dve, T_[:, 1, 0:1], sn[:, gp:gp + 1], [B, bt], [bt])
                k.cp(k.dve, T_[:, 2, 0:1], mag[:, gp:gp + 1], [B, bt], [bt])
                m = 1
                while m < 512:
                    er = T_[:, 0, m - 1:m]; ei = T_[:, 1, m - 1:m]
                    k.ts(k.dve, T_[:, 2, m:2 * m], T_[:, 2, 0:m], T_[:, 2, m - 1:m], None, ALU.mult, None, [bt], [bt])
                    k.ts(k.dve, tt_[0][:, :m], T_[:, 1, 0:m], ei, None, ALU.mult, None, [bt], [bt])
                    k.ts(k.dve, tt_[1][:, :m], T_[:, 0, 0:m], ei, None, ALU.mult, None, [bt], [bt])
                    k.stt(k.dve, T_[:, 0, m:2 * m], T_[:, 0, 0:m], er, tt_[0][:, :m], ALU.mult, ALU.subtract, [bt], [bt])
                    k.stt(k.dve, T_[:, 1, m:2 * m], T_[:, 1, 0:m], er, tt_[1][:, :m], ALU.mult, ALU.add, [bt], [bt])
                    m *= 2
                k.dma(s5tab[i][gp], T_[:], R=[bt], W=[s5c_b[i][gp]])
            k.barrier()

    def kv_cache_setup(i, s):
        with ExitStack() as es:
            k.barrier()
            cst = sbs(es, [128, 320], F32); cb = sbs(es, [128, 320], BF16); cT = sbs(es, [128, 384], BF16)
            B1, B2, B3 = Buf(), Buf(), Buf()
            for b in range(s.past // 128):
                kb = s.kv_b[i][(b * 128) // 512]
                k.dma(cst[:, 0:256], c_ckv[i, b * 128:(b + 1) * 128, :], W=[B1])
                k.dma(cst[:, 256:320], c_kr[i, b * 128:(b + 1) * 128, :], W=[B1])
                k.cp(k.act, cb[:, :], cst[:, :], [B1], [B2])
                k.dma(s.kvtok[i][b * 128:(b + 1) * 128, :], cb[:, 0:256], R=[B2], W=[kb])
                p, bp = nps(); pb = psb(p)
                k.tr(pb[:, 0:128], cb[:, 0:128], ident_b[:, :], [B2, b_const], [bp])
                k.tr(pb[:, 128:256], cb[:, 128:256], ident_b[:, :], [B2, b_const], [bp])
                k.tr(pb[:64, 256:384], cb[:, 256:320], ident_b[:, :], [B2, b_const], [bp])
                k.cp(k.dve, cT[:, 0:256], pb[:, 0:256], [bp], [B3])
                k.cp(k.dve, cT[:64, 256:384], pb[:64, 256:384], [bp], [B3])
                k.dma(s.kvT[i][:, :, b * 128:(b + 1) * 128], cT[:, 0:256].rearrange("p (a b) -> p a b", a=2), R=[B3], W=[kb])
                k.dma(s.krT[i][:, b * 128:(b + 1) * 128], cT[:64, 256:384], R=[B3], W=[kb])
            k.barrier()

    def layer_ab(l):
        lw = LW[l]; i = lw["i"]
        s5_setup(i, None)
        with ExitStack() as les:
            k.barrier()
            wkvn = sbs(les, [128, 2, 2048], BF16); WkT = sbs(les, [128, 8, 2, 128], BF16)
            qag = sbs(les, [128, 4], F32); kvg = sbs(les, [128, 256], F32); bglu = sbs(les, [128, 8], F32); dcol = sbs(les, [128, 8], F32)
            car = sbs(les, [128, 2, 32], F32); b_car = Buf()
            b_la = Buf()
            k.dma(wkvn[:], lw["kvb"].ap, R=[lw["kvb"].buf], W=[b_la])
            k.dma(qag[:], q_a_norm[i].rearrange("(c p) -> p c", p=128), W=[b_la], allow_slow_non_contiguous=True)
            k.dma(kvg[:], kv_a_norm[i].partition_broadcast(128), W=[b_la])
            k.dma(bglu[:], b_glu[i].rearrange("(c p) -> p c", p=128), W=[b_la], allow_slow_non_contiguous=True)
            k.dma(dcol[:], s5_d[i].rearrange("g p -> (g p)").rearrange("(c p) -> p c", p=128), W=[b_la], allow_slow_non_contiguous=True)
            for h in range(NH):
                p, bp = nps(); pb = psb(p)
                for half in range(2):
                    k.tr(pb[:, half * 128:(half + 1) * 128], wkvn[:, half, h * 256:h * 256 + 128], ident_b[:, :], [b_la, b_const], [bp])
                k.cp(k.act, WkT[:, h, :, :], pb[:, 0:256].rearrange("p (a b) -> p a b", a=2), [bp], [b_la])
            LS = dict(wkvn=wkvn, WkT=WkT, qag=qag, kvg=kvg, bglu=bglu, dcol=dcol, car=car, b_car=b_car, b_la=b_la)
            for s in seqs:
                if s.past == 0:
                    k.memset(k.pool, car[:], 0.0, [], [b_car])
                else:
                    k.dma(car[:, 0, :], s5re_in[i].rearrange("(gp g2) n -> (g2 n) gp", g2=2), W=[b_car], allow_slow_non_contiguous=True)
                    k.dma(car[:, 1, :], s5im_in[i].rearrange("(gp g2) n -> (g2 n) gp", g2=2), W=[b_car], allow_slow_non_contiguous=True)
                    kv_cache_setup(i, s)
                for ti in range(s.nt):
                    tile_begin(l, s, ti)
                    mixer_ab(l, s, ti, LS)
                    tile_end(l, s, ti)
                k.dma(s.s5re_o[i].rearrange("(gp g2) n -> (g2 n) gp", g2=2), car[:, 0, :], R=[b_car], W=[obuf()], allow_slow_non_contiguous=True)
                k.dma(s.s5im_o[i].rearrange("(gp g2) n -> (g2 n) gp", g2=2), car[:, 1, :], R=[b_car], W=[obuf()], allow_slow_non_contiguous=True)
            k.barrier()

    def mixer_ab(l, s, ti, LS):
        lw = LW[l]; i = lw["i"]; Tt = s.T; TS = min(128, Tt); nsub = Tt // TS
        pos_lo = s.pos0 + ti * Tt
        wkvn, WkT, qag, kvg, bglu, dcol, car, b_car, b_la = (LS[n] for n in ("wkvn", "WkT", "qag", "kvg", "bglu", "dcol", "car", "b_car", "b_la"))
        rmsnorm_fm(xt, b_xt, KC, lambda c: gmix[:, l, c:c + 1], Tt, xn, b_xn)
        with ExitStack() as es:
            k.barrier()
            mixT = sbs(es, [128, 16, TP], BF16); b_mix = bufs(16)
            if cfg.get("s5", True):
              with ExitStack() as e5:
                ubf = sbs(e5, [128, 8, TP], BF16); b_ubf = bufs(8)
                u = ubf; b_u = b_ubf

                def evac_u(mt, p, bp):
                    k.cp(k.act if mt % 2 == 0 else k.dve, ubf[:, mt, :Tt], p[:, :Tt], [bp], [b_ubf[mt]])
                proj_fm(lw["in_u"], lambda kc: xn[:, kc, :Tt], lambda kc: [b_xn[kc]], Tt, evac_u)
                tabs = [sbs(e5, [128, 3, TP], F32) for _ in range(2)]; blks = [sbs(e5, [128, 4, 128], BF16) for _ in range(2)]
                b_tb = bufs(2)
                tw = [sbs(e5, [128, TP], F32) for _ in range(4)]; b_tw = bufs(4)
                Z = [sbs(e5, [128, TP], F32) for _ in range(2)]; b_Z = bufs(2)
                sc = [sbs(e5, [128, TP], F32) for _ in range(2)]; b_sc = bufs(2)
                Sb = [sbs(e5, [128, 2, TP], BF16) for _ in range(2)]; b_Sb = bufs(2)
                yb = [sbs(e5, [128, TP], F32) for _ in range(2)]; b_yb = bufs(2)
                cw = sbs(e5, [128, 8], F32); b_cw = Buf()
                p_y = None
                for gp in range(32):
                    ct = gp // 4; x = gp % 2
                    T_ = tabs[x]; Bk = blks[x]; btb = b_tb[x]
                    if S5L < 0.4:
                        continue
                    k.dma(T_[:, :, :Tt], s5tab[i][gp][:, :, :Tt], R=[s5c_b[i][gp]], W=[btb])
                    k.dma(Bk[:], s5blk[i][gp], R=[s5c_b[i][gp]], W=[btb])
                    if S5L < 0.6:
                        continue
                    C_ = T_[:, 0, :Tt]; S_ = T_[:, 1, :Tt]; Rr = T_[:, 2, :Tt]
                    pr, bpr = nps(); pi_, bpi = nps()
                    k.mm(pr[:, :Tt], Bk[:, 0, :], ubf[:, ct, :Tt], True, True, [btb, b_ubf[ct]], [bpr])
                    k.mm(pi_[:, :Tt], Bk[:, 1, :], ubf[:, ct, :Tt], True, True, [btb, b_ubf[ct]], [bpi])
                    if S5L < 0.8:
                        continue
                    k.tt(k.dve, tw[0][:, :Tt], pr[:, :Tt], C_, ALU.mult, [bpr, btb], [b_tw[0]])
                    k.tt(k.dve, tw[1][:, :Tt], pi_[:, :Tt], S_, ALU.mult, [bpi, btb], [b_tw[1]])
                    k.tt(k.pool, Z[0][:, :Tt], tw[0][:, :Tt], tw[1][:, :Tt], ALU.add, [b_tw[0], b_tw[1]], [b_Z[0]])
                    k.tt(k.dve, tw[2][:, :Tt], pi_[:, :Tt], C_, ALU.mult, [bpi, btb], [b_tw[2]])
                    k.tt(k.dve, tw[3][:, :Tt], pr[:, :Tt], S_, ALU.mult, [bpr, btb], [b_tw[3]])
                    k.tt(k.pool, Z[1][:, :Tt], tw[2][:, :Tt], tw[3][:, :Tt], ALU.subtract, [b_tw[2], b_tw[3]], [b_Z[1]])
                    if S5L < 2:
                        continue
                    for ri in range(2):
                        A_, bA_ = Z[ri], b_Z[ri]
                        B_, bB_ = sc[ri], b_sc[ri]
                        sh = 1
                        while sh < Tt:
                            k.cp(k.act, B_[:, 0:sh], A_[:, 0:sh], [bA_], [bB_])
                            k.stt(k.dve, B_[:, sh:Tt], A_[:, 0:Tt - sh], T_[:, 2, sh - 1:sh], A_[:, sh:Tt], ALU.mult, ALU.add, [bA_, btb], [bB_])
                            A_, bA_, B_, bB_ = B_, bB_, A_, bA_
                            sh *= 2
                        k.stt(k.dve, sc[ri][:, :Tt], Rr, car[:, ri, gp:gp + 1], A_[:, :Tt], ALU.mult, ALU.add, [bA_, btb, b_car], [b_sc[ri]])
                    k.tt(k.pool, tw[0][:, :Tt], sc[0][:, :Tt], C_, ALU.mult, [b_sc[0], btb], [b_tw[0]])
                    k.tt(k.pool, tw[1][:, :Tt], sc[1][:, :Tt], S_, ALU.mult, [b_sc[1], btb], [b_tw[1]])
                    k.tt(k.dve, Sb[x][:, 0, :Tt], tw[0][:, :Tt], tw[1][:, :Tt], ALU.subtract, [b_tw[0], b_tw[1]], [b_Sb[x]])
                    k.tt(k.pool, tw[2][:, :Tt], sc[1][:, :Tt], C_, ALU.mult, [b_sc[1], btb], [b_tw[2]])
                    k.tt(k.pool, tw[3][:, :Tt], sc[0][:, :Tt], S_, ALU.mult, [b_sc[0], btb], [b_tw[3]])
                    k.tt(k.dve, Sb[x][:, 1, :Tt], tw[2][:, :Tt], tw[3][:, :Tt], ALU.add, [b_tw[2], b_tw[3]], [b_Sb[x]])
                    e_ = Tt - 1
                    lastC = T_[:, 0, e_:e_ + 1]; lastS = T_[:, 1, e_:e_ + 1]
                    k.tt(k.dve, cw[:, 0:1], sc[0][:, e_:e_ + 1], lastC, ALU.mult, [b_sc[0], btb], [b_cw])
                    k.tt(k.dve, cw[:, 1:2], sc[1][:, e_:e_ + 1], lastS, ALU.mult, [b_sc[1], btb], [b_cw])
                    k.tt(k.dve, cw[:, 2:3], sc[1][:, e_:e_ + 1], lastC, ALU.mult, [b_sc[1], btb], [b_cw])
                    k.tt(k.dve, cw[:, 3:4], sc[0][:, e_:e_ + 1], lastS, ALU.mult, [b_sc[0], btb], [b_cw])
                    k.tt(k.dve, car[:, 0, gp:gp + 1], cw[:, 0:1], cw[:, 1:2], ALU.subtract, [b_cw], [b_car])
                    k.tt(k.dve, car[:, 1, gp:gp + 1], cw[:, 2:3], cw[:, 3:4], ALU.add, [b_cw], [b_car])
                    if S5L < 3:
                        continue
                    if gp % 4 == 0:
                        (p_y, bp_y), = npa(1)
                    k.mm(p_y[:, :Tt], Bk[:, 2, :], Sb[x][:, 0, :Tt], gp % 4 == 0, False, [btb, b_Sb[x]], [bp_y])
                    k.mm(p_y[:, :Tt], Bk[:, 3, :], Sb[x][:, 1, :Tt], False, gp % 4 == 3, [btb, b_Sb[x]], [bp_y])
                    if gp % 4 == 3:
                        yy = yb[ct % 2]; byy = b_yb[ct % 2]
                        k.stt(k.dve, yy[:, :Tt], u[:, ct, :Tt], dcol[:, ct:ct + 1], p_y[:, :Tt], ALU.mult, ALU.add, [bp_y, b_ubf[ct], b_la], [byy])
                        g1 = tw[0]; g2_ = tw[1]
                        k.tt(k.pool, g1[:, :Tt], yy[:, :Tt], yy[:, :Tt], ALU.mult, [byy], [b_tw[0]])
                        k.ts(k.dve, g1[:, :Tt], g1[:, :Tt], 0.044715, 1.0, ALU.mult, ALU.add, [b_tw[0]], [b_tw[0]])
                        k.tt(k.pool, g1[:, :Tt], g1[:, :Tt], yy[:, :Tt], ALU.mult, [b_tw[0], byy], [b_tw[0]])
                        k.actf(g2_[:, :Tt], g1[:, :Tt], AF.Sigmoid, [b_tw[0]], [b_tw[1]], scale=2.0 * math.sqrt(2.0 / math.pi))
                        k.tt(k.pool, ubf[:, ct, :Tt], yy[:, :Tt], g2_[:, :Tt], ALU.mult, [byy, b_tw[1]], [b_ubf[ct]])
                def evac_glu(mt, p, bp):
                    yy = yb[mt % 2]; byy = b_yb[mt % 2]
                    k.actf(yy[:, :Tt], p[:, :Tt], AF.Sigmoid, [bp, b_la], [byy], bias=bglu[:, mt:mt + 1], scale=1.0)
                    k.tt(k.pool, mixT[:, 8 + mt, :Tt], ubf[:, mt, :Tt], yy[:, :Tt], ALU.mult, [byy, b_ubf[mt]], [b_mix[8 + mt]])
                if S5L >= 4:
                    proj_fm(lw["glu"], lambda kc: ubf[:, kc, :Tt], lambda kc: [b_ubf[kc]], Tt, evac_glu)
                else:
                    for c in range(8, 16):
                        k.memset(k.pool, mixT[:, c, :Tt], 0.0, [], [b_mix[c]])
                k.barrier()
            else:
                for c in range(8, 16):
                    k.memset(k.pool, mixT[:, c, :Tt], 0.0, [], [b_mix[c]])
            with ExitStack() as ea:
                k.barrier()
                cq = sbs(ea, [128, 4, TP], F32); b_cq = bufs(4); cqn = sbs(ea, [128, 4, TP], BF16); b_cqn = bufs(4)
                qn = [sbs(ea, [128, TP], BF16) for _ in range(2)]; b_qn = bufs(2)
                qp = sbs(ea, [128, 8, 2, TP], BF16); b_qp = bufs(8)
                qrT = sbs(ea, [64, 8, TP], BF16); b_qrT = Buf()
                cstab = sbs(ea, [128, 4, 64], F32); b_cs = Buf()
                tk = [sbs(ea, [128, 8, 32], F32) for _ in range(4)]; b_tk = bufs(4)
                qrt = sbs(ea, [128, 8, 64], BF16); b_qrt = Buf()
                ckvn = sbs(ea, [128, 256], F32); ckvb = sbs(ea, [128, 256], BF16); ckvTs = sbs(ea, [128, 2, 128], BF16)
                krn = sbs(ea, [128, 64], F32); krb = sbs(ea, [128, 64], BF16); krTs = sbs(ea, [64, 128], BF16)
                jk = sbs(ea, [128, 256], BF16); stq = sbs(ea, [128, 4], F32)
                b_kw = Buf()
                for sub in range(nsub):
                    k.dma(cstab[:TS, sub, :], h_mla_cs[pos_lo + sub * TS:pos_lo + (sub + 1) * TS, :], W=[b_cs])

                def evac_cq(mt, p, bp):
                    k.cp(k.act, cq[:, mt, :Tt], p[:, :Tt], [bp], [b_cq[mt]])
                proj_fm(lw["in_cq"], lambda kc: xn[:, kc, :Tt], lambda kc: [b_xn[kc]], Tt, evac_cq)

                def evac_kv(sub, p, bp):
                    tok0 = ti * Tt + sub * TS
                    key0 = s.past + tok0
                    kb = s.kv_b[i][key0 // 512]
                    k.actf(jk[:TS, :], p[:TS, 0:256], AF.Square, [bp], [b_kw], accum_out=stq[:TS, 0:1])
                    k.actf(stq[:TS, 1:2], stq[:TS, 0:1], AF.Sqrt, [b_kw], [b_kw], bias=eps_t[:TS, 0:1], scale=1.0 / 256)
                    k.op(k.dve, lambda e: e.reciprocal(out=stq[:TS, 2:3], in_=stq[:TS, 1:2]), [b_kw], [b_kw])
                    k.stt(k.dve, ckvn[:TS, :], p[:TS, 0:256], stq[:TS, 2:3], kvg[:TS, :], ALU.mult, ALU.mult, [bp, b_kw, b_la], [b_kw])
                    k.dma(s.ckv_o[i, tok0:tok0 + TS, :], ckvn[:TS, :], R=[b_kw], W=[obuf()])
                    k.cp(k.act, ckvb[:TS, :], ckvn[:TS, :], [b_kw], [b_kw])
                    k.dma(s.kvtok[i][key0:key0 + TS, :], ckvb[:TS, :], R=[b_kw], W=[kb])
                    p2, bp2 = nps(); pb = psb(p2)
                    for half in range(2):
                        k.tr(pb[:, half * TS:(half + 1) * TS], ckvb[:TS, half * 128:(half + 1) * 128], ident_b[:TS, :TS], [b_kw, b_const], [bp2])
                    k.cp(k.dve, ckvTs[:, :, :TS], pb[:, :2 * TS].rearrange("p (a b) -> p a b", a=2), [bp2], [b_kw])
                    k.dma(s.kvT[i][:, :, key0:key0 + TS], ckvTs[:, :, :TS], R=[b_kw], W=[kb])
                    cosv = cstab[:TS, sub, 0:32]; sinv = cstab[:TS, sub, 32:64]
                    x1 = p[:TS, 256:288]; x2 = p[:TS, 288:320]
                    k.tt(k.dve, tk[0][:TS, 0, :], x1, cosv, ALU.mult, [bp, b_cs], [b_tk[0]])
                    k.tt(k.dve, tk[1][:TS, 0, :], x2, sinv, ALU.mult, [bp, b_cs], [b_tk[1]])
                    k.tt(k.pool, krn[:TS, 0:32], tk[0][:TS, 0, :], tk[1][:TS, 0, :], ALU.subtract, [b_tk[0], b_tk[1]], [b_kw])
                    k.tt(k.dve, tk[2][:TS, 0, :], x1, sinv, ALU.mult, [bp, b_cs], [b_tk[2]])
                    k.tt(k.dve, tk[3][:TS, 0, :], x2, cosv, ALU.mult, [bp, b_cs], [b_tk[3]])
                    k.tt(k.pool, krn[:TS, 32:64], tk[2][:TS, 0, :], tk[3][:TS, 0, :], ALU.add, [b_tk[2], b_tk[3]], [b_kw])
                    k.dma(s.kr_o[i, tok0:tok0 + TS, :], krn[:TS, :], R=[b_kw], W=[obuf()])
                    k.cp(k.act, krb[:TS, :], krn[:TS, :], [b_kw], [b_kw])
                    p3, bp3 = nps(); pb3 = psb(p3)
                    k.tr(pb3[:64, :TS], krb[:TS, :], ident_b[:TS, :TS], [b_kw, b_const], [bp3])
                    k.cp(k.dve, krTs[:, :TS], pb3[:64, :TS], [bp3], [b_kw])
                    k.dma(s.krT[i][:, key0:key0 + TS], krTs[:, :TS], R=[b_kw], W=[kb])
                proj_tm(lw["in_kv"][0], lambda kc, sub: xn[:, kc, sub * TS:(sub + 1) * TS], lambda kc: [b_xn[kc]], TS, nsub, 320, evac_kv)
                rmsnorm_fm(cq, b_cq, 4, lambda c: qag[:, c:c + 1], Tt, cqn, b_cqn)
                v, bw = load_slab(lw["qb_nope"])
                for h in range(NH):
                    p, bp = nps()
                    for c in range(4):
                        k.mm(p[:, :Tt], v[:, c, h * 128:(h + 1) * 128], cqn[:, c, :Tt], c == 0, c == 3, [bw, b_cqn[c]], [bp])
                    x = h % 2
                    k.cp(k.act, qn[x][:, :Tt], p[:, :Tt], [bp], [b_qn[x]])
                    for half in range(2):
                        p2, bp2 = nps()
                        k.mm(p2[:, :Tt], WkT[:, h, half, :], qn[x][:, :Tt], True, True, [b_la, b_qn[x]], [bp2])
                        k.cp(k.dve if half == 0 else k.act, qp[:, h, half, :Tt], p2[:, :Tt], [bp2], [b_qp[h]])

                def evac_qr(sub, p, bp):
                    pv = p[:TS, :512].rearrange("p (h e) -> p h e", h=8)
                    x1 = pv[:, :, 0:32]; x2 = pv[:, :, 32:64]
                    cosb = cstab[:TS, sub:sub + 1, 0:32].broadcast_to([TS, 8, 32]); sinb = cstab[:TS, sub:sub + 1, 32:64].broadcast_to([TS, 8, 32])
                    k.tt(k.dve, tk[0][:TS], x1, cosb, ALU.mult, [bp, b_cs], [b_tk[0]])
                    k.tt(k.dve, tk[1][:TS], x2, sinb, ALU.mult, [bp, b_cs], [b_tk[1]])
                    k.tt(k.pool, qrt[:TS, :, 0:32], tk[0][:TS], tk[1][:TS], ALU.subtract, [b_tk[0], b_tk[1]], [b_qrt])
                    k.tt(k.dve, tk[2][:TS], x1, sinb, ALU.mult, [bp, b_cs], [b_tk[2]])
                    k.tt(k.dve, tk[3][:TS], x2, cosb, ALU.mult, [bp, b_cs], [b_tk[3]])
                    k.tt(k.pool, qrt[:TS, :, 32:64], tk[2][:TS], tk[3][:TS], ALU.add, [b_tk[2], b_tk[3]], [b_qrt])
                    p3, bp3 = nps(); pb3 = psb(p3)
                    for h in range(NH):
                        k.tr(pb3[:64, h * TS:(h + 1) * TS], qrt[:TS, h, :], ident_b[:TS, :TS], [b_qrt, b_const], [bp3])
                    k.cp(k.act, qrT[:, :, sub * TS:(sub + 1) * TS], pb3[:64, :8 * TS].rearrange("p (a b) -> p a b", a=8), [bp3], [b_qrT])
                proj_tm([lw["qb_rope"]], lambda kc, sub: cqn[:, kc, sub * TS:(sub + 1) * TS], lambda kc: [b_cqn[kc]], TS, nsub, 512, evac_qr)
                kbase = s.past + ti * Tt
                blocks = [(b * 128, 128, 0, False) for b in range(kbase // 128)]
                if Tt >= 128:
                    blocks += [(kbase + j * 128, 128, j * 128, True) for j in range(Tt // 128)]
                else:
                    blocks += [(kbase, Tt, 0, False)]
                sbl = {}
                for bi, bl in enumerate(blocks):
                    sbl.setdefault(bl[0] // 512, []).append((bi, bl))
                NKB = 3
                kvTb = [sbs(ea, [128, 2, 512], BF16) for _ in range(NKB)]; krTb = [sbs(ea, [64, 512], BF16) for _ in range(NKB)]
                kvkb = [sbs(ea, [128, 4, 256], BF16) for _ in range(NKB)]; b_kvs = bufs(NKB)
                PTb = [sbs(ea, [128, TP], BF16) for _ in range(3)]; b_PTb = bufs(3)
                recip = sbs(ea, [128, TP], F32); b_rc = Buf(); OLn = sbs(ea, [128, 2, TP], BF16); b_OLn = Buf()
                kvc = 0; ptc = 0
                for h in range(NH):
                    (pO0, bO0), (pO1, bO1), (pSm, bSm) = npa(3)
                    for sbi in sorted(sbl):
                        w = kvc % NKB; kvc += 1
                        lst = sbl[sbi]
                        k0 = sbi * 512
                        nk = sum(bl[1] for _, bl in lst)
                        kbuf = s.kv_b[i][sbi]
                        k.dma(kvTb[w][:, :, :nk], s.kvT[i][:, :, k0:k0 + nk], R=[kbuf], W=[b_kvs[w]])
                        k.dma(krTb[w][:, :nk], s.krT[i][:, k0:k0 + nk], R=[kbuf], W=[b_kvs[w]])
                        for (bi, bl) in lst:
                            o = bl[0] - k0
                            k.dma(kvkb[w][:bl[1], o // 128, :], s.kvtok[i][bl[0]:bl[0] + bl[1], :], R=[kbuf], W=[b_kvs[w]])
                        for (bi, (ks, kn, qlo, diag)) in lst:
                            o = ks - k0
                            qn_ = Tt - qlo
                            pS, bS = nps()
                            k.mm(pS[:kn, :qn_], kvTb[w][:, 0, o:o + kn], qp[:, h, 0, qlo:Tt], True, False, [b_kvs[w], b_qp[h]], [bS])
                            k.mm(pS[:kn, :qn_], kvTb[w][:, 1, o:o + kn], qp[:, h, 1, qlo:Tt], False, False, [b_kvs[w], b_qp[h]], [bS])
                            k.mm(pS[:kn, :qn_], krTb[w][:, o:o + kn], qrT[:, h, qlo:Tt], False, True, [b_kvs[w], b_qrT], [bS])
                            x = ptc % 3; ptc += 1
                            k.actf(PTb[x][:kn, :qn_], pS[:kn, :qn_], AF.Exp, [bS], [b_PTb[x]], scale=MLA_SCALE)
                            if diag:
                                k.memset(k.pool, PTb[x][64:128, 0:64], 0.0, [], [b_PTb[x]])
                            first = bi == 0; last = bi == len(blocks) - 1
                            k.mm(pO0[:, qlo:Tt], kvkb[w][:kn, o // 128, 0:128], PTb[x][:kn, :qn_], first, last, [b_kvs[w], b_PTb[x]], [bO0])
                            k.mm(pO1[:, qlo:Tt], kvkb[w][:kn, o // 128, 128:256], PTb[x][:kn, :qn_], first, last, [b_kvs[w], b_PTb[x]], [bO1])
                            k.mm(pSm[:, qlo:Tt], ones_b[:kn, :], PTb[x][:kn, :qn_], first, last, [b_const, b_PTb[x]], [bSm])
                    k.op(k.dve, lambda e: e.reciprocal(out=recip[:, :Tt], in_=pSm[:, :Tt]), [bSm], [b_rc])
                    k.tt(k.dve, OLn[:, 0, :Tt], pO0[:, :Tt], recip[:, :Tt], ALU.mult, [bO0, b_rc], [b_OLn])
                    k.tt(k.dve, OLn[:, 1, :Tt], pO1[:, :Tt], recip[:, :Tt], ALU.mult, [bO1, b_rc], [b_OLn])
                    pA, bA = nps()
                    for half in range(2):
                        k.mm(pA[:, :Tt], wkvn[:, half, h * 256 + 128:h * 256 + 256], OLn[:, half, :Tt], half == 0, half == 1, [b_la, b_OLn], [bA])
                    k.cp(k.act, mixT[:, h, :Tt], pA[:, :Tt], [bA], [b_mix[h]])
                k.barrier()
            proj_fm(lw["out"], lambda kc: mixT[:, kc, :Tt], lambda kc: [b_mix[kc]], Tt, add_to_x(Tt))
            k.barrier()

    def tile_begin(l, s, ti):
        Tt = s.T
        if l == 0:
            load_x0(s, ti)
        else:
            k.dma(xt[:, :, :Tt], s.xscr[ti], R=[s.xscr_b[ti]], W=b_xt)

    def tile_end(l, s, ti):
        Tt = s.T
        if DO_MLP:
            mlp(l, Tt)
        if l == DEPTH - 1:
            final_out(s, ti)
        else:
            k.dma(s.xscr[ti], xt[:, :, :Tt], R=b_xt, W=[s.xscr_b[ti]])

    for l in range(DEPTH):
        if types[l] == "ab":
            layer_ab(l)
        elif types[l] == "c":
            layer_c(l)
        else:
            for s in seqs:
                for ti in range(s.nt):
                    tile_begin(l, s, ti)
                    tile_end(l, s, ti)
    k.finish(out_bufs)
    k.barrier()
    return nc, k


_WNAMES = ["norm_mix", "norm_mlp", "norm_final", "w_in_ab", "q_a_norm", "kv_a_norm", "w_q_b", "w_kv_b",
           "s5_lam_re", "s5_lam_im", "s5_log_dt", "s5_b_re", "s5_b_im", "s5_c_re", "s5_c_im", "s5_d", "w_glu", "b_glu",
           "w_out_ab", "w_in_c", "ret_gn", "w_out_c", "w_up", "w_down"]


def run_cfg(cfg, inputs, trace=False):
    f = lambda a: np.ascontiguousarray(np.asarray(a), dtype=np.float32)
    xp_all = f(inputs["x_prompt"]); xs_all = f(inputs["x_sample"])
    B, SEQ, _ = xp_all.shape
    DB, DS, _ = xs_all.shape
    PAST = inputs["cache_mla_ckv"].shape[2]
    cfg = dict(cfg); cfg.update(SEQ=SEQ, DS=DS, PAST=PAST)
    if not cfg.get("mlp", True):
        inputs = dict(inputs); inputs["w_up"] = np.zeros((1, 1, 1), np.float32); inputs["w_down"] = np.zeros((1, 1, 1), np.float32)
    hc = host_consts(max(SEQ, PAST + DS))
    cfg["gL"] = hc["gL"]
    nc, k = build(cfg)
    k.nc = nc
    NABd = max(1, sum(1 for t in cfg["types"] if t == "ab")); NCd = max(1, sum(1 for t in cfg["types"] if t == "c"))
    w = {n: f(inputs[n]) for n in _WNAMES}
    def pad0(a, n):
        a = f(a)
        if a.shape[0] == 0:
            return np.zeros((n,) + a.shape[1:], np.float32)
        return a
    for n in list(w):
        if w[n].shape[0] == 0:
            w[n] = np.zeros((1,) + w[n].shape[1:], np.float32)
    consts = {"h_mla_cs": hc["mla_cs"], "h_ret_cos": hc["ret_cos"], "h_ret_sin": hc["ret_sin"], "h_DTt": hc["DTt"],
              "h_dq": hc["dq_rep"], "h_kdec128": hc["kdec128"], "h_kdec64": hc["kdec64"],
              "h_ident_f": hc["ident_f"], "h_ident_b": hc["ident_b"]}
    ckv = pad0(inputs["cache_mla_ckv"], 1); ckr = pad0(inputs["cache_mla_krope"], 1)
    s5r = pad0(inputs["state_s5_re"], 1); s5i = pad0(inputs["state_s5_im"], 1); rst = pad0(inputs["state_ret"], 1)
    in_maps = []
    for c in range(8):
        m = {"xp": xp_all[c % B], "xs": xs_all[c % DB], "c_ckv": np.ascontiguousarray(ckv[:, c % DB]),
             "c_kr": np.ascontiguousarray(ckr[:, c % DB]), "s5re_in": np.ascontiguousarray(s5r[:, c % DB]),
             "s5im_in": np.ascontiguousarray(s5i[:, c % DB]), "ret_in": np.ascontiguousarray(rst[:, c % DB])}
        m.update(w); m.update(consts)
        in_maps.append(m)
    for m in in_maps:
        for n in list(m):
            shp = k.in_shapes.get(n)
            if shp is not None and tuple(m[n].shape) != tuple(shp):
                m[n] = np.zeros(shp, m[n].dtype)
    global _LAST_IN_MAPS
    _LAST_IN_MAPS = in_maps
    if cfg.get("build_only"):
        return None, None, k
    res = run_bass_kernel_spmd(nc, in_maps, core_ids=list(range(8)), trace=trace)
    R = res.results
    NAB = sum(1 for t in cfg["types"] if t == "ab"); NC_ = sum(1 for t in cfg["types"] if t == "c")
    def gat(name, n, cores, lay):
        a = np.stack([np.asarray(R[c][name], dtype=np.float32) for c in cores], axis=0)
        if lay:
            a = np.swapaxes(a, 0, 1)[:n]
        return np.ascontiguousarray(a)
    pc = list(range(B)); sc = list(range(DB))
    outs = (gat("y_p", 0, pc, False), gat("y_s", 0, sc, False),
            gat("ckv_p", NAB, pc, True), gat("kr_p", NAB, pc, True), gat("s5re_p", NAB, pc, True), gat("s5im_p", NAB, pc, True),
            gat("ret_p", NC_, pc, True),
            gat("ckv_s", NAB, sc, True), gat("kr_s", NAB, sc, True), gat("s5re_s", NAB, sc, True), gat("s5im_s", NAB, sc, True),
            gat("ret_s", NC_, sc, True))
    return outs, res, k


def kernel(**inputs):
    depth = np.asarray(inputs["norm_mix"]).shape[0]
    cfg = {"DEPTH": depth, "types": ["ab" if l % 2 == 0 else "c" for l in range(depth)]}
    outs, _, _ = run_cfg(cfg, inputs)
    return outs
```

```python
import math
from contextlib import ExitStack
import numpy as np
import ml_dtypes
import concourse.bass as bass
import concourse.mybir as mybir
from concourse.bass_utils import run_bass_kernel_spmd

F32 = mybir.dt.float32
BF16 = mybir.dt.bfloat16
AF = mybir.ActivationFunctionType
ALU = mybir.AluOpType

D = 2048
KC = 16
EPS = 1e-6
GN_EPS = 1e-5
Q_LORA, KV_LORA, ROPE = 512, 256, 64
NH = 8
S5W = 1024
IN_AB = 1856
DFF = 8192
RH, RDK, RDV = 8, 256, 512
MLA_SCALE = (128 + 64) ** -0.5
SEM_LIMIT = 30000


class Buf:
    __slots__ = ("w", "r")

    def __init__(self):
        self.w = {}
        self.r = {}


def bufs(n):
    return [Buf() for _ in range(n)]


class Eng:
    def __init__(self, nc, e, name, same_sync):
        self.e = e
        self.name = name
        self.sem = nc.alloc_semaphore("s_" + name)
        self.cnt = 0
        self.seen = {}
        self.same_sync = same_sync


class K:
    def __init__(self, nc):
        self.nc = nc
        self.pe = Eng(nc, nc.tensor, "pe", False)
        self.act = Eng(nc, nc.scalar, "act", True)
        self.dve = Eng(nc, nc.vector, "dve", True)
        self.pool = Eng(nc, nc.gpsimd, "pool", True)
        self.sp = Eng(nc, nc.sync, "sp", False)
        self.dma_sems = {}
        self.ninstr = 0

    def _wait(self, E, ev):
        sem, val = ev
        if sem is E.sem and not E.same_sync:
            return
        key = id(sem)
        if E.seen.get(key, 0) >= val:
            return
        E.e.wait_ge(sem, val)
        E.seen[key] = val

    def _deps(self, E, reads, writes):
        for b in reads:
            for ev in b.w.values():
                self._wait(E, ev)
        for b in writes:
            for ev in b.r.values():
                self._wait(E, ev)
            for ev in b.w.values():
                self._wait(E, ev)

    def _mark(self, ev, reads, writes):
        key = id(ev[0])
        for b in reads:
            b.r[key] = ev
        for b in writes:
            b.w[key] = ev
            b.r = {}

    def op(self, E, fn, R=(), W=()):
        if E.cnt >= SEM_LIMIT:
            E.sem = self.nc.alloc_semaphore(f"s_{E.name}_{self.ninstr}")
            E.cnt = 0
        self._deps(E, R, W)
        ins = fn(E.e)
        E.cnt += 1
        ins.then_inc(E.sem, 1)
        self._mark((E.sem, E.cnt), R, W)
        self.ninstr += 1
        return ins

    def dma(self, out, in_, R=(), W=(), Q=None, nsem=12, **kw):
        Q = Q or self.sp
        pool = self.dma_sems.setdefault(Q.name, {"sems": [], "vals": [], "i": 0})
        if len(pool["sems"]) < nsem:
            pool["sems"].append(self.nc.alloc_semaphore(f"d_{Q.name}{len(pool['sems'])}"))
            pool["vals"].append(0)
        i = pool["i"] % len(pool["sems"])
        pool["i"] += 1
        sem = pool["sems"][i]
        if pool["vals"][i] > 0:
            self._wait(Q, (sem, pool["vals"][i]))
        if pool["vals"][i] >= SEM_LIMIT:
            sem = pool["sems"][i] = self.nc.alloc_semaphore(f"d_{Q.name}{i}_{self.ninstr}")
            pool["vals"][i] = 0
        self._deps(Q, R, W)
        ins = Q.e.dma_start(out=out, in_=in_, **kw)
        pool["vals"][i] += 16
        ins.then_inc(sem, 16)
        self._mark((sem, pool["vals"][i]), R, W)
        self.ninstr += 1
        return ins

    def barrier(self):
        engs = [self.pe, self.act, self.dve, self.pool, self.sp]
        evs = [(E.sem, E.cnt) for E in engs if E.cnt > 0]
        for pl in self.dma_sems.values():
            for sem, val in zip(pl["sems"], pl["vals"]):
                if val > 0:
                    evs.append((sem, val))
        for E in engs:
            for ev in evs:
                if ev[0] is not E.sem:
                    self._wait(E, ev)

    def finish(self, bl):
        for b in bl:
            for ev in b.w.values():
                self._wait(self.sp, ev)

    def mm(self, out, lhsT, rhs, start, stop, R, W):
        return self.op(self.pe, lambda e: e.matmul(out, lhsT=lhsT, rhs=rhs, start=start, stop=stop), R, W)

    def tr(self, out, in_, ident, R, W):
        return self.op(self.pe, lambda e: e.transpose(out, in_, ident), R, W)

    def actf(self, out, in_, func, R, W, **kw):
        return self.op(self.act, lambda e: e.activation(out=out, in_=in_, func=func, **kw), R, W)

    def cp(self, E, out, in_, R, W):
        if E is self.act:
            return self.op(E, lambda e: e.copy(out=out, in_=in_), R, W)
        return self.op(E, lambda e: e.tensor_copy(out=out, in_=in_), R, W)

    def tt(self, E, out, in0, in1, op, R, W):
        return self.op(E, lambda e: e.tensor_tensor(out=out, in0=in0, in1=in1, op=op), R, W)

    def ts(self, E, out, in0, s1, s2, op0, op1, R, W):
        if op1 is None:
            return self.op(E, lambda e: e.tensor_scalar(out=out, in0=in0, scalar1=s1, scalar2=None, op0=op0), R, W)
        return self.op(E, lambda e: e.tensor_scalar(out=out, in0=in0, scalar1=s1, scalar2=s2, op0=op0, op1=op1), R, W)

    def stt(self, E, out, in0, scalar, in1, op0, op1, R, W):
        return self.op(E, lambda e: e.scalar_tensor_tensor(out=out, in0=in0, scalar=scalar, in1=in1, op0=op0, op1=op1), R, W)

    def memset(self, E, ap, val, R, W):
        return self.op(E, lambda e: e.memset(ap, val), R, W)


class Slab:
    __slots__ = ("ap", "buf", "nk", "mw")

    def __init__(self, ap, nk, mw):
        self.ap = ap
        self.buf = Buf()
        self.nk = nk
        self.mw = mw


def host_consts(maxpos):
    half = 32
    inv = 10000.0 ** (-np.arange(half, dtype=np.float64) / half)
    pos = np.arange(maxpos, dtype=np.float64)
    ang = pos[:, None] * inv[None, :]
    mla_cs = np.concatenate([np.cos(ang), np.sin(ang)], axis=1).astype(np.float32)
    half = 128
    inv = 10000.0 ** (-np.arange(half, dtype=np.float64) / half)
    ang = inv[:, None] * pos[None, :]
    ret_cos = np.cos(ang).astype(np.float32)
    ret_sin = np.sin(ang).astype(np.float32)
    logg = np.log(1.0 - 2.0 ** (-5.0 - np.arange(RH, dtype=np.float64)))
    L = 128
    idx = np.arange(L, dtype=np.float64)
    diff = idx[None, :] - idx[:, None]
    DT = np.where(diff >= 0, np.exp(logg[:, None, None] * np.maximum(diff, 0.0)), 0.0) * RDK ** -0.5
    DTt = np.ascontiguousarray(DT.transpose(1, 0, 2)).astype(np.float32)
    dq = np.exp(logg[:, None] * (idx[None, :] + 1.0))
    dq_rep = np.broadcast_to(dq[None], (128, RH, L)).astype(np.float32).copy()
    kdec = {}
    for LL in (128, 64):
        ii = np.arange(LL, dtype=np.float64)
        kd = np.exp(logg[None, :] * (LL - 1.0 - ii[:, None])) * RDK ** -0.5
        full = np.zeros((128, RH), np.float32)
        full[:LL] = kd
        kdec[LL] = full
    gL = {LL: [float(np.exp(logg[h] * LL)) for h in range(RH)] for LL in (128, 64)}
    return dict(mla_cs=mla_cs, ret_cos=ret_cos, ret_sin=ret_sin, DTt=DTt, dq_rep=dq_rep,
                kdec128=kdec[128], kdec64=kdec[64], gL=gL,
                ident_f=np.eye(128, dtype=np.float32), ident_b=np.eye(128, dtype=np.float32).astype(ml_dtypes.bfloat16))


class Seq:
    pass


def build(cfg):
    SEQ, DS, PAST, DEPTH = cfg["SEQ"], cfg["DS"], cfg["PAST"], cfg["DEPTH"]
    types = cfg["types"]
    NAB = sum(1 for t in types if t == "ab")
    NC_ = sum(1 for t in types if t == "c")
    NABd, NCd = max(NAB, 1), max(NC_, 1)
    DO_MLP = cfg.get("mlp", True)
    S5L = cfg.get("s5l", 4)
    USE_HW_SCAN = True
    TP = cfg.get("T", 512)
    maxpos = max(SEQ, PAST + DS)
    nc = bass.Bass("TRN2", target_bir_lowering=False)
    k = K(nc)
    k.in_shapes = {}
    gLtab = cfg["gL"]
    HALFPI = math.pi / 2

    def din(name, shape, dt=F32, used=True):
        if not used:
            shape = [1] * len(shape)
        k.in_shapes[name] = tuple(shape)
        return nc.dram_tensor(name, list(shape), dt, kind="ExternalInput").ap()

    def dout(name, shape, dt=F32):
        return nc.dram_tensor(name, list(shape), dt, kind="ExternalOutput").ap()

    def dscr(name, shape, dt):
        return nc.dram_tensor(name, list(shape), dt, kind="Internal").ap()

    uab, uc = NAB > 0, NC_ > 0
    xp = din("xp", [SEQ, D]); xs = din("xs", [DS, D])
    c_ckv = din("c_ckv", [NABd, PAST, KV_LORA], used=uab); c_kr = din("c_kr", [NABd, PAST, ROPE], used=uab)
    s5re_in = din("s5re_in", [NABd, 64, 64], used=uab); s5im_in = din("s5im_in", [NABd, 64, 64], used=uab)
    ret_in = din("ret_in", [NCd, RH, RDK, RDV], used=uc)
    norm_mix = din("norm_mix", [DEPTH, D]); norm_mlp = din("norm_mlp", [DEPTH, D]); norm_final = din("norm_final", [D])
    w_in_ab = din("w_in_ab", [NABd, D, IN_AB], used=uab); q_a_norm = din("q_a_norm", [NABd, Q_LORA], used=uab)
    kv_a_norm = din("kv_a_norm", [NABd, KV_LORA], used=uab)
    w_q_b = din("w_q_b", [NABd, Q_LORA, NH * 192], used=uab); w_kv_b = din("w_kv_b", [NABd, KV_LORA, NH * 256], used=uab)
    lam_re = din("s5_lam_re", [NABd, 64, 64], used=uab); lam_im = din("s5_lam_im", [NABd, 64, 64], used=uab)
    log_dt = din("s5_log_dt", [NABd, 64], used=uab)
    b_re = din("s5_b_re", [NABd, 64, 64, 16], used=uab); b_im = din("s5_b_im", [NABd, 64, 64, 16], used=uab)
    c_re = din("s5_c_re", [NABd, 64, 16, 64], used=uab); c_im = din("s5_c_im", [NABd, 64, 16, 64], used=uab)
    s5_d = din("s5_d", [NABd, 64, 16], used=uab)
    w_glu = din("w_glu", [NABd, S5W, S5W], used=uab); b_glu = din("b_glu", [NABd, S5W], used=uab)
    w_out_ab = din("w_out_ab", [NABd, D, D], used=uab)
    w_in_c = din("w_in_c", [NCd, D, 12288], used=uc); ret_gn = din("ret_gn", [NCd, RH * RDV], used=uc)
    w_out_c = din("w_out_c", [NCd, RH * RDV, D], used=uc)
    w_up = din("w_up", [DEPTH, D, DFF], used=DO_MLP); w_down = din("w_down", [DEPTH, DFF, D], used=DO_MLP)
    h_mla_cs = din("h_mla_cs", [maxpos, 64]); h_ret_cos = din("h_ret_cos", [128, maxpos]); h_ret_sin = din("h_ret_sin", [128, maxpos])
    h_DTt = din("h_DTt", [128, RH, 128]); h_dq = din("h_dq", [128, RH, 128])
    h_kdec128 = din("h_kdec128", [128, RH]); h_kdec64 = din("h_kdec64", [128, RH])
    h_ident_f = din("h_ident_f", [128, 128]); h_ident_b = din("h_ident_b", [128, 128], BF16)

    y_p = dout("y_p", [SEQ, D]); y_s = dout("y_s", [DS, D])
    ckv_p = dout("ckv_p", [NABd, SEQ, KV_LORA]); kr_p = dout("kr_p", [NABd, SEQ, ROPE])
    s5re_p = dout("s5re_p", [NABd, 64, 64]); s5im_p = dout("s5im_p", [NABd, 64, 64]); ret_p = dout("ret_p", [NCd, RH, RDK, RDV])
    ckv_s = dout("ckv_s", [NABd, DS, KV_LORA]); kr_s = dout("kr_s", [NABd, DS, ROPE])
    s5re_s = dout("s5re_s", [NABd, 64, 64]); s5im_s = dout("s5im_s", [NABd, 64, 64]); ret_s = dout("ret_s", [NCd, RH, RDK, RDV])
    out_bufs = []

    def obuf():
        b = Buf(); out_bufs.append(b)
        return b

    sp_ = Seq(); sp_.name = "p"; sp_.x = xp; sp_.y = y_p; sp_.T = TP; sp_.nt = SEQ // TP; sp_.pos0 = 0; sp_.past = 0
    sp_.ckv_o, sp_.kr_o, sp_.s5re_o, sp_.s5im_o, sp_.ret_o = ckv_p, kr_p, s5re_p, s5im_p, ret_p
    ss_ = Seq(); ss_.name = "s"; ss_.x = xs; ss_.y = y_s; ss_.T = DS; ss_.nt = 1; ss_.pos0 = PAST; ss_.past = PAST
    ss_.ckv_o, ss_.kr_o, ss_.s5re_o, ss_.s5im_o, ss_.ret_o = ckv_s, kr_s, s5re_s, s5im_s, ret_s
    seqs = [sp_, ss_]
    for s in seqs:
        s.nkeys = s.past + s.nt * s.T
        s.xscr = dscr(f"xscr_{s.name}", [s.nt, 128, KC, s.T], F32)
        s.xscr_b = bufs(s.nt)
        s.kvT = [dscr(f"kvT_{s.name}{i}", [128, 2, s.nkeys], BF16) for i in range(NAB)]
        s.krT = [dscr(f"krT_{s.name}{i}", [64, s.nkeys], BF16) for i in range(NAB)]
        s.kvtok = [dscr(f"kvtok_{s.name}{i}", [s.nkeys, KV_LORA], BF16) for i in range(NAB)]
        nsb = (s.nkeys + 511) // 512
        s.kv_b = [[Buf() for _ in range(nsb)] for _ in range(NAB)]
    s5blk = [dscr(f"s5blk{i}", [32, 128, 4, 128], BF16) for i in range(NAB)]
    s5tab = [dscr(f"s5tab{i}", [32, 128, 3, 512], F32) for i in range(NAB)]
    s5c_b = [[Buf() for _ in range(32)] for _ in range(NAB)]

    def sb(name, shape, dt):
        return nc.alloc_sbuf_tensor(name, list(shape), dt)

    uid = [0]

    def sbs(es, shape, dt):
        uid[0] += 1
        return es.enter_context(nc.sbuf_tensor(f"t{uid[0]}", list(shape), dt))

    xt = sb("xt", [128, KC, TP], F32); b_xt = bufs(KC)
    xn = sb("xn", [128, KC, TP], BF16); b_xn = bufs(KC)
    WS = 2
    wsl = [sb(f"wsl{i}", [128, 16 * 512], BF16) for i in range(WS)]; b_wsl = bufs(WS)
    ident_f = sb("ident_f", [128, 128], F32); ident_b = sb("ident_b", [128, 128], BF16); ones_b = sb("ones_b", [128, 128], BF16)
    b_const = Buf()
    gmix = sb("gmix", [128, DEPTH, KC], F32); gmlp = sb("gmlp", [128, DEPTH, KC], F32)
    rstd = sb("rstd", [128, TP], F32); b_rstd = Buf()
    sqb = [sb(f"sqb{i}", [128, 4, TP], BF16) for i in range(2)]; b_sqb = bufs(2)
    eps_t = sb("eps_t", [128, 4], F32)
    NPS = 8
    ps = [nc.alloc_psum_tensor(f"ps{i}", [128, 512], F32) for i in range(NPS)]; b_ps = bufs(NPS)
    st = {"ps": 0, "ws": 0, "ce": 0}

    st["pa"] = 0

    def nps():
        i = 4 + st["ps"] % 4
        st["ps"] += 1
        return ps[i], b_ps[i]

    def npa(n):
        r = []
        for j in range(n):
            i = (st["pa"] + j) % 4
            r.append((ps[i], b_ps[i]))
        st["pa"] += n
        return r

    def psb(p):
        return p[:].bitcast(BF16)

    k.dma(ident_f[:], h_ident_f, W=[b_const])
    k.dma(ident_b[:], h_ident_b, W=[b_const])
    k.memset(k.pool, ones_b[:], 1.0, [], [b_const])
    k.dma(gmix[:], norm_mix.rearrange("l (c p) -> p l c", p=128), W=[b_const], allow_slow_non_contiguous=True)
    k.dma(gmlp[:], norm_mlp.rearrange("l (c p) -> p l c", p=128), W=[b_const], allow_slow_non_contiguous=True)
    k.memset(k.pool, eps_t[:, 0:1], EPS, [], [b_const])
    k.memset(k.pool, eps_t[:, 1:2], GN_EPS, [], [b_const])
    k.memset(k.pool, eps_t[:, 2:3], HALFPI, [], [b_const])
    k.memset(k.pool, eps_t[:, 3:4], 0.0, [], [b_const])

    slab_id = [0]

    def mk_slab(nk, mw):
        slab_id[0] += 1
        return Slab(dscr(f"slab{slab_id[0]}", [128, nk, mw], BF16), nk, mw)

    cast_engs = [k.pool, k.act, k.dve]

    def precast_all(jobs):
        with nc.sbuf_tensor("stg0", [128, 8192], F32) as s0, nc.sbuf_tensor("stg1", [128, 8192], F32) as s1:
            stg = [s0, s1]; b_stg = bufs(2)
            for n, job in enumerate(jobs):
                src3, slab = job[0], job[1]
                s = n % 2
                nk, mw = slab.nk, slab.mw
                sv = stg[s][:, :nk * mw].rearrange("p (c m) -> p c m", c=nk)
                if len(job) > 2:
                    sv = sv.rearrange(job[2], **job[3])
                    for c_ in range(nk):
                        k.dma(sv[:, c_], src3[:, c_], W=[b_stg[s]])
                else:
                    k.dma(sv, src3, W=[b_stg[s]])
                w = st["ws"] % WS; st["ws"] += 1
                wv = wsl[w][:, :nk * mw]
                E = cast_engs[n % 3]
                k.cp(E, wv, stg[s][:, :nk * mw], [b_stg[s]], [b_wsl[w]])
                k.dma(slab.ap, wv.rearrange("p (c m) -> p c m", c=nk), R=[b_wsl[w]], W=[slab.buf])
        k.barrier()

    def slabs_2d(W2, K_, M_, kgrp=16, mgrp=512):
        jobs, grid = [], []
        for m0 in range(0, M_, mgrp):
            mw = min(mgrp, M_ - m0)
            row = []
            for k0 in range(0, K_ // 128, kgrp):
                nk = min(kgrp, K_ // 128 - k0)
                sl = mk_slab(nk, mw)
                jobs.append((W2[k0 * 128:(k0 + nk) * 128, m0:m0 + mw].rearrange("(c p) m -> p c m", p=128), sl))
                row.append(sl)
            grid.append(row)
        return jobs, grid

    jobs = []
    LW = []
    iab = ic = 0
    for l in range(DEPTH):
        lw = {}
        if types[l] == "ab":
            i = iab; iab += 1
            lw["i"] = i
            j, lw["in_cq"] = slabs_2d(w_in_ab[i][:, 0:512], D, 512); jobs += j
            j, lw["in_kv"] = slabs_2d(w_in_ab[i][:, 512:832], D, 320); jobs += j
            j, lw["in_u"] = slabs_2d(w_in_ab[i][:, 832:1856], D, 1024); jobs += j
            wq = w_q_b[i].rearrange("(c p) (h e) -> p c h e", p=128, e=192)
            sl = mk_slab(4, 1024); jobs.append((wq[:, :, :, 0:128], sl, "p c (h e) -> p c h e", dict(h=8))); lw["qb_nope"] = sl
            sl = mk_slab(4, 512); jobs.append((wq[:, :, :, 128:192], sl, "p c (h e) -> p c h e", dict(h=8))); lw["qb_rope"] = sl
            sl = mk_slab(2, 2048); jobs.append((w_kv_b[i].rearrange("(c p) m -> p c m", p=128), sl)); lw["kvb"] = sl
            j, lw["glu"] = slabs_2d(w_glu[i], S5W, S5W); jobs += j
            j, lw["out"] = slabs_2d(w_out_ab[i], D, D); jobs += j
        elif types[l] == "c":
            i = ic; ic += 1
            lw["i"] = i
            lw["qk"], lw["v"], lw["g"], lw["out"] = [], [], [], []
            for h in range(RH):
                sl = mk_slab(16, 512)
                src = w_in_c[i][:, 0:4096].rearrange("(c p) (two h e) -> p c two h e", p=128, two=2, e=256)[:, :, :, h, :]
                jobs.append((src, sl, "p c (t e) -> p c t e", dict(t=2))); lw["qk"].append(sl)
                sl = mk_slab(16, 512); jobs.append((w_in_c[i][:, 4096 + h * 512:4096 + (h + 1) * 512].rearrange("(c p) m -> p c m", p=128), sl)); lw["v"].append(sl)
                sl = mk_slab(16, 512); jobs.append((w_in_c[i][:, 8192 + h * 512:8192 + (h + 1) * 512].rearrange("(c p) m -> p c m", p=128), sl)); lw["g"].append(sl)
                j, g = slabs_2d(w_out_c[i][h * 512:(h + 1) * 512, :], 512, D); jobs += j; lw["out"].append(g)
        if DO_MLP:
            j, lw["up"] = slabs_2d(w_up[l], D, DFF); jobs += j
            j, lw["down"] = slabs_2d(w_down[l], DFF, D); jobs += j
        LW.append(lw)
    precast_all(jobs)

    def load_slab(sl):
        w = st["ws"] % WS; st["ws"] += 1
        v = wsl[w][:, :sl.nk * sl.mw].rearrange("p (c m) -> p c m", c=sl.nk)
        k.dma(v, sl.ap, R=[sl.buf], W=[b_wsl[w]])
        return v, b_wsl[w]

    def proj_fm(grid, rhs_fn, rhs_bufs, Tt, evac, mt_w=128):
        mt = 0
        for row in grid:
            mw = row[0].mw
            nmt = mw // mt_w
            pss = npa(nmt)
            nkt = sum(sl.nk for sl in row)
            kc0 = 0
            for sl in row:
                v, bw = load_slab(sl)
                for j in range(nmt):
                    p, bp = pss[j]
                    for c in range(sl.nk):
                        kc = kc0 + c
                        k.mm(p[:mt_w, :Tt], v[:, c, j * mt_w:(j + 1) * mt_w], rhs_fn(kc), kc == 0, kc == nkt - 1,
                             [bw] + rhs_bufs(kc), [bp])
                kc0 += sl.nk
            for j in range(nmt):
                evac(mt, pss[j][0], pss[j][1])
                mt += 1

    def proj_tm(row, lhs_fn, lhs_bufs, TS, nsub, ncols, evac):
        pss = npa(nsub)
        nkt = sum(sl.nk for sl in row)
        kc0 = 0
        for sl in row:
            v, bw = load_slab(sl)
            for sub in range(nsub):
                p, bp = pss[sub]
                for c in range(sl.nk):
                    kc = kc0 + c
                    k.mm(p[:TS, :ncols], lhs_fn(kc, sub), v[:, c, :ncols], kc == 0, kc == nkt - 1, [bw] + lhs_bufs(kc), [bp])
            kc0 += sl.nk
        for sub in range(nsub):
            evac(sub, pss[sub][0], pss[sub][1])

    def rmsnorm_fm(src, b_src, nchunk, gain, Tt, dst, b_dst):
        p, bp = nps()
        for g0 in range(0, nchunk, 4):
            s = st["ce"] % 2; st["ce"] += 1
            k.actf(sqb[s][:, :, :Tt], src[:, g0:g0 + 4, :Tt], AF.Square, b_src[g0:g0 + 4], [b_sqb[s]])
            for c in range(4):
                k.mm(p[:, :Tt], ones_b[:], sqb[s][:, c, :Tt], g0 + c == 0, g0 + c == nchunk - 1, [b_sqb[s], b_const], [bp])
        k.actf(rstd[:, :Tt], p[:, :Tt], AF.Sqrt, [bp], [b_rstd], bias=eps_t[:, 0:1], scale=1.0 / (nchunk * 128))
        k.op(k.dve, lambda e: e.reciprocal(out=rstd[:, :Tt], in_=rstd[:, :Tt]), [b_rstd], [b_rstd])
        for c in range(nchunk):
            k.stt(k.dve, dst[:, c, :Tt], src[:, c, :Tt], gain(c), rstd[:, :Tt], ALU.mult, ALU.mult, [b_src[c], b_rstd, b_const], [b_dst[c]])

    def add_to_x(Tt):
        def ev(mt, p, bp):
            k.tt(k.dve, xt[:, mt, :Tt], xt[:, mt, :Tt], p[:, :Tt], ALU.add, [bp, b_xt[mt]], [b_xt[mt]])
        return ev

    def load_x0(s, ti):
        Tt = s.T; TS = min(128, Tt); nsub = Tt // TS
        with ExitStack() as es:
            k.barrier()
            xtok = sbs(es, [128, D], F32); b_xtok = Buf()
            for sub in range(nsub):
                t0 = ti * Tt + sub * TS
                k.dma(xtok[:TS, :], s.x[t0:t0 + TS, :], W=[b_xtok])
                for c4 in range(4):
                    p, bp = nps()
                    for j in range(4):
                        c = c4 * 4 + j
                        k.tr(p[:, j * TS:(j + 1) * TS], xtok[:TS, c * 128:(c + 1) * 128], ident_f[:TS, :TS], [b_xtok, b_const], [bp])
                    E = k.act if c4 % 2 == 0 else k.dve
                    k.cp(E, xt[:, c4 * 4:c4 * 4 + 4, sub * TS:(sub + 1) * TS], p[:, :4 * TS].rearrange("p (a b) -> p a b", a=4),
                         [bp], b_xt[c4 * 4:c4 * 4 + 4])
            k.barrier()

    def final_out(s, ti):
        Tt = s.T; TS = min(128, Tt); nsub = Tt // TS
        with ExitStack() as es:
            k.barrier()
            xtok = sbs(es, [128, D], F32); gfin = sbs(es, [128, D], F32); junk = sbs(es, [128, D], BF16); ssum = sbs(es, [128, 4], F32)
            b_xtok = Buf(); b_fin = Buf(); b_g = Buf()
            k.dma(gfin[:], norm_final.partition_broadcast(128), W=[b_g])
            for sub in range(nsub):
                t0 = ti * Tt + sub * TS
                for c4 in range(4):
                    p, bp = nps()
                    for j in range(4):
                        c = c4 * 4 + j
                        k.tr(p[:TS, j * 128:(j + 1) * 128], xt[:, c, sub * TS:(sub + 1) * TS], ident_f[:, :], [b_xt[c], b_const], [bp])
                    E = k.act if c4 % 2 == 0 else k.dve
                    k.cp(E, xtok[:TS, c4 * 512:(c4 + 1) * 512], p[:TS, :], [bp], [b_xtok])
                k.actf(junk[:TS, :], xtok[:TS, :], AF.Square, [b_xtok], [b_fin], accum_out=ssum[:TS, 0:1])
                k.actf(ssum[:TS, 1:2], ssum[:TS, 0:1], AF.Sqrt, [b_fin], [b_fin], bias=eps_t[:TS, 0:1], scale=1.0 / D)
                k.op(k.dve, lambda e: e.reciprocal(out=ssum[:TS, 2:3], in_=ssum[:TS, 1:2]), [b_fin], [b_fin])
                k.stt(k.dve, xtok[:TS, :], xtok[:TS, :], ssum[:TS, 2:3], gfin[:TS, :], ALU.mult, ALU.mult, [b_xtok, b_fin, b_g], [b_xtok])
                k.dma(s.y[t0:t0 + TS, :], xtok[:TS, :], R=[b_xtok], W=[obuf()])
            k.barrier()

    def mlp(l, Tt):
        lw = LW[l]
        with ExitStack() as es:
            k.barrier()
            hT = sbs(es, [128, 16, TP], BF16); tmp = [sbs(es, [128, TP], F32) for _ in range(2)]
            b_hT = bufs(16); b_tmp = bufs(2)
            rmsnorm_fm(xt, b_xt, KC, lambda c: gmlp[:, l, c:c + 1], Tt, xn, b_xn)
            for j in range(DFF // 2048):
                def evac_up(mt, p, bp):
                    s = mt % 2
                    k.actf(tmp[s][:, :Tt], p[:, :Tt], AF.Relu, [bp], [b_tmp[s]])
                    E = k.dve if mt % 2 == 0 else k.pool
                    k.tt(E, hT[:, mt, :Tt], tmp[s][:, :Tt], tmp[s][:, :Tt], ALU.mult, [b_tmp[s]], [b_hT[mt]])
                proj_fm([[lw["up"][j * 4 + q][0]] for q in range(4)], lambda kc: xn[:, kc, :Tt], lambda kc: [b_xn[kc]], Tt, evac_up)
                proj_fm([[lw["down"][q][j]] for q in range(4)], lambda kc: hT[:, kc, :Tt], lambda kc: [b_hT[kc]], Tt, add_to_x(Tt))
            k.barrier()

    def rope_fm(p0, bp0, p1, bp1, rc, rs_, b_rope, ta, b_ta, dst, b_dst, Tt):
        k.tt(k.dve, ta[0][:, :Tt], p0[:, :Tt], rc[:, :Tt], ALU.mult, [bp0, b_rope], [b_ta[0]])
        k.tt(k.dve, ta[1][:, :Tt], p1[:, :Tt], rs_[:, :Tt], ALU.mult, [bp1, b_rope], [b_ta[1]])
        k.tt(k.pool, dst[:, 0, :Tt], ta[0][:, :Tt], ta[1][:, :Tt], ALU.subtract, [b_ta[0], b_ta[1]], [b_dst[0]])
        k.tt(k.dve, ta[2][:, :Tt], p0[:, :Tt], rs_[:, :Tt], ALU.mult, [bp0, b_rope], [b_ta[2]])
        k.tt(k.dve, ta[3][:, :Tt], p1[:, :Tt], rc[:, :Tt], ALU.mult, [bp1, b_rope], [b_ta[3]])
        k.tt(k.pool, dst[:, 1, :Tt], ta[2][:, :Tt], ta[3][:, :Tt], ALU.add, [b_ta[2], b_ta[3]], [b_dst[1]])

    def layer_c(l):
        lw = LW[l]; i = lw["i"]
        with ExitStack() as les:
            k.barrier()
            Sst = sbs(les, [128, RH, 2, 512], F32); b_S = bufs(RH)
            DTt = sbs(les, [128, RH, 128], F32); dq = sbs(les, [128, RH, 128], F32)
            kd128 = sbs(les, [128, RH], F32); kd64 = sbs(les, [128, RH], F32)
            b_lc = Buf()
            k.dma(DTt[:], h_DTt, W=[b_lc]); k.dma(dq[:], h_dq, W=[b_lc])
            k.dma(kd128[:], h_kdec128, W=[b_lc]); k.dma(kd64[:], h_kdec64, W=[b_lc])
            for s in seqs:
                if s.past == 0:
                    for h in range(RH):
                        k.memset(k.pool, Sst[:, h, :, :], 0.0, [], [b_S[h]])
                else:
                    for h in range(RH):
                        k.dma(Sst[:, h, :, :], ret_in[i, h].rearrange("(c p) v -> p c v", p=128), W=[b_S[h]])
                for ti in range(s.nt):
                    tile_begin(l, s, ti)
                    mixer_c(l, s, ti, Sst, b_S, DTt, dq, kd128 if s.T >= 128 else kd64, b_lc)
                    tile_end(l, s, ti)
                for h in range(RH):
                    k.dma(s.ret_o[i, h].rearrange("(c p) v -> p c v", p=128), Sst[:, h, :, :], R=[b_S[h]], W=[obuf()])
            k.barrier()

    def mixer_c(l, s, ti, Sst, b_S, DTt, dq, kdec, b_lc):
        lw = LW[l]; i = lw["i"]; Tt = s.T; L = min(128, Tt); nch = Tt // L
        pos_lo = s.pos0 + ti * Tt
        gLs = gLtab[L]
        rmsnorm_fm(xt, b_xt, KC, lambda c: gmix[:, l, c:c + 1], Tt, xn, b_xn)
        with ExitStack() as es:
            k.barrier()
            rc = sbs(es, [128, TP], F32); rs_ = sbs(es, [128, TP], F32); b_rope = Buf()
            k.dma(rc[:, :Tt], h_ret_cos[:, pos_lo:pos_lo + Tt], W=[b_rope])
            k.dma(rs_[:, :Tt], h_ret_sin[:, pos_lo:pos_lo + Tt], W=[b_rope])
            qr = sbs(es, [128, 2, TP], BF16); kr_ = sbs(es, [128, 2, TP], BF16); qt = sbs(es, [128, 2, TP], BF16)
            b_qr = bufs(2); b_kr = bufs(2); b_qt = bufs(2)
            ta = [sbs(es, [128, TP], F32) for _ in range(4)]; b_ta = bufs(4)
            vt = sbs(es, [128, 4, 512], BF16); gt = sbs(es, [128, 4, 512], BF16); ktk = sbs(es, [128, 4, 256], BF16)
            b_vt = bufs(4); b_gt = bufs(4); b_ktk = bufs(4)
            PT = [sbs(es, [128, 128], BF16) for _ in range(2)]; b_PT = bufs(2)
            onf = [sbs(es, [128, 512], F32) for _ in range(2)]; b_onf = bufs(2)
            onb = [sbs(es, [128, 512], BF16) for _ in range(2)]; b_onb = bufs(2)
            onT = sbs(es, [128, 4, TP], BF16); b_onT = bufs(4)
            Sbf = sbs(es, [128, 2, 512], BF16); b_Sbf = bufs(2)
            gnr = [sbs(es, [128, 512], F32) for _ in range(2)]; b_gnr = bufs(2)
            stats = [sbs(es, [128, 16], F32) for _ in range(2)]; b_stats = bufs(2)
            cnt = 0
            for h in range(RH):
                g_ = h % 2
                k.dma(gnr[g_][:], ret_gn[i, h * 512:(h + 1) * 512].partition_broadcast(128), W=[b_gnr[g_]])
                hold = []

                def evac_qk(mt, p, bp):
                    hold.append((p, bp))
                    if mt == 1:
                        rope_fm(hold[0][0], hold[0][1], hold[1][0], hold[1][1], rc, rs_, b_rope, ta, b_ta, qr, b_qr, Tt)
                    if mt == 3:
                        rope_fm(hold[2][0], hold[2][1], hold[3][0], hold[3][1], rc, rs_, b_rope, ta, b_ta, kr_, b_kr, Tt)
                proj_fm([[lw["qk"][h]]], lambda kc: xn[:, kc, :Tt], lambda kc: [b_xn[kc]], Tt, evac_qk)
                for i2 in range(2):
                    k.tt(k.pool, qt[:, i2, :Tt].rearrange("p (c n) -> p c n", n=L), qr[:, i2, :Tt].rearrange("p (c n) -> p c n", n=L),
                         dq[:, h:h + 1, :L].broadcast_to([128, nch, L]), ALU.mult, [b_qr[i2], b_lc], [b_qt[i2]])

                def evac_v(sub, p, bp):
                    k.cp(k.act, vt[:L, sub, :], p[:L, :512], [bp], [b_vt[sub]])

                def evac_g(sub, p, bp):
                    k.actf(gt[:L, sub, :], p[:L, :512], AF.Silu, [bp], [b_gt[sub]])
                lhs = lambda kc, sub: xn[:, kc, sub * L:(sub + 1) * L]
                proj_tm([lw["v"][h]], lhs, lambda kc: [b_xn[kc]], L, nch, 512, evac_v)
                proj_tm([lw["g"][h]], lhs, lambda kc: [b_xn[kc]], L, nch, 512, evac_g)
                for sub in range(nch):
                    p, bp = nps(); pb = psb(p)
                    for i2 in range(2):
                        k.tr(pb[:L, i2 * 128:(i2 + 1) * 128], kr_[:, i2, sub * L:(sub + 1) * L], ident_b[:, :], [b_kr[i2], b_const], [bp])
                    k.ts(k.dve, ktk[:L, sub, :], pb[:L, :256], kdec[:L, h:h + 1], None, ALU.mult, None, [bp, b_lc], [b_ktk[sub]])
                for i2 in range(2):
                    k.cp(k.act, Sbf[:, i2, :], Sst[:, h, i2, :], [b_S[h]], [b_Sbf[i2]])
                for ci in range(nch):
                    c0 = ci * L
                    x = cnt % 2; cnt += 1
                    p_s, bp_s = nps()
                    for i2 in range(2):
                        k.mm(p_s[:L, :L], kr_[:, i2, c0:c0 + L], qr[:, i2, c0:c0 + L], i2 == 0, i2 == 1, [b_kr[i2], b_qr[i2]], [bp_s])
                    k.tt(k.dve, PT[x][:L, :L], p_s[:L, :L], DTt[:L, h, :L], ALU.mult, [bp_s, b_lc], [b_PT[x]])
                    p_o, bp_o = nps()
                    k.mm(p_o[:L, :512], PT[x][:L, :L], vt[:L, ci, :], True, False, [b_PT[x], b_vt[ci]], [bp_o])
                    for i2 in range(2):
                        k.mm(p_o[:L, :512], qt[:, i2, c0:c0 + L], Sbf[:, i2, :], False, i2 == 1, [b_qt[i2], b_Sbf[i2]], [bp_o])
                    sx = stats[x]; bsx = b_stats[x]
                    k.op(k.dve, lambda e: e.bn_stats(out=sx[:L, 0:6], in_=p_o[:L, :512]), [bp_o], [bsx])
                    k.op(k.dve, lambda e: e.bn_aggr(out=sx[:L, 6:8], in_=sx[:L, 0:6]), [bsx], [bsx])
                    k.actf(sx[:L, 8:9], sx[:L, 7:8], AF.Sqrt, [bsx], [bsx], bias=eps_t[:L, 1:2], scale=1.0)
                    k.op(k.dve, lambda e: e.reciprocal(out=sx[:L, 9:10], in_=sx[:L, 8:9]), [bsx], [bsx])
                    k.ts(k.dve, onf[x][:L, :], p_o[:L, :512], sx[:L, 6:7], sx[:L, 9:10], ALU.subtract, ALU.mult, [bp_o, bsx], [b_onf[x]])
                    k.tt(k.pool, onf[x][:L, :], onf[x][:L, :], gnr[g_][:L, :], ALU.mult, [b_onf[x], b_gnr[g_]], [b_onf[x]])
                    k.tt(k.pool, onb[x][:L, :], onf[x][:L, :], gt[:L, ci, :], ALU.mult, [b_onf[x], b_gt[ci]], [b_onb[x]])
                    p_t, bp_t = nps(); ptb = psb(p_t)
                    for j in range(4):
                        k.tr(ptb[:, j * L:(j + 1) * L], onb[x][:L, j * 128:(j + 1) * 128], ident_b[:L, :L], [b_onb[x], b_const], [bp_t])
                    k.cp(k.act, onT[:, :, c0:c0 + L], ptb[:, :4 * L].rearrange("p (a b) -> p a b", a=4), [bp_t], b_onT)
                    for i2 in range(2):
                        p_d, bp_d = nps()
                        k.mm(p_d[:, :512], ktk[:L, ci, i2 * 128:(i2 + 1) * 128], vt[:L, ci, :], True, True, [b_ktk[ci], b_vt[ci]], [bp_d])
                        k.stt(k.dve, Sst[:, h, i2, :], Sst[:, h, i2, :], gLs[h], p_d[:, :512], ALU.mult, ALU.add, [bp_d, b_S[h]], [b_S[h]])
                        k.cp(k.act, Sbf[:, i2, :], Sst[:, h, i2, :], [b_S[h]], [b_Sbf[i2]])
                proj_fm(lw["out"][h], lambda kc: onT[:, kc, :Tt], lambda kc: [b_onT[kc]], Tt, add_to_x(Tt))
            k.barrier()

    def s5_setup(i, es0):
        with ExitStack() as es:
            k.barrier()
            def t32(n=32):
                return sbs(es, [128, n], F32)
            lr = t32(); li = t32(); dtt = t32(); mag = t32(); th = t32(); cc = t32(); sn = t32()
            c2 = t32(); s2 = t32(); cs_ = t32(); ar = t32(); ai = t32(); den = t32(); am1 = t32(); fre = t32(); fim = t32(); u1 = t32(); u2 = t32()
            B = Buf()
            k.dma(lr[:], lam_re[i].rearrange("(gp g2) n -> (g2 n) gp", g2=2), W=[B], allow_slow_non_contiguous=True)
            k.dma(li[:], lam_im[i].rearrange("(gp g2) n -> (g2 n) gp", g2=2), W=[B], allow_slow_non_contiguous=True)
            for g2 in range(2):
                src = bass.AP(tensor=log_dt.tensor, offset=i * 64 + g2, ap=[[0, 64], [2, 32]])
                k.dma(dtt[g2 * 64:(g2 + 1) * 64, :], src, W=[B], allow_slow_non_contiguous=True)
            R_, W_ = [B], [B]
            k.ts(k.dve, lr[:], lr[:], -1e-4, None, ALU.min, None, R_, W_)
            k.actf(dtt[:], dtt[:], AF.Exp, R_, W_)
            k.tt(k.dve, u1[:], lr[:], dtt[:], ALU.mult, R_, W_)
            k.actf(mag[:], u1[:], AF.Exp, R_, W_)
            k.tt(k.dve, th[:], li[:], dtt[:], ALU.mult, R_, W_)
            k.actf(cc[:], th[:], AF.Sin, R_, W_, bias=eps_t[:, 2:3], scale=1.0 / 16)
            k.actf(sn[:], th[:], AF.Sin, R_, W_, bias=eps_t[:, 3:4], scale=1.0 / 16)
            for _ in range(4):
                k.tt(k.dve, c2[:], cc[:], cc[:], ALU.mult, R_, W_)
                k.tt(k.dve, s2[:], sn[:], sn[:], ALU.mult, R_, W_)
                k.tt(k.dve, cs_[:], cc[:], sn[:], ALU.mult, R_, W_)
                k.tt(k.dve, cc[:], c2[:], s2[:], ALU.subtract, R_, W_)
                k.ts(k.dve, sn[:], cs_[:], 2.0, None, ALU.mult, None, R_, W_)
            k.tt(k.dve, ar[:], mag[:], cc[:], ALU.mult, R_, W_)
            k.tt(k.dve, ai[:], mag[:], sn[:], ALU.mult, R_, W_)
            k.tt(k.dve, c2[:], lr[:], lr[:], ALU.mult, R_, W_)
            k.tt(k.dve, s2[:], li[:], li[:], ALU.mult, R_, W_)
            k.tt(k.dve, den[:], c2[:], s2[:], ALU.add, R_, W_)
            k.op(k.dve, lambda e: e.reciprocal(out=den[:], in_=den[:]), R_, W_)
            k.ts(k.dve, am1[:], ar[:], -1.0, None, ALU.add, None, R_, W_)
            k.tt(k.dve, u1[:], am1[:], lr[:], ALU.mult, R_, W_)
            k.tt(k.dve, u2[:], ai[:], li[:], ALU.mult, R_, W_)
            k.tt(k.dve, u1[:], u1[:], u2[:], ALU.add, R_, W_)
            k.tt(k.dve, fre[:], u1[:], den[:], ALU.mult, R_, W_)
            k.tt(k.dve, u1[:], ai[:], lr[:], ALU.mult, R_, W_)
            k.tt(k.dve, u2[:], am1[:], li[:], ALU.mult, R_, W_)
            k.tt(k.dve, u1[:], u1[:], u2[:], ALU.subtract, R_, W_)
            k.tt(k.dve, fim[:], u1[:], den[:], ALU.mult, R_, W_)
            bre = sbs(es, [128, 32, 16], F32); bim = sbs(es, [128, 32, 16], F32)
            cre = sbs(es, [128, 32, 16], F32); cim = sbs(es, [128, 32, 16], F32)
            w1 = sbs(es, [128, 32, 16], F32); w2 = sbs(es, [128, 32, 16], F32)
            bbr = sbs(es, [128, 32, 16], F32); bbi = sbs(es, [128, 32, 16], F32)
            k.dma(bre[:], b_re[i].rearrange("(gp g2) n p -> (g2 n) gp p", g2=2), W=[B])
            k.dma(bim[:], b_im[i].rearrange("(gp g2) n p -> (g2 n) gp p", g2=2), W=[B])
            for g2 in range(2):
                for gp_ in range(32):
                    k.dma(cre[g2 * 64:(g2 + 1) * 64, gp_, :], c_re[i][2 * gp_ + g2].rearrange("q n -> n q"), W=[B], allow_slow_non_contiguous=True)
                    k.dma(cim[g2 * 64:(g2 + 1) * 64, gp_, :], c_im[i][2 * gp_ + g2].rearrange("q n -> n q"), W=[B], allow_slow_non_contiguous=True)
            freb = fre[:, :].rearrange("p (g o) -> p g o", o=1).broadcast_to([128, 32, 16])
            fimb = fim[:, :].rearrange("p (g o) -> p g o", o=1).broadcast_to([128, 32, 16])
            k.tt(k.dve, w1[:], bre[:], freb, ALU.mult, R_, W_)
            k.tt(k.dve, w2[:], bim[:], fimb, ALU.mult, R_, W_)
            k.tt(k.dve, bbr[:], w1[:], w2[:], ALU.subtract, R_, W_)
            k.tt(k.dve, w1[:], bim[:], freb, ALU.mult, R_, W_)
            k.tt(k.dve, w2[:], bre[:], fimb, ALU.mult, R_, W_)
            k.tt(k.dve, bbi[:], w1[:], w2[:], ALU.add, R_, W_)
            k.ts(k.dve, cim[:], cim[:], -1.0, None, ALU.mult, None, R_, W_)
            natR = [sbs(es, [128, 128], BF16) for _ in range(4)]; natI = [sbs(es, [128, 128], BF16) for _ in range(4)]
            blkq = [sbs(es, [128, 4, 128], BF16) for _ in range(4)]; b_q = bufs(4)
            for q in range(4):
                k.memset(k.pool, natR[q][:], 0.0, [], [b_q[q]]); k.memset(k.pool, natI[q][:], 0.0, [], [b_q[q]])
                k.memset(k.pool, blkq[q][:], 0.0, [], [b_q[q]])
            tab = [sbs(es, [128, 3, 512], F32) for _ in range(2)]; b_tab = bufs(2)
            tt_ = [sbs(es, [128, 256], F32) for _ in range(2)]
            for gp in range(32):
                q = gp % 4
                bq = b_q[q]
                for g2 in range(2):
                    rows = slice(g2 * 64, (g2 + 1) * 64)
                    cols = slice(q * 32 + g2 * 16, q * 32 + g2 * 16 + 16)
                    k.cp(k.dve, natR[q][rows, cols], bbr[rows, gp, :], [B, bq], [bq])
                    k.cp(k.dve, natI[q][rows, cols], bbi[rows, gp, :], [B, bq], [bq])
                    k.cp(k.pool, blkq[q][rows, 2, cols], cre[rows, gp, :], [B, bq], [bq])
                    k.cp(k.pool, blkq[q][rows, 3, cols], cim[rows, gp, :], [B, bq], [bq])
                p, bp = nps(); pb = psb(p)
                k.tr(pb[:, 0:128], natR[q][:, :], ident_b[:, :], [bq, b_const], [bp])
                k.tr(pb[:, 128:256], natI[q][:, :], ident_b[:, :], [bq, b_const], [bp])
                k.cp(k.act, blkq[q][:, 0:2, :], pb[:, 0:256].rearrange("p (a b) -> p a b", a=2), [bp, bq], [bq])
                k.dma(s5blk[i][gp], blkq[q][:], R=[bq], W=[s5c_b[i][gp]])
                x = gp % 2
                T_ = tab[x]; bt = b_tab[x]
                k.cp(k.dve, T_[:, 0, 0:1], cc[:, gp:gp + 1], [B, bt], [bt])
                k.cp(k.dve, T_[:, 1, 0:1], sn[:, gp:gp + 1], [B, bt], [bt])
                k.cp(k.pool, T_[:, 2, :], mag[:, gp:gp + 1].broadcast_to([128, 512]), [B, bt], [bt])
                m = 1
                while m < 512:
                    er = T_[:, 0, m - 1:m]; ei = T_[:, 1, m - 1:m]
                    k.ts(k.dve, tt_[0][:, :m], T_[:, 1, 0:m], ei, None, ALU.mult, None, [bt], [bt])
                    k.ts(k.dve, tt_[1][:, :m], T_[:, 0, 0:m], ei, None, ALU.mult, None, [bt], [bt])
                    k.stt(k.dve, T_[:, 0, m:2 * m], T_[:, 0, 0:m], er, tt_[0][:, :m], ALU.mult, ALU.subtract, [bt], [bt])
                    k.stt(k.dve, T_[:, 1, m:2 * m], T_[:, 1, 0:m], er, tt_[1][:, :m], ALU.mult, ALU.add, [bt], [bt])
                    m *= 2
                k.dma(s5tab[i][gp], T_[:], R=[bt], W=[s5c_b[i][gp]])
            k.barrier()

    def kv_cache_setup(i, s):
        with ExitStack() as es:
            k.barrier()
            cst = sbs(es, [128, 320], F32); cb = sbs(es, [128, 320], BF16); cT = sbs(es, [128, 384], BF16)
            B1, B2, B3 = Buf(), Buf(), Buf()
            for b in range(s.past // 128):
                kb = s.kv_b[i][(b * 128) // 512]
                k.dma(cst[:, 0:256], c_ckv[i, b * 128:(b + 1) * 128, :], W=[B1])
                k.dma(cst[:, 256:320], c_kr[i, b * 128:(b + 1) * 128, :], W=[B1])
                k.cp(k.act, cb[:, :], cst[:, :], [B1], [B2])
                k.dma(s.kvtok[i][b * 128:(b + 1) * 128, :], cb[:, 0:256], R=[B2], W=[kb])
                p, bp = nps(); pb = psb(p)
                k.tr(pb[:, 0:128], cb[:, 0:128], ident_b[:, :], [B2, b_const], [bp])
                k.tr(pb[:, 128:256], cb[:, 128:256], ident_b[:, :], [B2, b_const], [bp])
                k.tr(pb[:64, 256:384], cb[:, 256:320], ident_b[:, :], [B2, b_const], [bp])
                k.cp(k.dve, cT[:, 0:256], pb[:, 0:256], [bp], [B3])
                k.cp(k.dve, cT[:64, 256:384], pb[:64, 256:384], [bp], [B3])
                k.dma(s.kvT[i][:, :, b * 128:(b + 1) * 128], cT[:, 0:256].rearrange("p (a b) -> p a b", a=2), R=[B3], W=[kb])
                k.dma(s.krT[i][:, b * 128:(b + 1) * 128], cT[:64, 256:384], R=[B3], W=[kb])
            k.barrier()

    def layer_ab(l):
        lw = LW[l]; i = lw["i"]
        s5_setup(i, None)
        with ExitStack() as les:
            k.barrier()
            wkvn = sbs(les, [128, 2, 2048], BF16); WkT = sbs(les, [128, 8, 2, 128], BF16)
            qag = sbs(les, [128, 4], F32); kvg = sbs(les, [128, 256], F32); bglu = sbs(les, [128, 8], F32); dcol = sbs(les, [128, 8], F32)
            car = sbs(les, [128, 2, 32], F32); b_car = Buf()
            b_la = Buf()
            k.dma(wkvn[:], lw["kvb"].ap, R=[lw["kvb"].buf], W=[b_la])
            k.dma(qag[:], q_a_norm[i].rearrange("(c p) -> p c", p=128), W=[b_la], allow_slow_non_contiguous=True)
            k.dma(kvg[:], kv_a_norm[i].partition_broadcast(128), W=[b_la])
            k.dma(bglu[:], b_glu[i].rearrange("(c p) -> p c", p=128), W=[b_la], allow_slow_non_contiguous=True)
            k.dma(dcol[:], s5_d[i].rearrange("g p -> (g p)").rearrange("(c p) -> p c", p=128), W=[b_la], allow_slow_non_contiguous=True)
            for h in range(NH):
                p, bp = nps(); pb = psb(p)
                for half in range(2):
                    k.tr(pb[:, half * 128:(half + 1) * 128], wkvn[:, half, h * 256:h * 256 + 128], ident_b[:, :], [b_la, b_const], [bp])
                k.cp(k.act, WkT[:, h, :, :], pb[:, 0:256].rearrange("p (a b) -> p a b", a=2), [bp], [b_la])
            LS = dict(wkvn=wkvn, WkT=WkT, qag=qag, kvg=kvg, bglu=bglu, dcol=dcol, car=car, b_car=b_car, b_la=b_la)
            for s in seqs:
                if s.past == 0:
                    k.memset(k.pool, car[:], 0.0, [], [b_car])
                else:
                    k.dma(car[:, 0, :], s5re_in[i].rearrange("(gp g2) n -> (g2 n) gp", g2=2), W=[b_car], allow_slow_non_contiguous=True)
                    k.dma(car[:, 1, :], s5im_in[i].rearrange("(gp g2) n -> (g2 n) gp", g2=2), W=[b_car], allow_slow_non_contiguous=True)
                    kv_cache_setup(i, s)
                for ti in range(s.nt):
                    tile_begin(l, s, ti)
                    mixer_ab(l, s, ti, LS)
                    tile_end(l, s, ti)
                k.dma(s.s5re_o[i].rearrange("(gp g2) n -> (g2 n) gp", g2=2), car[:, 0, :], R=[b_car], W=[obuf()], allow_slow_non_contiguous=True)
                k.dma(s.s5im_o[i].rearrange("(gp g2) n -> (g2 n) gp", g2=2), car[:, 1, :], R=[b_car], W=[obuf()], allow_slow_non_contiguous=True)
            k.barrier()

    def mixer_ab(l, s, ti, LS):
        lw = LW[l]; i = lw["i"]; Tt = s.T; TS = min(128, Tt); nsub = Tt // TS
        pos_lo = s.pos0 + ti * Tt
        wkvn, WkT, qag, kvg, bglu, dcol, car, b_car, b_la = (LS[n] for n in ("wkvn", "WkT", "qag", "kvg", "bglu", "dcol", "car", "b_car", "b_la"))
        rmsnorm_fm(xt, b_xt, KC, lambda c: gmix[:, l, c:c + 1], Tt, xn, b_xn)
        with ExitStack() as es:
            k.barrier()
            mixT = sbs(es, [128, 16, TP], BF16); b_mix = bufs(16)
            if cfg.get("s5", True):
              with ExitStack() as e5:
                ubf = sbs(e5, [128, 8, TP], BF16); b_ubf = bufs(8)
                u = ubf; b_u = b_ubf

                def evac_u(mt, p, bp):
                    k.cp(k.act if mt % 2 == 0 else k.dve, ubf[:, mt, :Tt], p[:, :Tt], [bp], [b_ubf[mt]])
                proj_fm(lw["in_u"], lambda kc: xn[:, kc, :Tt], lambda kc: [b_xn[kc]], Tt, evac_u)
                tabs = [sbs(e5, [128, 3, TP], F32) for _ in range(2)]; blks = [sbs(e5, [128, 4, 128], BF16) for _ in range(2)]
                b_tb = bufs(2)
                tw = [sbs(e5, [128, TP], F32) for _ in range(4)]; b_tw = bufs(4)
                Z = [sbs(e5, [128, TP], F32) for _ in range(2)]; b_Z = bufs(2)
                sc = [sbs(e5, [128, TP], F32) for _ in range(2)]; b_sc = bufs(2)
                Sb = [sbs(e5, [128, 2, TP], BF16) for _ in range(2)]; b_Sb = bufs(2)
                yb = [sbs(e5, [128, TP], F32) for _ in range(2)]; b_yb = bufs(2)
                cw = sbs(e5, [128, 8], F32); b_cw = Buf()
                p_y = None
                for gp in range(32):
                    ct = gp // 4; x = gp % 2
                    T_ = tabs[x]; Bk = blks[x]; btb = b_tb[x]
                    if S5L < 0.4:
                        continue
                    k.dma(T_[:, :, :Tt], s5tab[i][gp][:, :, :Tt], R=[s5c_b[i][gp]], W=[btb])
                    k.dma(Bk[:], s5blk[i][gp], R=[s5c_b[i][gp]], W=[btb])
                    if S5L < 0.6:
                        continue
                    C_ = T_[:, 0, :Tt]; S_ = T_[:, 1, :Tt]; Rr = T_[:, 2, :Tt]
                    pr, bpr = nps(); pi_, bpi = nps()
                    k.mm(pr[:, :Tt], Bk[:, 0, :], ubf[:, ct, :Tt], True, True, [btb, b_ubf[ct]], [bpr])
                    k.mm(pi_[:, :Tt], Bk[:, 1, :], ubf[:, ct, :Tt], True, True, [btb, b_ubf[ct]], [bpi])
                    if S5L < 0.8:
                        continue
                    k.tt(k.dve, tw[0][:, :Tt], pr[:, :Tt], C_, ALU.mult, [bpr, btb], [b_tw[0]])
                    k.tt(k.dve, tw[1][:, :Tt], pi_[:, :Tt], S_, ALU.mult, [bpi, btb], [b_tw[1]])
                    k.tt(k.pool, Z[0][:, :Tt], tw[0][:, :Tt], tw[1][:, :Tt], ALU.add, [b_tw[0], b_tw[1]], [b_Z[0]])
                    k.tt(k.dve, tw[2][:, :Tt], pi_[:, :Tt], C_, ALU.mult, [bpi, btb], [b_tw[2]])
                    k.tt(k.dve, tw[3][:, :Tt], pr[:, :Tt], S_, ALU.mult, [bpr, btb], [b_tw[3]])
                    k.tt(k.pool, Z[1][:, :Tt], tw[2][:, :Tt], tw[3][:, :Tt], ALU.subtract, [b_tw[2], b_tw[3]], [b_Z[1]])
                    if S5L < 2:
                        continue
                    for ri in range(2):
                        if USE_HW_SCAN:
                            k.op(k.dve, lambda e: e.tensor_tensor_scan(out=sc[ri][:, :Tt], data0=Rr, data1=Z[ri][:, :Tt],
                                                                       initial=car[:, ri, gp:gp + 1], op0=ALU.mult, op1=ALU.add),
                                 [btb, b_Z[ri], b_car], [b_sc[ri]])
                            continue
                        A_, bA_ = Z[ri], b_Z[ri]
                        B_, bB_ = sc[ri], b_sc[ri]
                        k.cp(k.dve, cw[:, 4:5], T_[:, 2, 0:1], [btb], [b_cw])
                        sh = 1
                        while sh < Tt:
                            k.cp(k.act, B_[:, 0:sh], A_[:, 0:sh], [bA_], [bB_])
                            k.stt(k.dve, B_[:, sh:Tt], A_[:, 0:Tt - sh], cw[:, 4:5], A_[:, sh:Tt], ALU.mult, ALU.add, [bA_, b_cw], [bB_])
                            k.tt(k.dve, cw[:, 4:5], cw[:, 4:5], cw[:, 4:5], ALU.mult, [b_cw], [b_cw])
                            A_, bA_, B_, bB_ = B_, bB_, A_, bA_
                            sh *= 2
                        raise NotImplementedError("carry term needs r^(t+1) table")
                    k.tt(k.pool, tw[0][:, :Tt], sc[0][:, :Tt], C_, ALU.mult, [b_sc[0], btb], [b_tw[0]])
                    k.tt(k.pool, tw[1][:, :Tt], sc[1][:, :Tt], S_, ALU.mult, [b_sc[1], btb], [b_tw[1]])
                    k.tt(k.dve, Sb[x][:, 0, :Tt], tw[0][:, :Tt], tw[1][:, :Tt], ALU.subtract, [b_tw[0], b_tw[1]], [b_Sb[x]])
                    k.tt(k.pool, tw[2][:, :Tt], sc[1][:, :Tt], C_, ALU.mult, [b_sc[1], btb], [b_tw[2]])
                    k.tt(k.pool, tw[3][:, :Tt], sc[0][:, :Tt], S_, ALU.mult, [b_sc[0], btb], [b_tw[3]])
                    k.tt(k.dve, Sb[x][:, 1, :Tt], tw[2][:, :Tt], tw[3][:, :Tt], ALU.add, [b_tw[2], b_tw[3]], [b_Sb[x]])
                    e_ = Tt - 1
                    lastC = T_[:, 0, e_:e_ + 1]; lastS = T_[:, 1, e_:e_ + 1]
                    k.tt(k.dve, cw[:, 0:1], sc[0][:, e_:e_ + 1], lastC, ALU.mult, [b_sc[0], btb], [b_cw])
                    k.tt(k.dve, cw[:, 1:2], sc[1][:, e_:e_ + 1], lastS, ALU.mult, [b_sc[1], btb], [b_cw])
                    k.tt(k.dve, cw[:, 2:3], sc[1][:, e_:e_ + 1], lastC, ALU.mult, [b_sc[1], btb], [b_cw])
                    k.tt(k.dve, cw[:, 3:4], sc[0][:, e_:e_ + 1], lastS, ALU.mult, [b_sc[0], btb], [b_cw])
                    k.tt(k.dve, car[:, 0, gp:gp + 1], cw[:, 0:1], cw[:, 1:2], ALU.subtract, [b_cw], [b_car])
                    k.tt(k.dve, car[:, 1, gp:gp + 1], cw[:, 2:3], cw[:, 3:4], ALU.add, [b_cw], [b_car])
                    if S5L < 3:
                        continue
                    if gp % 4 == 0:
                        (p_y, bp_y), = npa(1)
                    k.mm(p_y[:, :Tt], Bk[:, 2, :], Sb[x][:, 0, :Tt], gp % 4 == 0, False, [btb, b_Sb[x]], [bp_y])
                    k.mm(p_y[:, :Tt], Bk[:, 3, :], Sb[x][:, 1, :Tt], False, gp % 4 == 3, [btb, b_Sb[x]], [bp_y])
                    if gp % 4 == 3:
                        yy = yb[ct % 2]; byy = b_yb[ct % 2]
                        k.stt(k.dve, yy[:, :Tt], u[:, ct, :Tt], dcol[:, ct:ct + 1], p_y[:, :Tt], ALU.mult, ALU.add, [bp_y, b_ubf[ct], b_la], [byy])
                        g1 = tw[0]; g2_ = tw[1]
                        k.tt(k.pool, g1[:, :Tt], yy[:, :Tt], yy[:, :Tt], ALU.mult, [byy], [b_tw[0]])
                        k.ts(k.dve, g1[:, :Tt], g1[:, :Tt], 0.044715, 1.0, ALU.mult, ALU.add, [b_tw[0]], [b_tw[0]])
                        k.tt(k.pool, g1[:, :Tt], g1[:, :Tt], yy[:, :Tt], ALU.mult, [b_tw[0], byy], [b_tw[0]])
                        k.actf(g2_[:, :Tt], g1[:, :Tt], AF.Sigmoid, [b_tw[0]], [b_tw[1]], scale=2.0 * math.sqrt(2.0 / math.pi))
                        k.tt(k.pool, ubf[:, ct, :Tt], yy[:, :Tt], g2_[:, :Tt], ALU.mult, [byy, b_tw[1]], [b_ubf[ct]])
                def evac_glu(mt, p, bp):
                    yy = yb[mt % 2]; byy = b_yb[mt % 2]
                    k.actf(yy[:, :Tt], p[:, :Tt], AF.Sigmoid, [bp, b_la], [byy], bias=bglu[:, mt:mt + 1], scale=1.0)
                    k.tt(k.pool, mixT[:, 8 + mt, :Tt], ubf[:, mt, :Tt], yy[:, :Tt], ALU.mult, [byy, b_ubf[mt]], [b_mix[8 + mt]])
                if S5L >= 4:
                    proj_fm(lw["glu"], lambda kc: ubf[:, kc, :Tt], lambda kc: [b_ubf[kc]], Tt, evac_glu)
                else:
                    for c in range(8, 16):
                        k.memset(k.pool, mixT[:, c, :Tt], 0.0, [], [b_mix[c]])
                k.barrier()
            else:
                for c in range(8, 16):
                    k.memset(k.pool, mixT[:, c, :Tt], 0.0, [], [b_mix[c]])
            with ExitStack() as ea:
                k.barrier()
                cq = sbs(ea, [128, 4, TP], F32); b_cq = bufs(4); cqn = sbs(ea, [128, 4, TP], BF16); b_cqn = bufs(4)
                qn = [sbs(ea, [128, TP], BF16) for _ in range(2)]; b_qn = bufs(2)
                qp = sbs(ea, [128, 8, 2, TP], BF16); b_qp = bufs(8)
                qrT = sbs(ea, [64, 8, TP], BF16); b_qrT = Buf()
                cstab = sbs(ea, [128, 4, 64], F32); b_cs = Buf()
                tk = [sbs(ea, [128, 8, 32], F32) for _ in range(4)]; b_tk = bufs(4)
                qrt = sbs(ea, [128, 8, 64], BF16); b_qrt = Buf()
                ckvn = sbs(ea, [128, 256], F32); ckvb = sbs(ea, [128, 256], BF16); ckvTs = sbs(ea, [128, 2, 128], BF16)
                krn = sbs(ea, [128, 64], F32); krb = sbs(ea, [128, 64], BF16); krTs = sbs(ea, [64, 128], BF16)
                jk = sbs(ea, [128, 256], BF16); stq = sbs(ea, [128, 4], F32)
                b_kw = Buf()
                for sub in range(nsub):
                    k.dma(cstab[:TS, sub, :], h_mla_cs[pos_lo + sub * TS:pos_lo + (sub + 1) * TS, :], W=[b_cs])

                def evac_cq(mt, p, bp):
                    k.cp(k.act, cq[:, mt, :Tt], p[:, :Tt], [bp], [b_cq[mt]])
                proj_fm(lw["in_cq"], lambda kc: xn[:, kc, :Tt], lambda kc: [b_xn[kc]], Tt, evac_cq)

                def evac_kv(sub, p, bp):
                    tok0 = ti * Tt + sub * TS
                    key0 = s.past + tok0
                    kb = s.kv_b[i][key0 // 512]
                    k.actf(jk[:TS, :], p[:TS, 0:256], AF.Square, [bp], [b_kw], accum_out=stq[:TS, 0:1])
                    k.actf(stq[:TS, 1:2], stq[:TS, 0:1], AF.Sqrt, [b_kw], [b_kw], bias=eps_t[:TS, 0:1], scale=1.0 / 256)
                    k.op(k.dve, lambda e: e.reciprocal(out=stq[:TS, 2:3], in_=stq[:TS, 1:2]), [b_kw], [b_kw])
                    k.stt(k.dve, ckvn[:TS, :], p[:TS, 0:256], stq[:TS, 2:3], kvg[:TS, :], ALU.mult, ALU.mult, [bp, b_kw, b_la], [b_kw])
                    k.dma(s.ckv_o[i, tok0:tok0 + TS, :], ckvn[:TS, :], R=[b_kw], W=[obuf()])
                    k.cp(k.act, ckvb[:TS, :], ckvn[:TS, :], [b_kw], [b_kw])
                    k.dma(s.kvtok[i][key0:key0 + TS, :], ckvb[:TS, :], R=[b_kw], W=[kb])
                    p2, bp2 = nps(); pb = psb(p2)
                    for half in range(2):
                        k.tr(pb[:, half * TS:(half + 1) * TS], ckvb[:TS, half * 128:(half + 1) * 128], ident_b[:TS, :TS], [b_kw, b_const], [bp2])
                    k.cp(k.dve, ckvTs[:, :, :TS], pb[:, :2 * TS].rearrange("p (a b) -> p a b", a=2), [bp2], [b_kw])
                    k.dma(s.kvT[i][:, :, key0:key0 + TS], ckvTs[:, :, :TS], R=[b_kw], W=[kb])
                    cosv = cstab[:TS, sub, 0:32]; sinv = cstab[:TS, sub, 32:64]
                    x1 = p[:TS, 256:288]; x2 = p[:TS, 288:320]
                    k.tt(k.dve, tk[0][:TS, 0, :], x1, cosv, ALU.mult, [bp, b_cs], [b_tk[0]])
                    k.tt(k.dve, tk[1][:TS, 0, :], x2, sinv, ALU.mult, [bp, b_cs], [b_tk[1]])
                    k.tt(k.pool, krn[:TS, 0:32], tk[0][:TS, 0, :], tk[1][:TS, 0, :], ALU.subtract, [b_tk[0], b_tk[1]], [b_kw])
                    k.tt(k.dve, tk[2][:TS, 0, :], x1, sinv, ALU.mult, [bp, b_cs], [b_tk[2]])
                    k.tt(k.dve, tk[3][:TS, 0, :], x2, cosv, ALU.mult, [bp, b_cs], [b_tk[3]])
                    k.tt(k.pool, krn[:TS, 32:64], tk[2][:TS, 0, :], tk[3][:TS, 0, :], ALU.add, [b_tk[2], b_tk[3]], [b_kw])
                    k.dma(s.kr_o[i, tok0:tok0 + TS, :], krn[:TS, :], R=[b_kw], W=[obuf()])
                    k.cp(k.act, krb[:TS, :], krn[:TS, :], [b_kw], [b_kw])
                    p3, bp3 = nps(); pb3 = psb(p3)
                    k.tr(pb3[:64, :TS], krb[:TS, :], ident_b[:TS, :TS], [b_kw, b_const], [bp3])
                    k.cp(k.dve, krTs[:, :TS], pb3[:64, :TS], [bp3], [b_kw])
                    k.dma(s.krT[i][:, key0:key0 + TS], krTs[:, :TS], R=[b_kw], W=[kb])
                proj_tm(lw["in_kv"][0], lambda kc, sub: xn[:, kc, sub * TS:(sub + 1) * TS], lambda kc: [b_xn[kc]], TS, nsub, 320, evac_kv)
                rmsnorm_fm(cq, b_cq, 4, lambda c: qag[:, c:c + 1], Tt, cqn, b_cqn)
                v, bw = load_slab(lw["qb_nope"])
                for h in range(NH):
                    p, bp = nps()
                    for c in range(4):
                        k.mm(p[:, :Tt], v[:, c, h * 128:(h + 1) * 128], cqn[:, c, :Tt], c == 0, c == 3, [bw, b_cqn[c]], [bp])
                    x = h % 2
                    k.cp(k.act, qn[x][:, :Tt], p[:, :Tt], [bp], [b_qn[x]])
                    for half in range(2):
                        p2, bp2 = nps()
                        k.mm(p2[:, :Tt], WkT[:, h, half, :], qn[x][:, :Tt], True, True, [b_la, b_qn[x]], [bp2])
                        k.cp(k.dve if half == 0 else k.act, qp[:, h, half, :Tt], p2[:, :Tt], [bp2], [b_qp[h]])

                def evac_qr(sub, p, bp):
                    pv = p[:TS, :512].rearrange("p (h e) -> p h e", h=8)
                    x1 = pv[:, :, 0:32]; x2 = pv[:, :, 32:64]
                    cosb = cstab[:TS, sub:sub + 1, 0:32].broadcast_to([TS, 8, 32]); sinb = cstab[:TS, sub:sub + 1, 32:64].broadcast_to([TS, 8, 32])
                    k.tt(k.dve, tk[0][:TS], x1, cosb, ALU.mult, [bp, b_cs], [b_tk[0]])
                    k.tt(k.dve, tk[1][:TS], x2, sinb, ALU.mult, [bp, b_cs], [b_tk[1]])
                    k.tt(k.pool, qrt[:TS, :, 0:32], tk[0][:TS], tk[1][:TS], ALU.subtract, [b_tk[0], b_tk[1]], [b_qrt])
                    k.tt(k.dve, tk[2][:TS], x1, sinb, ALU.mult, [bp, b_cs], [b_tk[2]])
                    k.tt(k.dve, tk[3][:TS], x2, cosb, ALU.mult, [bp, b_cs], [b_tk[3]])
                    k.tt(k.pool, qrt[:TS, :, 32:64], tk[2][:TS], tk[3][:TS], ALU.add, [b_tk[2], b_tk[3]], [b_qrt])
                    p3, bp3 = nps(); pb3 = psb(p3)
                    for h in range(NH):
                        k.tr(pb3[:64, h * TS:(h + 1) * TS], qrt[:TS, h, :], ident_b[:TS, :TS], [b_qrt, b_const], [bp3])
                    k.cp(k.act, qrT[:, :, sub * TS:(sub + 1) * TS], pb3[:64, :8 * TS].rearrange("p (a b) -> p a b", a=8), [bp3], [b_qrT])
                proj_tm([lw["qb_rope"]], lambda kc, sub: cqn[:, kc, sub * TS:(sub + 1) * TS], lambda kc: [b_cqn[kc]], TS, nsub, 512, evac_qr)
                kbase = s.past + ti * Tt
                blocks = [(b * 128, 128, 0, False) for b in range(kbase // 128)]
                if Tt >= 128:
                    blocks += [(kbase + j * 128, 128, j * 128, True) for j in range(Tt // 128)]
                else:
                    blocks += [(kbase, Tt, 0, False)]
                sbl = {}
                for bi, bl in enumerate(blocks):
                    sbl.setdefault(bl[0] // 512, []).append((bi, bl))
                NKB = 3
                kvTb = [sbs(ea, [128, 2, 512], BF16) for _ in range(NKB)]; krTb = [sbs(ea, [64, 512], BF16) for _ in range(NKB)]
                kvkb = [sbs(ea, [128, 4, 256], BF16) for _ in range(NKB)]; b_kvs = bufs(NKB)
                PTb = [sbs(ea, [128, TP], BF16) for _ in range(3)]; b_PTb = bufs(3)
                recip = sbs(ea, [128, TP], F32); b_rc = Buf(); OLn = sbs(ea, [128, 2, TP], BF16); b_OLn = Buf()
                steps = []
                for h in range(NH):
                    for sbi in sorted(sbl):
                        for n_, (bi, bl) in enumerate(sbl[sbi]):
                            steps.append((h, sbi, n_ == 0, bi, bl))
                kvc = [0]; ptc = [0]
                cur = {}

                def emit_S(stp):
                    h, sbi, first_sb, bi, (ks, kn, qlo, diag) = stp
                    if first_sb:
                        w = kvc[0] % NKB; kvc[0] += 1
                        lst = sbl[sbi]
                        k0 = sbi * 512
                        nk = sum(bl[1] for _, bl in lst)
                        kbuf = s.kv_b[i][sbi]
                        k.dma(kvTb[w][:, :, :nk], s.kvT[i][:, :, k0:k0 + nk], R=[kbuf], W=[b_kvs[w]])
                        k.dma(krTb[w][:, :nk], s.krT[i][:, k0:k0 + nk], R=[kbuf], W=[b_kvs[w]])
                        if nk == 512:
                            k.dma(kvkb[w][:, :, :], s.kvtok[i][k0:k0 + 512, :].rearrange("(b p) c -> p b c", p=128), R=[kbuf], W=[b_kvs[w]])
                        else:
                            for (_, bl) in lst:
                                o_ = bl[0] - k0
                                k.dma(kvkb[w][:bl[1], o_ // 128, :], s.kvtok[i][bl[0]:bl[0] + bl[1], :], R=[kbuf], W=[b_kvs[w]])
                        cur["w"] = w
                    w = cur["w"]
                    k0 = sbi * 512
                    o = ks - k0
                    qn_ = Tt - qlo
                    pS, bS = nps()
                    k.mm(pS[:kn, :qn_], kvTb[w][:, 0, o:o + kn], qp[:, h, 0, qlo:Tt], True, False, [b_kvs[w], b_qp[h]], [bS])
                    k.mm(pS[:kn, :qn_], kvTb[w][:, 1, o:o + kn], qp[:, h, 1, qlo:Tt], False, False, [b_kvs[w], b_qp[h]], [bS])
                    k.mm(pS[:kn, :qn_], krTb[w][:, o:o + kn], qrT[:, h, qlo:Tt], False, True, [b_kvs[w], b_qrT], [bS])
                    x = ptc[0] % 3; ptc[0] += 1
                    k.actf(PTb[x][:kn, :qn_], pS[:kn, :qn_], AF.Exp, [bS], [b_PTb[x]], scale=MLA_SCALE)
                    if diag:
                        k.memset(k.pool, PTb[x][64:128, 0:64], 0.0, [], [b_PTb[x]])
                    return (w, o, x)

                acc = {}

                def emit_PV(stp, info):
                    h, sbi, first_sb, bi, (ks, kn, qlo, diag) = stp
                    w, o, x = info
                    qn_ = Tt - qlo
                    first = bi == 0; last = bi == len(blocks) - 1
                    if first:
                        acc["a"] = npa(3)
                    (pO0, bO0), (pO1, bO1), (pSm, bSm) = acc["a"]
                    k.mm(pO0[:, qlo:Tt], kvkb[w][:kn, o // 128, 0:128], PTb[x][:kn, :qn_], first, last, [b_kvs[w], b_PTb[x]], [bO0])
                    k.mm(pO1[:, qlo:Tt], kvkb[w][:kn, o // 128, 128:256], PTb[x][:kn, :qn_], first, last, [b_kvs[w], b_PTb[x]], [bO1])
                    k.mm(pSm[:, qlo:Tt], ones_b[:kn, :], PTb[x][:kn, :qn_], first, last, [b_const, b_PTb[x]], [bSm])
                    if last:
                        k.op(k.dve, lambda e: e.reciprocal(out=recip[:, :Tt], in_=pSm[:, :Tt]), [bSm], [b_rc])
                        k.tt(k.dve, OLn[:, 0, :Tt], pO0[:, :Tt], recip[:, :Tt], ALU.mult, [bO0, b_rc], [b_OLn])
                        k.tt(k.dve, OLn[:, 1, :Tt], pO1[:, :Tt], recip[:, :Tt], ALU.mult, [bO1, b_rc], [b_OLn])
                        pA, bA = nps()
                        for half in range(2):
                            k.mm(pA[:, :Tt], wkvn[:, half, h * 256 + 128:h * 256 + 256], OLn[:, half, :Tt], half == 0, half == 1, [b_la, b_OLn], [bA])
                        k.cp(k.act, mixT[:, h, :Tt], pA[:, :Tt], [bA], [b_mix[h]])

                pend = None
                for stp in steps + [None]:
                    info = emit_S(stp) if stp is not None else None
                    if pend is not None:
                        emit_PV(*pend)
                    pend = (stp, info) if stp is not None else None
                k.barrier()
            proj_fm(lw["out"], lambda kc: mixT[:, kc, :Tt], lambda kc: [b_mix[kc]], Tt, add_to_x(Tt))
            k.barrier()

    def tile_begin(l, s, ti):
        Tt = s.T
        if l == 0:
            load_x0(s, ti)
        else:
            k.dma(xt[:, :, :Tt], s.xscr[ti], R=[s.xscr_b[ti]], W=b_xt)

    def tile_end(l, s, ti):
        Tt = s.T
        if DO_MLP:
            mlp(l, Tt)
        if l == DEPTH - 1:
            final_out(s, ti)
        else:
            k.dma(s.xscr[ti], xt[:, :, :Tt], R=b_xt, W=[s.xscr_b[ti]])

    for l in range(DEPTH):
        if types[l] == "ab":
            layer_ab(l)
        elif types[l] == "c":
            layer_c(l)
        else:
            for s in seqs:
                for ti in range(s.nt):
                    tile_begin(l, s, ti)
                    tile_end(l, s, ti)
    k.finish(out_bufs)
    k.barrier()
    return nc, k


_WNAMES = ["norm_mix", "norm_mlp", "norm_final", "w_in_ab", "q_a_norm", "kv_a_norm", "w_q_b", "w_kv_b",
           "s5_lam_re", "s5_lam_im", "s5_log_dt", "s5_b_re", "s5_b_im", "s5_c_re", "s5_c_im", "s5_d", "w_glu", "b_glu",
           "w_out_ab", "w_in_c", "ret_gn", "w_out_c", "w_up", "w_down"]


def run_cfg(cfg, inputs, trace=False):
    f = lambda a: np.ascontiguousarray(np.asarray(a), dtype=np.float32)
    xp_all = f(inputs["x_prompt"]); xs_all = f(inputs["x_sample"])
    B, SEQ, _ = xp_all.shape
    DB, DS, _ = xs_all.shape
    PAST = inputs["cache_mla_ckv"].shape[2]
    cfg = dict(cfg); cfg.update(SEQ=SEQ, DS=DS, PAST=PAST)
    if not cfg.get("mlp", True):
        inputs = dict(inputs); inputs["w_up"] = np.zeros((1, 1, 1), np.float32); inputs["w_down"] = np.zeros((1, 1, 1), np.float32)
    hc = host_consts(max(SEQ, PAST + DS))
    cfg["gL"] = hc["gL"]
    nc, k = build(cfg)
    k.nc = nc
    NABd = max(1, sum(1 for t in cfg["types"] if t == "ab")); NCd = max(1, sum(1 for t in cfg["types"] if t == "c"))
    w = {n: f(inputs[n]) for n in _WNAMES}
    def pad0(a, n):
        a = f(a)
        if a.shape[0] == 0:
            return np.zeros((n,) + a.shape[1:], np.float32)
        return a
    for n in list(w):
        if w[n].shape[0] == 0:
            w[n] = np.zeros((1,) + w[n].shape[1:], np.float32)
    consts = {"h_mla_cs": hc["mla_cs"], "h_ret_cos": hc["ret_cos"], "h_ret_sin": hc["ret_sin"], "h_DTt": hc["DTt"],
              "h_dq": hc["dq_rep"], "h_kdec128": hc["kdec128"], "h_kdec64": hc["kdec64"],
              "h_ident_f": hc["ident_f"], "h_ident_b": hc["ident_b"]}
    ckv = pad0(inputs["cache_mla_ckv"], 1); ckr = pad0(inputs["cache_mla_krope"], 1)
    s5r = pad0(inputs["state_s5_re"], 1); s5i = pad0(inputs["state_s5_im"], 1); rst = pad0(inputs["state_ret"], 1)
    in_maps = []
    for c in range(8):
        m = {"xp": xp_all[c % B], "xs": xs_all[c % DB], "c_ckv": np.ascontiguousarray(ckv[:, c % DB]),
             "c_kr": np.ascontiguousarray(ckr[:, c % DB]), "s5re_in": np.ascontiguousarray(s5r[:, c % DB]),
             "s5im_in": np.ascontiguousarray(s5i[:, c % DB]), "ret_in": np.ascontiguousarray(rst[:, c % DB])}
        m.update(w); m.update(consts)
        in_maps.append(m)
    for m in in_maps:
        for n in list(m):
            shp = k.in_shapes.get(n)
            if shp is not None and tuple(m[n].shape) != tuple(shp):
                m[n] = np.zeros(shp, m[n].dtype)
    global _LAST_IN_MAPS
    _LAST_IN_MAPS = in_maps
    if cfg.get("build_only"):
        return None, None, k
    res = run_bass_kernel_spmd(nc, in_maps, core_ids=list(range(8)), trace=trace)
    R = res.results
    NAB = sum(1 for t in cfg["types"] if t == "ab"); NC_ = sum(1 for t in cfg["types"] if t == "c")
    def gat(name, n, cores, lay):
        a = np.stack([np.asarray(R[c][name], dtype=np.float32) for c in cores], axis=0)
        if lay:
            a = np.swapaxes(a, 0, 1)[:n]
        return np.ascontiguousarray(a)
    pc = list(range(B)); sc = list(range(DB))
    outs = (gat("y_p", 0, pc, False), gat("y_s", 0, sc, False),
            gat("ckv_p", NAB, pc, True), gat("kr_p", NAB, pc, True), gat("s5re_p", NAB, pc, True), gat("s5im_p", NAB, pc, True),
            gat("ret_p", NC_, pc, True),
            gat("ckv_s", NAB, sc, True), gat("kr_s", NAB, sc, True), gat("s5re_s", NAB, sc, True), gat("s5im_s", NAB, sc, True),
            gat("ret_s", NC_, sc, True))
    return outs, res, k


def kernel(**inputs):
    depth = np.asarray(inputs["norm_mix"]).shape[0]
    cfg = {"DEPTH": depth, "types": ["ab" if l % 2 == 0 else "c" for l in range(depth)]}
    outs, _, _ = run_cfg(cfg, inputs)
    return outs
```

```python
import math
from contextlib import ExitStack
import numpy as np
import ml_dtypes
import concourse.bass as bass
import concourse.mybir as mybir
from concourse.bass_utils import run_bass_kernel_spmd

F32 = mybir.dt.float32
BF16 = mybir.dt.bfloat16
AF = mybir.ActivationFunctionType
ALU = mybir.AluOpType

D = 2048
KC = 16
EPS = 1e-6
GN_EPS = 1e-5
Q_LORA, KV_LORA, ROPE = 512, 256, 64
NH = 8
S5W = 1024
IN_AB = 1856
DFF = 8192
RH, RDK, RDV = 8, 256, 512
MLA_SCALE = (128 + 64) ** -0.5
SEM_LIMIT = 30000


class Buf:
    __slots__ = ("w", "r")

    def __init__(self):
        self.w = {}
        self.r = {}


def bufs(n):
    return [Buf() for _ in range(n)]


class Eng:
    def __init__(self, nc, e, name, same_sync):
        self.e = e
        self.name = name
        self.sem = nc.alloc_semaphore("s_" + name)
        self.cnt = 0
        self.seen = {}
        self.same_sync = same_sync


class K:
    def __init__(self, nc):
        self.nc = nc
        self.pe = Eng(nc, nc.tensor, "pe", False)
        self.act = Eng(nc, nc.scalar, "act", True)
        self.dve = Eng(nc, nc.vector, "dve", True)
        self.pool = Eng(nc, nc.gpsimd, "pool", True)
        self.sp = Eng(nc, nc.sync, "sp", False)
        self.dma_sems = {}
        self.ninstr = 0

    def _wait(self, E, ev):
        sem, val = ev
        if sem is E.sem and not E.same_sync:
            return
        key = id(sem)
        if E.seen.get(key, 0) >= val:
            return
        E.e.wait_ge(sem, val)
        E.seen[key] = val

    def _deps(self, E, reads, writes):
        for b in reads:
            for ev in b.w.values():
                self._wait(E, ev)
        for b in writes:
            for ev in b.r.values():
                self._wait(E, ev)
            for ev in b.w.values():
                self._wait(E, ev)

    def _mark(self, ev, reads, writes):
        key = id(ev[0])
        for b in reads:
            b.r[key] = ev
        for b in writes:
            b.w[key] = ev
            b.r = {}

    def op(self, E, fn, R=(), W=()):
        if E.cnt >= SEM_LIMIT:
            E.sem = self.nc.alloc_semaphore(f"s_{E.name}_{self.ninstr}")
            E.cnt = 0
        self._deps(E, R, W)
        ins = fn(E.e)
        E.cnt += 1
        ins.then_inc(E.sem, 1)
        self._mark((E.sem, E.cnt), R, W)
        self.ninstr += 1
        return ins

    def dma(self, out, in_, R=(), W=(), Q=None, nsem=12, **kw):
        Q = Q or self.sp
        pool = self.dma_sems.setdefault(Q.name, {"sems": [], "vals": [], "i": 0})
        if len(pool["sems"]) < nsem:
            pool["sems"].append(self.nc.alloc_semaphore(f"d_{Q.name}{len(pool['sems'])}"))
            pool["vals"].append(0)
        i = pool["i"] % len(pool["sems"])
        pool["i"] += 1
        sem = pool["sems"][i]
        if pool["vals"][i] > 0:
            self._wait(Q, (sem, pool["vals"][i]))
        if pool["vals"][i] >= SEM_LIMIT:
            sem = pool["sems"][i] = self.nc.alloc_semaphore(f"d_{Q.name}{i}_{self.ninstr}")
            pool["vals"][i] = 0
        self._deps(Q, R, W)
        ins = Q.e.dma_start(out=out, in_=in_, **kw)
        pool["vals"][i] += 16
        ins.then_inc(sem, 16)
        self._mark((sem, pool["vals"][i]), R, W)
        self.ninstr += 1
        return ins

    def barrier(self):
        engs = [self.pe, self.act, self.dve, self.pool, self.sp]
        evs = [(E.sem, E.cnt) for E in engs if E.cnt > 0]
        for pl in self.dma_sems.values():
            for sem, val in zip(pl["sems"], pl["vals"]):
                if val > 0:
                    evs.append((sem, val))
        for E in engs:
            for ev in evs:
                if ev[0] is not E.sem:
                    self._wait(E, ev)

    def finish(self, bl):
        for b in bl:
            for ev in b.w.values():
                self._wait(self.sp, ev)

    def mm(self, out, lhsT, rhs, start, stop, R, W):
        return self.op(self.pe, lambda e: e.matmul(out, lhsT=lhsT, rhs=rhs, start=start, stop=stop), R, W)

    def tr(self, out, in_, ident, R, W):
        return self.op(self.pe, lambda e: e.transpose(out, in_, ident), R, W)

    def actf(self, out, in_, func, R, W, **kw):
        return self.op(self.act, lambda e: e.activation(out=out, in_=in_, func=func, **kw), R, W)

    def cp(self, E, out, in_, R, W):
        if E is self.act:
            return self.op(E, lambda e: e.copy(out=out, in_=in_), R, W)
        return self.op(E, lambda e: e.tensor_copy(out=out, in_=in_), R, W)

    def tt(self, E, out, in0, in1, op, R, W):
        return self.op(E, lambda e: e.tensor_tensor(out=out, in0=in0, in1=in1, op=op), R, W)

    def ts(self, E, out, in0, s1, s2, op0, op1, R, W):
        if op1 is None:
            return self.op(E, lambda e: e.tensor_scalar(out=out, in0=in0, scalar1=s1, scalar2=None, op0=op0), R, W)
        return self.op(E, lambda e: e.tensor_scalar(out=out, in0=in0, scalar1=s1, scalar2=s2, op0=op0, op1=op1), R, W)

    def stt(self, E, out, in0, scalar, in1, op0, op1, R, W):
        return self.op(E, lambda e: e.scalar_tensor_tensor(out=out, in0=in0, scalar=scalar, in1=in1, op0=op0, op1=op1), R, W)

    def memset(self, E, ap, val, R, W):
        return self.op(E, lambda e: e.memset(ap, val), R, W)


class Slab:
    __slots__ = ("ap", "buf", "nk", "mw")

    def __init__(self, ap, nk, mw):
        self.ap = ap
        self.buf = Buf()
        self.nk = nk
        self.mw = mw


def host_consts(maxpos):
    half = 32
    inv = 10000.0 ** (-np.arange(half, dtype=np.float64) / half)
    pos = np.arange(maxpos, dtype=np.float64)
    ang = pos[:, None] * inv[None, :]
    mla_cs = np.concatenate([np.cos(ang), np.sin(ang)], axis=1).astype(np.float32)
    half = 128
    inv = 10000.0 ** (-np.arange(half, dtype=np.float64) / half)
    ang = inv[:, None] * pos[None, :]
    ret_cos = np.cos(ang).astype(np.float32)
    ret_sin = np.sin(ang).astype(np.float32)
    logg = np.log(1.0 - 2.0 ** (-5.0 - np.arange(RH, dtype=np.float64)))
    L = 128
    idx = np.arange(L, dtype=np.float64)
    diff = idx[None, :] - idx[:, None]
    DT = np.where(diff >= 0, np.exp(logg[:, None, None] * np.maximum(diff, 0.0)), 0.0) * RDK ** -0.5
    DTt = np.ascontiguousarray(DT.transpose(1, 0, 2)).astype(np.float32)
    dq = np.exp(logg[:, None] * (idx[None, :] + 1.0))
    dq_rep = np.broadcast_to(dq[None], (128, RH, L)).astype(np.float32).copy()
    kdec = {}
    for LL in (128, 64):
        ii = np.arange(LL, dtype=np.float64)
        kd = np.exp(logg[None, :] * (LL - 1.0 - ii[:, None])) * RDK ** -0.5
        full = np.zeros((128, RH), np.float32)
        full[:LL] = kd
        kdec[LL] = full
    gL = {LL: [float(np.exp(logg[h] * LL)) for h in range(RH)] for LL in (128, 64)}
    return dict(mla_cs=mla_cs, ret_cos=ret_cos, ret_sin=ret_sin, DTt=DTt, dq_rep=dq_rep,
                kdec128=kdec[128], kdec64=kdec[64], gL=gL,
                ident_f=np.eye(128, dtype=np.float32), ident_b=np.eye(128, dtype=np.float32).astype(ml_dtypes.bfloat16))


class Seq:
    pass


def build(cfg):
    SEQ, DS, PAST, DEPTH = cfg["SEQ"], cfg["DS"], cfg["PAST"], cfg["DEPTH"]
    types = cfg["types"]
    NAB = sum(1 for t in types if t == "ab")
    NC_ = sum(1 for t in types if t == "c")
    NABd, NCd = max(NAB, 1), max(NC_, 1)
    DO_MLP = cfg.get("mlp", True)
    S5L = cfg.get("s5l", 4)
    USE_HW_SCAN = True
    TP = cfg.get("T", 512)
    maxpos = max(SEQ, PAST + DS)
    nc = bass.Bass("TRN2", target_bir_lowering=False)
    k = K(nc)
    k.in_shapes = {}
    gLtab = cfg["gL"]
    HALFPI = math.pi / 2

    def din(name, shape, dt=F32, used=True):
        if not used:
            shape = [1] * len(shape)
        k.in_shapes[name] = tuple(shape)
        return nc.dram_tensor(name, list(shape), dt, kind="ExternalInput").ap()

    def dout(name, shape, dt=F32):
        return nc.dram_tensor(name, list(shape), dt, kind="ExternalOutput").ap()

    def dscr(name, shape, dt):
        return nc.dram_tensor(name, list(shape), dt, kind="Internal").ap()

    uab, uc = NAB > 0, NC_ > 0
    xp = din("xp", [SEQ, D]); xs = din("xs", [DS, D])
    c_ckv = din("c_ckv", [NABd, PAST, KV_LORA], used=uab); c_kr = din("c_kr", [NABd, PAST, ROPE], used=uab)
    s5re_in = din("s5re_in", [NABd, 64, 64], used=uab); s5im_in = din("s5im_in", [NABd, 64, 64], used=uab)
    ret_in = din("ret_in", [NCd, RH, RDK, RDV], used=uc)
    norm_mix = din("norm_mix", [DEPTH, D]); norm_mlp = din("norm_mlp", [DEPTH, D]); norm_final = din("norm_final", [D])
    w_in_ab = din("w_in_ab", [NABd, D, IN_AB], used=uab); q_a_norm = din("q_a_norm", [NABd, Q_LORA], used=uab)
    kv_a_norm = din("kv_a_norm", [NABd, KV_LORA], used=uab)
    w_q_b = din("w_q_b", [NABd, Q_LORA, NH * 192], used=uab); w_kv_b = din("w_kv_b", [NABd, KV_LORA, NH * 256], used=uab)
    lam_re = din("s5_lam_re", [NABd, 64, 64], used=uab); lam_im = din("s5_lam_im", [NABd, 64, 64], used=uab)
    log_dt = din("s5_log_dt", [NABd, 64], used=uab)
    b_re = din("s5_b_re", [NABd, 64, 64, 16], used=uab); b_im = din("s5_b_im", [NABd, 64, 64, 16], used=uab)
    c_re = din("s5_c_re", [NABd, 64, 16, 64], used=uab); c_im = din("s5_c_im", [NABd, 64, 16, 64], used=uab)
    s5_d = din("s5_d", [NABd, 64, 16], used=uab)
    w_glu = din("w_glu", [NABd, S5W, S5W], used=uab); b_glu = din("b_glu", [NABd, S5W], used=uab)
    w_out_ab = din("w_out_ab", [NABd, D, D], used=uab)
    w_in_c = din("w_in_c", [NCd, D, 12288], used=uc); ret_gn = din("ret_gn", [NCd, RH * RDV], used=uc)
    w_out_c = din("w_out_c", [NCd, RH * RDV, D], used=uc)
    w_up = din("w_up", [DEPTH, D, DFF], used=DO_MLP); w_down = din("w_down", [DEPTH, DFF, D], used=DO_MLP)
    h_mla_cs = din("h_mla_cs", [maxpos, 64]); h_ret_cos = din("h_ret_cos", [128, maxpos]); h_ret_sin = din("h_ret_sin", [128, maxpos])
    h_DTt = din("h_DTt", [128, RH, 128]); h_dq = din("h_dq", [128, RH, 128])
    h_kdec128 = din("h_kdec128", [128, RH]); h_kdec64 = din("h_kdec64", [128, RH])
    h_ident_f = din("h_ident_f", [128, 128]); h_ident_b = din("h_ident_b", [128, 128], BF16)

    y_p = dout("y_p", [SEQ, D]); y_s = dout("y_s", [DS, D])
    ckv_p = dout("ckv_p", [NABd, SEQ, KV_LORA]); kr_p = dout("kr_p", [NABd, SEQ, ROPE])
    s5re_p = dout("s5re_p", [NABd, 64, 64]); s5im_p = dout("s5im_p", [NABd, 64, 64]); ret_p = dout("ret_p", [NCd, RH, RDK, RDV])
    ckv_s = dout("ckv_s", [NABd, DS, KV_LORA]); kr_s = dout("kr_s", [NABd, DS, ROPE])
    s5re_s = dout("s5re_s", [NABd, 64, 64]); s5im_s = dout("s5im_s", [NABd, 64, 64]); ret_s = dout("ret_s", [NCd, RH, RDK, RDV])
    out_bufs = []

    def obuf():
        b = Buf(); out_bufs.append(b)
        return b

    sp_ = Seq(); sp_.name = "p"; sp_.x = xp; sp_.y = y_p; sp_.T = TP; sp_.nt = SEQ // TP; sp_.pos0 = 0; sp_.past = 0
    sp_.ckv_o, sp_.kr_o, sp_.s5re_o, sp_.s5im_o, sp_.ret_o = ckv_p, kr_p, s5re_p, s5im_p, ret_p
    ss_ = Seq(); ss_.name = "s"; ss_.x = xs; ss_.y = y_s; ss_.T = DS; ss_.nt = 1; ss_.pos0 = PAST; ss_.past = PAST
    ss_.ckv_o, ss_.kr_o, ss_.s5re_o, ss_.s5im_o, ss_.ret_o = ckv_s, kr_s, s5re_s, s5im_s, ret_s
    seqs = [sp_, ss_]
    for s in seqs:
        s.nkeys = s.past + s.nt * s.T
        s.xscr = dscr(f"xscr_{s.name}", [s.nt, 128, KC, s.T], F32)
        s.xscr_b = bufs(s.nt)
        s.kvT = [dscr(f"kvT_{s.name}{i}", [128, 2, s.nkeys], BF16) for i in range(NAB)]
        s.krT = [dscr(f"krT_{s.name}{i}", [64, s.nkeys], BF16) for i in range(NAB)]
        s.kvtok = [dscr(f"kvtok_{s.name}{i}", [s.nkeys, KV_LORA], BF16) for i in range(NAB)]
        nsb = (s.nkeys + 511) // 512
        s.kv_b = [[Buf() for _ in range(nsb)] for _ in range(NAB)]
    s5blk = [dscr(f"s5blk{i}", [32, 128, 4, 128], BF16) for i in range(NAB)]
    s5tab = [dscr(f"s5tab{i}", [32, 128, 3, 512], F32) for i in range(NAB)]
    s5c_b = [[Buf() for _ in range(32)] for _ in range(NAB)]

    def sb(name, shape, dt):
        return nc.alloc_sbuf_tensor(name, list(shape), dt)

    uid = [0]

    def sbs(es, shape, dt):
        uid[0] += 1
        return es.enter_context(nc.sbuf_tensor(f"t{uid[0]}", list(shape), dt))

    xt = sb("xt", [128, KC, TP], F32); b_xt = bufs(KC)
    xn = sb("xn", [128, KC, TP], BF16); b_xn = bufs(KC)
    WS = 2
    wsl = [sb(f"wsl{i}", [128, 16 * 512], BF16) for i in range(WS)]; b_wsl = bufs(WS)
    ident_f = sb("ident_f", [128, 128], F32); ident_b = sb("ident_b", [128, 128], BF16); ones_b = sb("ones_b", [128, 128], BF16)
    b_const = Buf()
    gmix = sb("gmix", [128, DEPTH, KC], F32); gmlp = sb("gmlp", [128, DEPTH, KC], F32)
    rstd = sb("rstd", [128, TP], F32); b_rstd = Buf()
    sqb = [sb(f"sqb{i}", [128, 4, TP], BF16) for i in range(2)]; b_sqb = bufs(2)
    eps_t = sb("eps_t", [128, 4], F32)
    NPS = 8
    ps = [nc.alloc_psum_tensor(f"ps{i}", [128, 512], F32) for i in range(NPS)]; b_ps = bufs(NPS)
    st = {"ps": 0, "ws": 0, "ce": 0}

    st["pa"] = 0

    def nps():
        i = 4 + st["ps"] % 4
        st["ps"] += 1
        return ps[i], b_ps[i]

    def npa(n):
        r = []
        for j in range(n):
            i = (st["pa"] + j) % 4
            r.append((ps[i], b_ps[i]))
        st["pa"] += n
        return r

    def psb(p):
        return p[:].bitcast(BF16)

    k.dma(ident_f[:], h_ident_f, W=[b_const])
    k.dma(ident_b[:], h_ident_b, W=[b_const])
    k.memset(k.pool, ones_b[:], 1.0, [], [b_const])
    k.dma(gmix[:], norm_mix.rearrange("l (c p) -> p l c", p=128), W=[b_const], allow_slow_non_contiguous=True)
    k.dma(gmlp[:], norm_mlp.rearrange("l (c p) -> p l c", p=128), W=[b_const], allow_slow_non_contiguous=True)
    k.memset(k.pool, eps_t[:, 0:1], EPS, [], [b_const])
    k.memset(k.pool, eps_t[:, 1:2], GN_EPS, [], [b_const])
    k.memset(k.pool, eps_t[:, 2:3], HALFPI, [], [b_const])
    k.memset(k.pool, eps_t[:, 3:4], 0.0, [], [b_const])

    slab_id = [0]

    def mk_slab(nk, mw):
        slab_id[0] += 1
        return Slab(dscr(f"slab{slab_id[0]}", [128, nk, mw], BF16), nk, mw)

    cast_engs = [k.pool, k.act, k.dve]

    def precast_all(jobs):
        with nc.sbuf_tensor("stg0", [128, 8192], F32) as s0, nc.sbuf_tensor("stg1", [128, 8192], F32) as s1:
            stg = [s0, s1]; b_stg = bufs(2)
            for n, job in enumerate(jobs):
                src3, slab = job[0], job[1]
                s = n % 2
                nk, mw = slab.nk, slab.mw
                sv = stg[s][:, :nk * mw].rearrange("p (c m) -> p c m", c=nk)
                if len(job) > 2:
                    sv = sv.rearrange(job[2], **job[3])
                    for c_ in range(nk):
                        k.dma(sv[:, c_], src3[:, c_], W=[b_stg[s]])
                else:
                    k.dma(sv, src3, W=[b_stg[s]])
                w = st["ws"] % len(wsl); st["ws"] += 1
                wv = wsl[w][:, :nk * mw]
                E = cast_engs[n % 3]
                k.cp(E, wv, stg[s][:, :nk * mw], [b_stg[s]], [b_wsl[w]])
                k.dma(slab.ap, wv.rearrange("p (c m) -> p c m", c=nk), R=[b_wsl[w]], W=[slab.buf])
        k.barrier()

    def slabs_2d(W2, K_, M_, kgrp=16, mgrp=512):
        jobs, grid = [], []
        for m0 in range(0, M_, mgrp):
            mw = min(mgrp, M_ - m0)
            row = []
            for k0 in range(0, K_ // 128, kgrp):
                nk = min(kgrp, K_ // 128 - k0)
                sl = mk_slab(nk, mw)
                jobs.append((W2[k0 * 128:(k0 + nk) * 128, m0:m0 + mw].rearrange("(c p) m -> p c m", p=128), sl))
                row.append(sl)
            grid.append(row)
        return jobs, grid

    jobs = []
    LW = []
    iab = ic = 0
    for l in range(DEPTH):
        lw = {}
        if types[l] == "ab":
            i = iab; iab += 1
            lw["i"] = i
            j, lw["in_cq"] = slabs_2d(w_in_ab[i][:, 0:512], D, 512); jobs += j
            j, lw["in_kv"] = slabs_2d(w_in_ab[i][:, 512:832], D, 320); jobs += j
            j, lw["in_u"] = slabs_2d(w_in_ab[i][:, 832:1856], D, 1024); jobs += j
            wq = w_q_b[i].rearrange("(c p) (h e) -> p c h e", p=128, e=192)
            sl = mk_slab(4, 1024); jobs.append((wq[:, :, :, 0:128], sl, "p c (h e) -> p c h e", dict(h=8))); lw["qb_nope"] = sl
            sl = mk_slab(4, 512); jobs.append((wq[:, :, :, 128:192], sl, "p c (h e) -> p c h e", dict(h=8))); lw["qb_rope"] = sl
            sl = mk_slab(2, 2048); jobs.append((w_kv_b[i].rearrange("(c p) m -> p c m", p=128), sl)); lw["kvb"] = sl
            j, lw["glu"] = slabs_2d(w_glu[i], S5W, S5W); jobs += j
            j, lw["out"] = slabs_2d(w_out_ab[i], D, D); jobs += j
        elif types[l] == "c":
            i = ic; ic += 1
            lw["i"] = i
            lw["qk"], lw["v"], lw["g"], lw["out"] = [], [], [], []
            for h in range(RH):
                sl = mk_slab(16, 512)
                src = w_in_c[i][:, 0:4096].rearrange("(c p) (two h e) -> p c two h e", p=128, two=2, e=256)[:, :, :, h, :]
                jobs.append((src, sl, "p c (t e) -> p c t e", dict(t=2))); lw["qk"].append(sl)
                sl = mk_slab(16, 512); jobs.append((w_in_c[i][:, 4096 + h * 512:4096 + (h + 1) * 512].rearrange("(c p) m -> p c m", p=128), sl)); lw["v"].append(sl)
                sl = mk_slab(16, 512); jobs.append((w_in_c[i][:, 8192 + h * 512:8192 + (h + 1) * 512].rearrange("(c p) m -> p c m", p=128), sl)); lw["g"].append(sl)
                j, g = slabs_2d(w_out_c[i][h * 512:(h + 1) * 512, :], 512, D); jobs += j; lw["out"].append(g)
        if DO_MLP:
            j, lw["up"] = slabs_2d(w_up[l], D, DFF); jobs += j
            j, lw["down"] = slabs_2d(w_down[l], DFF, D); jobs += j
        LW.append(lw)
    precast_all(jobs)

    def load_slab(sl):
        w = st["ws"] % len(wsl); st["ws"] += 1
        v = wsl[w][:, :sl.nk * sl.mw].rearrange("p (c m) -> p c m", c=sl.nk)
        k.dma(v, sl.ap, R=[sl.buf], W=[b_wsl[w]])
        return v, b_wsl[w]

    def proj_fm(grid, rhs_fn, rhs_bufs, Tt, evac, mt_w=128):
        mt = 0
        for row in grid:
            mw = row[0].mw
            nmt = mw // mt_w
            pss = npa(nmt)
            nkt = sum(sl.nk for sl in row)
            kc0 = 0
            for sl in row:
                v, bw = load_slab(sl)
                for j in range(nmt):
                    p, bp = pss[j]
                    for c in range(sl.nk):
                        kc = kc0 + c
                        k.mm(p[:mt_w, :Tt], v[:, c, j * mt_w:(j + 1) * mt_w], rhs_fn(kc), kc == 0, kc == nkt - 1,
                             [bw] + rhs_bufs(kc), [bp])
                kc0 += sl.nk
            for j in range(nmt):
                evac(mt, pss[j][0], pss[j][1])
                mt += 1

    def proj_tm(row, lhs_fn, lhs_bufs, TS, nsub, ncols, evac):
        pss = npa(nsub)
        nkt = sum(sl.nk for sl in row)
        kc0 = 0
        for sl in row:
            v, bw = load_slab(sl)
            for sub in range(nsub):
                p, bp = pss[sub]
                for c in range(sl.nk):
                    kc = kc0 + c
                    k.mm(p[:TS, :ncols], lhs_fn(kc, sub), v[:, c, :ncols], kc == 0, kc == nkt - 1, [bw] + lhs_bufs(kc), [bp])
            kc0 += sl.nk
        for sub in range(nsub):
            evac(sub, pss[sub][0], pss[sub][1])

    def rmsnorm_fm(src, b_src, nchunk, gain, Tt, dst, b_dst):
        p, bp = nps()
        for g0 in range(0, nchunk, 4):
            s = st["ce"] % 2; st["ce"] += 1
            k.actf(sqb[s][:, :, :Tt], src[:, g0:g0 + 4, :Tt], AF.Square, b_src[g0:g0 + 4], [b_sqb[s]])
            for c in range(4):
                k.mm(p[:, :Tt], ones_b[:], sqb[s][:, c, :Tt], g0 + c == 0, g0 + c == nchunk - 1, [b_sqb[s], b_const], [bp])
        k.actf(rstd[:, :Tt], p[:, :Tt], AF.Sqrt, [bp], [b_rstd], bias=eps_t[:, 0:1], scale=1.0 / (nchunk * 128))
        k.op(k.dve, lambda e: e.reciprocal(out=rstd[:, :Tt], in_=rstd[:, :Tt]), [b_rstd], [b_rstd])
        for c in range(nchunk):
            k.stt(k.dve, dst[:, c, :Tt], src[:, c, :Tt], gain(c), rstd[:, :Tt], ALU.mult, ALU.mult, [b_src[c], b_rstd, b_const], [b_dst[c]])

    def add_to_x(Tt):
        def ev(mt, p, bp):
            k.tt(k.dve, xt[:, mt, :Tt], xt[:, mt, :Tt], p[:, :Tt], ALU.add, [bp, b_xt[mt]], [b_xt[mt]])
        return ev

    def load_x0(s, ti):
        Tt = s.T; TS = min(128, Tt); nsub = Tt // TS
        with ExitStack() as es:
            k.barrier()
            xtok = sbs(es, [128, D], F32); b_xtok = Buf()
            for sub in range(nsub):
                t0 = ti * Tt + sub * TS
                k.dma(xtok[:TS, :], s.x[t0:t0 + TS, :], W=[b_xtok])
                for c4 in range(4):
                    p, bp = nps()
                    for j in range(4):
                        c = c4 * 4 + j
                        k.tr(p[:, j * TS:(j + 1) * TS], xtok[:TS, c * 128:(c + 1) * 128], ident_f[:TS, :TS], [b_xtok, b_const], [bp])
                    E = k.act if c4 % 2 == 0 else k.dve
                    k.cp(E, xt[:, c4 * 4:c4 * 4 + 4, sub * TS:(sub + 1) * TS], p[:, :4 * TS].rearrange("p (a b) -> p a b", a=4),
                         [bp], b_xt[c4 * 4:c4 * 4 + 4])
            k.barrier()

    def final_out(s, ti):
        Tt = s.T; TS = min(128, Tt); nsub = Tt // TS
        with ExitStack() as es:
            k.barrier()
            xtok = sbs(es, [128, D], F32); gfin = sbs(es, [128, D], F32); junk = sbs(es, [128, D], BF16); ssum = sbs(es, [128, 4], F32)
            b_xtok = Buf(); b_fin = Buf(); b_g = Buf()
            k.dma(gfin[:], norm_final.partition_broadcast(128), W=[b_g])
            for sub in range(nsub):
                t0 = ti * Tt + sub * TS
                for c4 in range(4):
                    p, bp = nps()
                    for j in range(4):
                        c = c4 * 4 + j
                        k.tr(p[:TS, j * 128:(j + 1) * 128], xt[:, c, sub * TS:(sub + 1) * TS], ident_f[:, :], [b_xt[c], b_const], [bp])
                    E = k.act if c4 % 2 == 0 else k.dve
                    k.cp(E, xtok[:TS, c4 * 512:(c4 + 1) * 512], p[:TS, :], [bp], [b_xtok])
                k.actf(junk[:TS, :], xtok[:TS, :], AF.Square, [b_xtok], [b_fin], accum_out=ssum[:TS, 0:1])
                k.actf(ssum[:TS, 1:2], ssum[:TS, 0:1], AF.Sqrt, [b_fin], [b_fin], bias=eps_t[:TS, 0:1], scale=1.0 / D)
                k.op(k.dve, lambda e: e.reciprocal(out=ssum[:TS, 2:3], in_=ssum[:TS, 1:2]), [b_fin], [b_fin])
                k.stt(k.dve, xtok[:TS, :], xtok[:TS, :], ssum[:TS, 2:3], gfin[:TS, :], ALU.mult, ALU.mult, [b_xtok, b_fin, b_g], [b_xtok])
                k.dma(s.y[t0:t0 + TS, :], xtok[:TS, :], R=[b_xtok], W=[obuf()])
            k.barrier()

    def mlp(l, Tt):
        lw = LW[l]
        with ExitStack() as es:
            k.barrier()
            hT = sbs(es, [128, 16, TP], BF16); tmp = [sbs(es, [128, TP], F32) for _ in range(2)]
            b_hT = bufs(16); b_tmp = bufs(2)
            n_extra = max(0, min(4, (nc.sbuf_bytes_remaining - 30000) // 16512))
            wsl.extend([sbs(es, [128, 16 * 512], BF16) for _ in range(n_extra)]); b_wsl.extend(bufs(n_extra))
            rmsnorm_fm(xt, b_xt, KC, lambda c: gmlp[:, l, c:c + 1], Tt, xn, b_xn)
            for j in range(DFF // 2048):
                def evac_up(mt, p, bp):
                    s = mt % 2
                    k.actf(tmp[s][:, :Tt], p[:, :Tt], AF.Relu, [bp], [b_tmp[s]])
                    E = k.dve if mt % 2 == 0 else k.pool
                    k.tt(E, hT[:, mt, :Tt], tmp[s][:, :Tt], tmp[s][:, :Tt], ALU.mult, [b_tmp[s]], [b_hT[mt]])
                proj_fm([[lw["up"][j * 4 + q][0]] for q in range(4)], lambda kc: xn[:, kc, :Tt], lambda kc: [b_xn[kc]], Tt, evac_up)
                proj_fm([[lw["down"][q][j]] for q in range(4)], lambda kc: hT[:, kc, :Tt], lambda kc: [b_hT[kc]], Tt, add_to_x(Tt))
            k.barrier()
            del wsl[2:]; del b_wsl[2:]

    def rope_fm(p0, bp0, p1, bp1, rc, rs_, b_rope, ta, b_ta, dst, b_dst, Tt):
        k.tt(k.dve, ta[0][:, :Tt], p0[:, :Tt], rc[:, :Tt], ALU.mult, [bp0, b_rope], [b_ta[0]])
        k.tt(k.dve, ta[1][:, :Tt], p1[:, :Tt], rs_[:, :Tt], ALU.mult, [bp1, b_rope], [b_ta[1]])
        k.tt(k.pool, dst[:, 0, :Tt], ta[0][:, :Tt], ta[1][:, :Tt], ALU.subtract, [b_ta[0], b_ta[1]], [b_dst[0]])
        k.tt(k.dve, ta[2][:, :Tt], p0[:, :Tt], rs_[:, :Tt], ALU.mult, [bp0, b_rope], [b_ta[2]])
        k.tt(k.dve, ta[3][:, :Tt], p1[:, :Tt], rc[:, :Tt], ALU.mult, [bp1, b_rope], [b_ta[3]])
        k.tt(k.pool, dst[:, 1, :Tt], ta[2][:, :Tt], ta[3][:, :Tt], ALU.add, [b_ta[2], b_ta[3]], [b_dst[1]])

    def layer_c(l):
        lw = LW[l]; i = lw["i"]
        with ExitStack() as les:
            k.barrier()
            Sst = sbs(les, [128, RH, 2, 512], F32); b_S = bufs(RH)
            DTt = sbs(les, [128, RH, 128], F32); dq = sbs(les, [128, RH, 128], F32)
            kd128 = sbs(les, [128, RH], F32); kd64 = sbs(les, [128, RH], F32)
            b_lc = Buf()
            k.dma(DTt[:], h_DTt, W=[b_lc]); k.dma(dq[:], h_dq, W=[b_lc])
            k.dma(kd128[:], h_kdec128, W=[b_lc]); k.dma(kd64[:], h_kdec64, W=[b_lc])
            for s in seqs:
                if s.past == 0:
                    for h in range(RH):
                        k.memset(k.pool, Sst[:, h, :, :], 0.0, [], [b_S[h]])
                else:
                    for h in range(RH):
                        k.dma(Sst[:, h, :, :], ret_in[i, h].rearrange("(c p) v -> p c v", p=128), W=[b_S[h]])
                for ti in range(s.nt):
                    tile_begin(l, s, ti)
                    mixer_c(l, s, ti, Sst, b_S, DTt, dq, kd128 if s.T >= 128 else kd64, b_lc)
                    tile_end(l, s, ti)
                for h in range(RH):
                    k.dma(s.ret_o[i, h].rearrange("(c p) v -> p c v", p=128), Sst[:, h, :, :], R=[b_S[h]], W=[obuf()])
            k.barrier()

    def mixer_c(l, s, ti, Sst, b_S, DTt, dq, kdec, b_lc):
        lw = LW[l]; i = lw["i"]; Tt = s.T; L = min(128, Tt); nch = Tt // L
        pos_lo = s.pos0 + ti * Tt
        gLs = gLtab[L]
        rmsnorm_fm(xt, b_xt, KC, lambda c: gmix[:, l, c:c + 1], Tt, xn, b_xn)
        with ExitStack() as es:
            k.barrier()
            rc = sbs(es, [128, TP], F32); rs_ = sbs(es, [128, TP], F32); b_rope = Buf()
            k.dma(rc[:, :Tt], h_ret_cos[:, pos_lo:pos_lo + Tt], W=[b_rope])
            k.dma(rs_[:, :Tt], h_ret_sin[:, pos_lo:pos_lo + Tt], W=[b_rope])
            qr = sbs(es, [128, 2, TP], BF16); kr_ = sbs(es, [128, 2, TP], BF16); qt = sbs(es, [128, 2, TP], BF16)
            b_qr = bufs(2); b_kr = bufs(2); b_qt = bufs(2)
            ta = [sbs(es, [128, TP], F32) for _ in range(4)]; b_ta = bufs(4)
            vt = sbs(es, [128, 4, 512], BF16); gt = sbs(es, [128, 4, 512], BF16); ktk = sbs(es, [128, 4, 256], BF16)
            b_vt = bufs(4); b_gt = bufs(4); b_ktk = bufs(4)
            PT = [sbs(es, [128, 128], BF16) for _ in range(2)]; b_PT = bufs(2)
            onf = [sbs(es, [128, 512], F32) for _ in range(2)]; b_onf = bufs(2)
            onb = [sbs(es, [128, 512], BF16) for _ in range(2)]; b_onb = bufs(2)
            onT = sbs(es, [128, 4, TP], BF16); b_onT = bufs(4)
            Sbf = sbs(es, [128, 2, 512], BF16); b_Sbf = bufs(2)
            gnr = [sbs(es, [128, 512], F32) for _ in range(2)]; b_gnr = bufs(2)
            stats = [sbs(es, [128, 16], F32) for _ in range(2)]; b_stats = bufs(2)
            cnt = 0
            for h in range(RH):
                g_ = h % 2
                k.dma(gnr[g_][:], ret_gn[i, h * 512:(h + 1) * 512].partition_broadcast(128), W=[b_gnr[g_]])
                hold = []

                def evac_qk(mt, p, bp):
                    hold.append((p, bp))
                    if mt == 1:
                        rope_fm(hold[0][0], hold[0][1], hold[1][0], hold[1][1], rc, rs_, b_rope, ta, b_ta, qr, b_qr, Tt)
                    if mt == 3:
                        rope_fm(hold[2][0], hold[2][1], hold[3][0], hold[3][1], rc, rs_, b_rope, ta, b_ta, kr_, b_kr, Tt)
                proj_fm([[lw["qk"][h]]], lambda kc: xn[:, kc, :Tt], lambda kc: [b_xn[kc]], Tt, evac_qk)
                for i2 in range(2):
                    k.tt(k.pool, qt[:, i2, :Tt].rearrange("p (c n) -> p c n", n=L), qr[:, i2, :Tt].rearrange("p (c n) -> p c n", n=L),
                         dq[:, h:h + 1, :L].broadcast_to([128, nch, L]), ALU.mult, [b_qr[i2], b_lc], [b_qt[i2]])

                def evac_v(sub, p, bp):
                    k.cp(k.act, vt[:L, sub, :], p[:L, :512], [bp], [b_vt[sub]])

                def evac_g(sub, p, bp):
                    k.actf(gt[:L, sub, :], p[:L, :512], AF.Silu, [bp], [b_gt[sub]])
                lhs = lambda kc, sub: xn[:, kc, sub * L:(sub + 1) * L]
                proj_tm([lw["v"][h]], lhs, lambda kc: [b_xn[kc]], L, nch, 512, evac_v)
                proj_tm([lw["g"][h]], lhs, lambda kc: [b_xn[kc]], L, nch, 512, evac_g)
                for sub in range(nch):
                    p, bp = nps(); pb = psb(p)
                    for i2 in range(2):
                        k.tr(pb[:L, i2 * 128:(i2 + 1) * 128], kr_[:, i2, sub * L:(sub + 1) * L], ident_b[:, :], [b_kr[i2], b_const], [bp])
                    k.ts(k.dve, ktk[:L, sub, :], pb[:L, :256], kdec[:L, h:h + 1], None, ALU.mult, None, [bp, b_lc], [b_ktk[sub]])
                for i2 in range(2):
                    k.cp(k.act, Sbf[:, i2, :], Sst[:, h, i2, :], [b_S[h]], [b_Sbf[i2]])
                for ci in range(nch):
                    c0 = ci * L
                    x = cnt % 2; cnt += 1
                    p_s, bp_s = nps()
                    for i2 in range(2):
                        k.mm(p_s[:L, :L], kr_[:, i2, c0:c0 + L], qr[:, i2, c0:c0 + L], i2 == 0, i2 == 1, [b_kr[i2], b_qr[i2]], [bp_s])
                    k.tt(k.dve, PT[x][:L, :L], p_s[:L, :L], DTt[:L, h, :L], ALU.mult, [bp_s, b_lc], [b_PT[x]])
                    p_o, bp_o = nps()
                    k.mm(p_o[:L, :512], PT[x][:L, :L], vt[:L, ci, :], True, False, [b_PT[x], b_vt[ci]], [bp_o])
                    for i2 in range(2):
                        k.mm(p_o[:L, :512], qt[:, i2, c0:c0 + L], Sbf[:, i2, :], False, i2 == 1, [b_qt[i2], b_Sbf[i2]], [bp_o])
                    sx = stats[x]; bsx = b_stats[x]
                    k.op(k.dve, lambda e: e.bn_stats(out=sx[:L, 0:6], in_=p_o[:L, :512]), [bp_o], [bsx])
                    k.op(k.dve, lambda e: e.bn_aggr(out=sx[:L, 6:8], in_=sx[:L, 0:6]), [bsx], [bsx])
                    k.actf(sx[:L, 8:9], sx[:L, 7:8], AF.Sqrt, [bsx], [bsx], bias=eps_t[:L, 1:2], scale=1.0)
                    k.op(k.dve, lambda e: e.reciprocal(out=sx[:L, 9:10], in_=sx[:L, 8:9]), [bsx], [bsx])
                    k.ts(k.dve, onf[x][:L, :], p_o[:L, :512], sx[:L, 6:7], sx[:L, 9:10], ALU.subtract, ALU.mult, [bp_o, bsx], [b_onf[x]])
                    k.tt(k.pool, onf[x][:L, :], onf[x][:L, :], gnr[g_][:L, :], ALU.mult, [b_onf[x], b_gnr[g_]], [b_onf[x]])
                    k.tt(k.pool, onb[x][:L, :], onf[x][:L, :], gt[:L, ci, :], ALU.mult, [b_onf[x], b_gt[ci]], [b_onb[x]])
                    p_t, bp_t = nps(); ptb = psb(p_t)
                    for j in range(4):
                        k.tr(ptb[:, j * L:(j + 1) * L], onb[x][:L, j * 128:(j + 1) * 128], ident_b[:L, :L], [b_onb[x], b_const], [bp_t])
                    k.cp(k.act, onT[:, :, c0:c0 + L], ptb[:, :4 * L].rearrange("p (a b) -> p a b", a=4), [bp_t], b_onT)
                    for i2 in range(2):
                        p_d, bp_d = nps()
                        k.mm(p_d[:, :512], ktk[:L, ci, i2 * 128:(i2 + 1) * 128], vt[:L, ci, :], True, True, [b_ktk[ci], b_vt[ci]], [bp_d])
                        k.stt(k.dve, Sst[:, h, i2, :], Sst[:, h, i2, :], gLs[h], p_d[:, :512], ALU.mult, ALU.add, [bp_d, b_S[h]], [b_S[h]])
                        k.cp(k.act, Sbf[:, i2, :], Sst[:, h, i2, :], [b_S[h]], [b_Sbf[i2]])
                proj_fm(lw["out"][h], lambda kc: onT[:, kc, :Tt], lambda kc: [b_onT[kc]], Tt, add_to_x(Tt))
            k.barrier()

    def s5_setup(i, es0):
        with ExitStack() as es:
            k.barrier()
            def t32(n=32):
                return sbs(es, [128, n], F32)
            lr = t32(); li = t32(); dtt = t32(); mag = t32(); th = t32(); cc = t32(); sn = t32()
            c2 = t32(); s2 = t32(); cs_ = t32(); ar = t32(); ai = t32(); den = t32(); am1 = t32(); fre = t32(); fim = t32(); u1 = t32(); u2 = t32()
            B = Buf()
            k.dma(lr[:], lam_re[i].rearrange("(gp g2) n -> (g2 n) gp", g2=2), W=[B], allow_slow_non_contiguous=True)
            k.dma(li[:], lam_im[i].rearrange("(gp g2) n -> (g2 n) gp", g2=2), W=[B], allow_slow_non_contiguous=True)
            for g2 in range(2):
                src = bass.AP(tensor=log_dt.tensor, offset=i * 64 + g2, ap=[[0, 64], [2, 32]])
                k.dma(dtt[g2 * 64:(g2 + 1) * 64, :], src, W=[B], allow_slow_non_contiguous=True)
            R_, W_ = [B], [B]
            k.ts(k.dve, lr[:], lr[:], -1e-4, None, ALU.min, None, R_, W_)
            k.actf(dtt[:], dtt[:], AF.Exp, R_, W_)
            k.tt(k.dve, u1[:], lr[:], dtt[:], ALU.mult, R_, W_)
            k.actf(mag[:], u1[:], AF.Exp, R_, W_)
            k.tt(k.dve, th[:], li[:], dtt[:], ALU.mult, R_, W_)
            k.actf(cc[:], th[:], AF.Sin, R_, W_, bias=eps_t[:, 2:3], scale=1.0 / 16)
            k.actf(sn[:], th[:], AF.Sin, R_, W_, bias=eps_t[:, 3:4], scale=1.0 / 16)
            for _ in range(4):
                k.tt(k.dve, c2[:], cc[:], cc[:], ALU.mult, R_, W_)
                k.tt(k.dve, s2[:], sn[:], sn[:], ALU.mult, R_, W_)
                k.tt(k.dve, cs_[:], cc[:], sn[:], ALU.mult, R_, W_)
                k.tt(k.dve, cc[:], c2[:], s2[:], ALU.subtract, R_, W_)
                k.ts(k.dve, sn[:], cs_[:], 2.0, None, ALU.mult, None, R_, W_)
            k.tt(k.dve, ar[:], mag[:], cc[:], ALU.mult, R_, W_)
            k.tt(k.dve, ai[:], mag[:], sn[:], ALU.mult, R_, W_)
            k.tt(k.dve, c2[:], lr[:], lr[:], ALU.mult, R_, W_)
            k.tt(k.dve, s2[:], li[:], li[:], ALU.mult, R_, W_)
            k.tt(k.dve, den[:], c2[:], s2[:], ALU.add, R_, W_)
            k.op(k.dve, lambda e: e.reciprocal(out=den[:], in_=den[:]), R_, W_)
            k.ts(k.dve, am1[:], ar[:], -1.0, None, ALU.add, None, R_, W_)
            k.tt(k.dve, u1[:], am1[:], lr[:], ALU.mult, R_, W_)
            k.tt(k.dve, u2[:], ai[:], li[:], ALU.mult, R_, W_)
            k.tt(k.dve, u1[:], u1[:], u2[:], ALU.add, R_, W_)
            k.tt(k.dve, fre[:], u1[:], den[:], ALU.mult, R_, W_)
            k.tt(k.dve, u1[:], ai[:], lr[:], ALU.mult, R_, W_)
            k.tt(k.dve, u2[:], am1[:], li[:], ALU.mult, R_, W_)
            k.tt(k.dve, u1[:], u1[:], u2[:], ALU.subtract, R_, W_)
            k.tt(k.dve, fim[:], u1[:], den[:], ALU.mult, R_, W_)
            bre = sbs(es, [128, 32, 16], F32); bim = sbs(es, [128, 32, 16], F32)
            cre = sbs(es, [128, 32, 16], F32); cim = sbs(es, [128, 32, 16], F32)
            w1 = sbs(es, [128, 32, 16], F32); w2 = sbs(es, [128, 32, 16], F32)
            bbr = sbs(es, [128, 32, 16], F32); bbi = sbs(es, [128, 32, 16], F32)
            k.dma(bre[:], b_re[i].rearrange("(gp g2) n p -> (g2 n) gp p", g2=2), W=[B])
            k.dma(bim[:], b_im[i].rearrange("(gp g2) n p -> (g2 n) gp p", g2=2), W=[B])
            for g2 in range(2):
                for gp_ in range(32):
                    k.dma(cre[g2 * 64:(g2 + 1) * 64, gp_, :], c_re[i][2 * gp_ + g2].rearrange("q n -> n q"), W=[B], allow_slow_non_contiguous=True)
                    k.dma(cim[g2 * 64:(g2 + 1) * 64, gp_, :], c_im[i][2 * gp_ + g2].rearrange("q n -> n q"), W=[B], allow_slow_non_contiguous=True)
            freb = fre[:, :].rearrange("p (g o) -> p g o", o=1).broadcast_to([128, 32, 16])
            fimb = fim[:, :].rearrange("p (g o) -> p g o", o=1).broadcast_to([128, 32, 16])
            k.tt(k.dve, w1[:], bre[:], freb, ALU.mult, R_, W_)
            k.tt(k.dve, w2[:], bim[:], fimb, ALU.mult, R_, W_)
            k.tt(k.dve, bbr[:], w1[:], w2[:], ALU.subtract, R_, W_)
            k.tt(k.dve, w1[:], bim[:], freb, ALU.mult, R_, W_)
            k.tt(k.dve, w2[:], bre[:], fimb, ALU.mult, R_, W_)
            k.tt(k.dve, bbi[:], w1[:], w2[:], ALU.add, R_, W_)
            k.ts(k.dve, cim[:], cim[:], -1.0, None, ALU.mult, None, R_, W_)
            natR = [sbs(es, [128, 128], BF16) for _ in range(4)]; natI = [sbs(es, [128, 128], BF16) for _ in range(4)]
            blkq = [sbs(es, [128, 4, 128], BF16) for _ in range(4)]; b_q = bufs(4)
            for q in range(4):
                k.memset(k.pool, natR[q][:], 0.0, [], [b_q[q]]); k.memset(k.pool, natI[q][:], 0.0, [], [b_q[q]])
                k.memset(k.pool, blkq[q][:], 0.0, [], [b_q[q]])
            tab = [sbs(es, [128, 3, 512], F32) for _ in range(2)]; b_tab = bufs(2)
            tt_ = [sbs(es, [128, 256], F32) for _ in range(2)]
            for gp in range(32):
                q = gp % 4
                bq = b_q[q]
                for g2 in range(2):
                    rows = slice(g2 * 64, (g2 + 1) * 64)
                    cols = slice(q * 32 + g2 * 16, q * 32 + g2 * 16 + 16)
                    k.cp(k.dve, natR[q][rows, cols], bbr[rows, gp, :], [B, bq], [bq])
                    k.cp(k.dve, natI[q][rows, cols], bbi[rows, gp, :], [B, bq], [bq])
                    k.cp(k.pool, blkq[q][rows, 2, cols], cre[rows, gp, :], [B, bq], [bq])
                    k.cp(k.pool, blkq[q][rows, 3, cols], cim[rows, gp, :], [B, bq], [bq])
                p, bp = nps(); pb = psb(p)
                k.tr(pb[:, 0:128], natR[q][:, :], ident_b[:, :], [bq, b_const], [bp])
                k.tr(pb[:, 128:256], natI[q][:, :], ident_b[:, :], [bq, b_const], [bp])
                k.cp(k.act, blkq[q][:, 0:2, :], pb[:, 0:256].rearrange("p (a b) -> p a b", a=2), [bp, bq], [bq])
                k.dma(s5blk[i][gp], blkq[q][:], R=[bq], W=[s5c_b[i][gp]])
                x = gp % 2
                T_ = tab[x]; bt = b_tab[x]
                k.cp(k.dve, T_[:, 0, 0:1], cc[:, gp:gp + 1], [B, bt], [bt])
                k.cp(k.dve, T_[:, 1, 0:1], sn[:, gp:gp + 1], [B, bt], [bt])
                k.cp(k.pool, T_[:, 2, :], mag[:, gp:gp + 1].broadcast_to([128, 512]), [B, bt], [bt])
                m = 1
                while m < 512:
                    er = T_[:, 0, m - 1:m]; ei = T_[:, 1, m - 1:m]
                    k.ts(k.dve, tt_[0][:, :m], T_[:, 1, 0:m], ei, None, ALU.mult, None, [bt], [bt])
                    k.ts(k.dve, tt_[1][:, :m], T_[:, 0, 0:m], ei, None, ALU.mult, None, [bt], [bt])
                    k.stt(k.dve, T_[:, 0, m:2 * m], T_[:, 0, 0:m], er, tt_[0][:, :m], ALU.mult, ALU.subtract, [bt], [bt])
                    k.stt(k.dve, T_[:, 1, m:2 * m], T_[:, 1, 0:m], er, tt_[1][:, :m], ALU.mult, ALU.add, [bt], [bt])
                    m *= 2
                k.dma(s5tab[i][gp], T_[:], R=[bt], W=[s5c_b[i][gp]])
            k.barrier()

    def kv_cache_setup(i, s):
        with ExitStack() as es:
            k.barrier()
            cst = sbs(es, [128, 320], F32); cb = sbs(es, [128, 320], BF16); cT = sbs(es, [128, 384], BF16)
            B1, B2, B3 = Buf(), Buf(), Buf()
            for b in range(s.past // 128):
                kb = s.kv_b[i][(b * 128) // 512]
                k.dma(cst[:, 0:256], c_ckv[i, b * 128:(b + 1) * 128, :], W=[B1])
                k.dma(cst[:, 256:320], c_kr[i, b * 128:(b + 1) * 128, :], W=[B1])
                k.cp(k.act, cb[:, :], cst[:, :], [B1], [B2])
                k.dma(s.kvtok[i][b * 128:(b + 1) * 128, :], cb[:, 0:256], R=[B2], W=[kb])
                p, bp = nps(); pb = psb(p)
                k.tr(pb[:, 0:128], cb[:, 0:128], ident_b[:, :], [B2, b_const], [bp])
                k.tr(pb[:, 128:256], cb[:, 128:256], ident_b[:, :], [B2, b_const], [bp])
                k.tr(pb[:64, 256:384], cb[:, 256:320], ident_b[:, :], [B2, b_const], [bp])
                k.cp(k.dve, cT[:, 0:256], pb[:, 0:256], [bp], [B3])
                k.cp(k.dve, cT[:64, 256:384], pb[:64, 256:384], [bp], [B3])
                k.dma(s.kvT[i][:, :, b * 128:(b + 1) * 128], cT[:, 0:256].rearrange("p (a b) -> p a b", a=2), R=[B3], W=[kb])
                k.dma(s.krT[i][:, b * 128:(b + 1) * 128], cT[:64, 256:384], R=[B3], W=[kb])
            k.barrier()

    def layer_ab(l):
        lw = LW[l]; i = lw["i"]
        s5_setup(i, None)
        with ExitStack() as les:
            k.barrier()
            wkvn = sbs(les, [128, 2, 2048], BF16); WkT = sbs(les, [128, 8, 2, 128], BF16)
            qag = sbs(les, [128, 4], F32); kvg = sbs(les, [128, 256], F32); bglu = sbs(les, [128, 8], F32); dcol = sbs(les, [128, 8], F32)
            car = sbs(les, [128, 2, 32], F32); b_car = Buf()
            b_la = Buf()
            k.dma(wkvn[:], lw["kvb"].ap, R=[lw["kvb"].buf], W=[b_la])
            k.dma(qag[:], q_a_norm[i].rearrange("(c p) -> p c", p=128), W=[b_la], allow_slow_non_contiguous=True)
            k.dma(kvg[:], kv_a_norm[i].partition_broadcast(128), W=[b_la])
            k.dma(bglu[:], b_glu[i].rearrange("(c p) -> p c", p=128), W=[b_la], allow_slow_non_contiguous=True)
            k.dma(dcol[:], s5_d[i].rearrange("g p -> (g p)").rearrange("(c p) -> p c", p=128), W=[b_la], allow_slow_non_contiguous=True)
            for h in range(NH):
                p, bp = nps(); pb = psb(p)
                for half in range(2):
                    k.tr(pb[:, half * 128:(half + 1) * 128], wkvn[:, half, h * 256:h * 256 + 128], ident_b[:, :], [b_la, b_const], [bp])
                k.cp(k.act, WkT[:, h, :, :], pb[:, 0:256].rearrange("p (a b) -> p a b", a=2), [bp], [b_la])
            LS = dict(wkvn=wkvn, WkT=WkT, qag=qag, kvg=kvg, bglu=bglu, dcol=dcol, car=car, b_car=b_car, b_la=b_la)
            for s in seqs:
                if s.past == 0:
                    k.memset(k.pool, car[:], 0.0, [], [b_car])
                else:
                    k.dma(car[:, 0, :], s5re_in[i].rearrange("(gp g2) n -> (g2 n) gp", g2=2), W=[b_car], allow_slow_non_contiguous=True)
                    k.dma(car[:, 1, :], s5im_in[i].rearrange("(gp g2) n -> (g2 n) gp", g2=2), W=[b_car], allow_slow_non_contiguous=True)
                    kv_cache_setup(i, s)
                for ti in range(s.nt):
                    tile_begin(l, s, ti)
                    mixer_ab(l, s, ti, LS)
                    tile_end(l, s, ti)
                k.dma(s.s5re_o[i].rearrange("(gp g2) n -> (g2 n) gp", g2=2), car[:, 0, :], R=[b_car], W=[obuf()], allow_slow_non_contiguous=True)
                k.dma(s.s5im_o[i].rearrange("(gp g2) n -> (g2 n) gp", g2=2), car[:, 1, :], R=[b_car], W=[obuf()], allow_slow_non_contiguous=True)
            k.barrier()

    def mixer_ab(l, s, ti, LS):
        lw = LW[l]; i = lw["i"]; Tt = s.T; TS = min(128, Tt); nsub = Tt // TS
        pos_lo = s.pos0 + ti * Tt
        wkvn, WkT, qag, kvg, bglu, dcol, car, b_car, b_la = (LS[n] for n in ("wkvn", "WkT", "qag", "kvg", "bglu", "dcol", "car", "b_car", "b_la"))
        rmsnorm_fm(xt, b_xt, KC, lambda c: gmix[:, l, c:c + 1], Tt, xn, b_xn)
        with ExitStack() as es:
            k.barrier()
            mixT = sbs(es, [128, 16, TP], BF16); b_mix = bufs(16)
            if cfg.get("s5", True):
              with ExitStack() as e5:
                ubf = sbs(e5, [128, 8, TP], BF16); b_ubf = bufs(8)
                u = ubf; b_u = b_ubf

                def evac_u(mt, p, bp):
                    k.cp(k.act if mt % 2 == 0 else k.dve, ubf[:, mt, :Tt], p[:, :Tt], [bp], [b_ubf[mt]])
                proj_fm(lw["in_u"], lambda kc: xn[:, kc, :Tt], lambda kc: [b_xn[kc]], Tt, evac_u)
                tabs = [sbs(e5, [128, 3, TP], F32) for _ in range(2)]; blks = [sbs(e5, [128, 4, 128], BF16) for _ in range(2)]
                b_tb = bufs(2)
                tw = [sbs(e5, [128, TP], F32) for _ in range(4)]; b_tw = bufs(4)
                Z = [sbs(e5, [128, TP], F32) for _ in range(2)]; b_Z = bufs(2)
                sc = [sbs(e5, [128, TP], F32) for _ in range(2)]; b_sc = bufs(2)
                Sb = [sbs(e5, [128, 2, TP], BF16) for _ in range(2)]; b_Sb = bufs(2)
                yb = [sbs(e5, [128, TP], F32) for _ in range(2)]; b_yb = bufs(2)
                cw = sbs(e5, [128, 8], F32); b_cw = Buf()
                p_y = None
                for gp in range(32):
                    ct = gp // 4; x = gp % 2
                    T_ = tabs[x]; Bk = blks[x]; btb = b_tb[x]
                    if S5L < 0.4:
                        continue
                    k.dma(T_[:, :, :Tt], s5tab[i][gp][:, :, :Tt], R=[s5c_b[i][gp]], W=[btb])
                    k.dma(Bk[:], s5blk[i][gp], R=[s5c_b[i][gp]], W=[btb])
                    if S5L < 0.6:
                        continue
                    C_ = T_[:, 0, :Tt]; S_ = T_[:, 1, :Tt]; Rr = T_[:, 2, :Tt]
                    pr, bpr = nps(); pi_, bpi = nps()
                    k.mm(pr[:, :Tt], Bk[:, 0, :], ubf[:, ct, :Tt], True, True, [btb, b_ubf[ct]], [bpr])
                    k.mm(pi_[:, :Tt], Bk[:, 1, :], ubf[:, ct, :Tt], True, True, [btb, b_ubf[ct]], [bpi])
                    if S5L < 0.8:
                        continue
                    k.tt(k.dve, tw[0][:, :Tt], pr[:, :Tt], C_, ALU.mult, [bpr, btb], [b_tw[0]])
                    k.tt(k.dve, tw[1][:, :Tt], pi_[:, :Tt], S_, ALU.mult, [bpi, btb], [b_tw[1]])
                    k.tt(k.pool, Z[0][:, :Tt], tw[0][:, :Tt], tw[1][:, :Tt], ALU.add, [b_tw[0], b_tw[1]], [b_Z[0]])
                    k.tt(k.dve, tw[2][:, :Tt], pi_[:, :Tt], C_, ALU.mult, [bpi, btb], [b_tw[2]])
                    k.tt(k.dve, tw[3][:, :Tt], pr[:, :Tt], S_, ALU.mult, [bpr, btb], [b_tw[3]])
                    k.tt(k.pool, Z[1][:, :Tt], tw[2][:, :Tt], tw[3][:, :Tt], ALU.subtract, [b_tw[2], b_tw[3]], [b_Z[1]])
                    if S5L < 2:
                        continue
                    for ri in range(2):
                        if USE_HW_SCAN:
                            k.op(k.dve, lambda e: e.tensor_tensor_scan(out=sc[ri][:, :Tt], data0=Rr, data1=Z[ri][:, :Tt],
                                                                       initial=car[:, ri, gp:gp + 1], op0=ALU.mult, op1=ALU.add),
                                 [btb, b_Z[ri], b_car], [b_sc[ri]])
                            continue
                        A_, bA_ = Z[ri], b_Z[ri]
                        B_, bB_ = sc[ri], b_sc[ri]
                        k.cp(k.dve, cw[:, 4:5], T_[:, 2, 0:1], [btb], [b_cw])
                        sh = 1
                        while sh < Tt:
                            k.cp(k.act, B_[:, 0:sh], A_[:, 0:sh], [bA_], [bB_])
                            k.stt(k.dve, B_[:, sh:Tt], A_[:, 0:Tt - sh], cw[:, 4:5], A_[:, sh:Tt], ALU.mult, ALU.add, [bA_, b_cw], [bB_])
                            k.tt(k.dve, cw[:, 4:5], cw[:, 4:5], cw[:, 4:5], ALU.mult, [b_cw], [b_cw])
                            A_, bA_, B_, bB_ = B_, bB_, A_, bA_
                            sh *= 2
                        raise NotImplementedError("carry term needs r^(t+1) table")
                    k.tt(k.pool, tw[0][:, :Tt], sc[0][:, :Tt], C_, ALU.mult, [b_sc[0], btb], [b_tw[0]])
                    k.tt(k.pool, tw[1][:, :Tt], sc[1][:, :Tt], S_, ALU.mult, [b_sc[1], btb], [b_tw[1]])
                    k.tt(k.dve, Sb[x][:, 0, :Tt], tw[0][:, :Tt], tw[1][:, :Tt], ALU.subtract, [b_tw[0], b_tw[1]], [b_Sb[x]])
                    k.tt(k.pool, tw[2][:, :Tt], sc[1][:, :Tt], C_, ALU.mult, [b_sc[1], btb], [b_tw[2]])
                    k.tt(k.pool, tw[3][:, :Tt], sc[0][:, :Tt], S_, ALU.mult, [b_sc[0], btb], [b_tw[3]])
                    k.tt(k.dve, Sb[x][:, 1, :Tt], tw[2][:, :Tt], tw[3][:, :Tt], ALU.add, [b_tw[2], b_tw[3]], [b_Sb[x]])
                    e_ = Tt - 1
                    lastC = T_[:, 0, e_:e_ + 1]; lastS = T_[:, 1, e_:e_ + 1]
                    k.tt(k.dve, cw[:, 0:1], sc[0][:, e_:e_ + 1], lastC, ALU.mult, [b_sc[0], btb], [b_cw])
                    k.tt(k.dve, cw[:, 1:2], sc[1][:, e_:e_ + 1], lastS, ALU.mult, [b_sc[1], btb], [b_cw])
                    k.tt(k.dve, cw[:, 2:3], sc[1][:, e_:e_ + 1], lastC, ALU.mult, [b_sc[1], btb], [b_cw])
                    k.tt(k.dve, cw[:, 3:4], sc[0][:, e_:e_ + 1], lastS, ALU.mult, [b_sc[0], btb], [b_cw])
                    k.tt(k.dve, car[:, 0, gp:gp + 1], cw[:, 0:1], cw[:, 1:2], ALU.subtract, [b_cw], [b_car])
                    k.tt(k.dve, car[:, 1, gp:gp + 1], cw[:, 2:3], cw[:, 3:4], ALU.add, [b_cw], [b_car])
                    if S5L < 3:
                        continue
                    if gp % 4 == 0:
                        (p_y, bp_y), = npa(1)
                    k.mm(p_y[:, :Tt], Bk[:, 2, :], Sb[x][:, 0, :Tt], gp % 4 == 0, False, [btb, b_Sb[x]], [bp_y])
                    k.mm(p_y[:, :Tt], Bk[:, 3, :], Sb[x][:, 1, :Tt], False, gp % 4 == 3, [btb, b_Sb[x]], [bp_y])
                    if gp % 4 == 3:
                        yy = yb[ct % 2]; byy = b_yb[ct % 2]
                        k.stt(k.dve, yy[:, :Tt], u[:, ct, :Tt], dcol[:, ct:ct + 1], p_y[:, :Tt], ALU.mult, ALU.add, [bp_y, b_ubf[ct], b_la], [byy])
                        g1 = tw[0]; g2_ = tw[1]
                        k.tt(k.pool, g1[:, :Tt], yy[:, :Tt], yy[:, :Tt], ALU.mult, [byy], [b_tw[0]])
                        k.ts(k.dve, g1[:, :Tt], g1[:, :Tt], 0.044715, 1.0, ALU.mult, ALU.add, [b_tw[0]], [b_tw[0]])
                        k.tt(k.pool, g1[:, :Tt], g1[:, :Tt], yy[:, :Tt], ALU.mult, [b_tw[0], byy], [b_tw[0]])
                        k.actf(g2_[:, :Tt], g1[:, :Tt], AF.Sigmoid, [b_tw[0]], [b_tw[1]], scale=2.0 * math.sqrt(2.0 / math.pi))
                        k.tt(k.pool, ubf[:, ct, :Tt], yy[:, :Tt], g2_[:, :Tt], ALU.mult, [byy, b_tw[1]], [b_ubf[ct]])
                def evac_glu(mt, p, bp):
                    yy = yb[mt % 2]; byy = b_yb[mt % 2]
                    k.actf(yy[:, :Tt], p[:, :Tt], AF.Sigmoid, [bp, b_la], [byy], bias=bglu[:, mt:mt + 1], scale=1.0)
                    k.tt(k.pool, mixT[:, 8 + mt, :Tt], ubf[:, mt, :Tt], yy[:, :Tt], ALU.mult, [byy, b_ubf[mt]], [b_mix[8 + mt]])
                if S5L >= 4:
                    proj_fm(lw["glu"], lambda kc: ubf[:, kc, :Tt], lambda kc: [b_ubf[kc]], Tt, evac_glu)
                else:
                    for c in range(8, 16):
                        k.memset(k.pool, mixT[:, c, :Tt], 0.0, [], [b_mix[c]])
                k.barrier()
            else:
                for c in range(8, 16):
                    k.memset(k.pool, mixT[:, c, :Tt], 0.0, [], [b_mix[c]])
            with ExitStack() as ea:
                k.barrier()
                cq = sbs(ea, [128, 4, TP], F32); b_cq = bufs(4); cqn = sbs(ea, [128, 4, TP], BF16); b_cqn = bufs(4)
                qn = [sbs(ea, [128, TP], BF16) for _ in range(2)]; b_qn = bufs(2)
                qp = sbs(ea, [128, 8, 2, TP], BF16); b_qp = bufs(8)
                qrT = sbs(ea, [64, 8, TP], BF16); b_qrT = Buf()
                cstab = sbs(ea, [128, 4, 64], F32); b_cs = Buf()
                tk = [sbs(ea, [128, 8, 32], F32) for _ in range(4)]; b_tk = bufs(4)
                qrt = sbs(ea, [128, 8, 64], BF16); b_qrt = Buf()
                ckvn = sbs(ea, [128, 256], F32); ckvb = sbs(ea, [128, 256], BF16); ckvTs = sbs(ea, [128, 2, 128], BF16)
                krn = sbs(ea, [128, 64], F32); krb = sbs(ea, [128, 64], BF16); krTs = sbs(ea, [64, 128], BF16)
                jk = sbs(ea, [128, 256], BF16); stq = sbs(ea, [128, 4], F32)
                b_kw = Buf()
                for sub in range(nsub):
                    k.dma(cstab[:TS, sub, :], h_mla_cs[pos_lo + sub * TS:pos_lo + (sub + 1) * TS, :], W=[b_cs])

                def evac_cq(mt, p, bp):
                    k.cp(k.act, cq[:, mt, :Tt], p[:, :Tt], [bp], [b_cq[mt]])
                proj_fm(lw["in_cq"], lambda kc: xn[:, kc, :Tt], lambda kc: [b_xn[kc]], Tt, evac_cq)

                def evac_kv(sub, p, bp):
                    tok0 = ti * Tt + sub * TS
                    key0 = s.past + tok0
                    kb = s.kv_b[i][key0 // 512]
                    k.actf(jk[:TS, :], p[:TS, 0:256], AF.Square, [bp], [b_kw], accum_out=stq[:TS, 0:1])
                    k.actf(stq[:TS, 1:2], stq[:TS, 0:1], AF.Sqrt, [b_kw], [b_kw], bias=eps_t[:TS, 0:1], scale=1.0 / 256)
                    k.op(k.dve, lambda e: e.reciprocal(out=stq[:TS, 2:3], in_=stq[:TS, 1:2]), [b_kw], [b_kw])
                    k.stt(k.dve, ckvn[:TS, :], p[:TS, 0:256], stq[:TS, 2:3], kvg[:TS, :], ALU.mult, ALU.mult, [bp, b_kw, b_la], [b_kw])
                    k.dma(s.ckv_o[i, tok0:tok0 + TS, :], ckvn[:TS, :], R=[b_kw], W=[obuf()])
                    k.cp(k.act, ckvb[:TS, :], ckvn[:TS, :], [b_kw], [b_kw])
                    k.dma(s.kvtok[i][key0:key0 + TS, :], ckvb[:TS, :], R=[b_kw], W=[kb])
                    p2, bp2 = nps(); pb = psb(p2)
                    for half in range(2):
                        k.tr(pb[:, half * TS:(half + 1) * TS], ckvb[:TS, half * 128:(half + 1) * 128], ident_b[:TS, :TS], [b_kw, b_const], [bp2])
                    k.cp(k.dve, ckvTs[:, :, :TS], pb[:, :2 * TS].rearrange("p (a b) -> p a b", a=2), [bp2], [b_kw])
                    k.dma(s.kvT[i][:, :, key0:key0 + TS], ckvTs[:, :, :TS], R=[b_kw], W=[kb])
                    cosv = cstab[:TS, sub, 0:32]; sinv = cstab[:TS, sub, 32:64]
                    x1 = p[:TS, 256:288]; x2 = p[:TS, 288:320]
                    k.tt(k.dve, tk[0][:TS, 0, :], x1, cosv, ALU.mult, [bp, b_cs], [b_tk[0]])
                    k.tt(k.dve, tk[1][:TS, 0, :], x2, sinv, ALU.mult, [bp, b_cs], [b_tk[1]])
                    k.tt(k.pool, krn[:TS, 0:32], tk[0][:TS, 0, :], tk[1][:TS, 0, :], ALU.subtract, [b_tk[0], b_tk[1]], [b_kw])
                    k.tt(k.dve, tk[2][:TS, 0, :], x1, sinv, ALU.mult, [bp, b_cs], [b_tk[2]])
                    k.tt(k.dve, tk[3][:TS, 0, :], x2, cosv, ALU.mult, [bp, b_cs], [b_tk[3]])
                    k.tt(k.pool, krn[:TS, 32:64], tk[2][:TS, 0, :], tk[3][:TS, 0, :], ALU.add, [b_tk[2], b_tk[3]], [b_kw])
                    k.dma(s.kr_o[i, tok0:tok0 + TS, :], krn[:TS, :], R=[b_kw], W=[obuf()])
                    k.cp(k.act, krb[:TS, :], krn[:TS, :], [b_kw], [b_kw])
                    p3, bp3 = nps(); pb3 = psb(p3)
                    k.tr(pb3[:64, :TS], krb[:TS, :], ident_b[:TS, :TS], [b_kw, b_const], [bp3])
                    k.cp(k.dve, krTs[:, :TS], pb3[:64, :TS], [bp3], [b_kw])
                    k.dma(s.krT[i][:, key0:key0 + TS], krTs[:, :TS], R=[b_kw], W=[kb])
                proj_tm(lw["in_kv"][0], lambda kc, sub: xn[:, kc, sub * TS:(sub + 1) * TS], lambda kc: [b_xn[kc]], TS, nsub, 320, evac_kv)
                rmsnorm_fm(cq, b_cq, 4, lambda c: qag[:, c:c + 1], Tt, cqn, b_cqn)
                v, bw = load_slab(lw["qb_nope"])
                for h in range(NH):
                    p, bp = nps()
                    for c in range(4):
                        k.mm(p[:, :Tt], v[:, c, h * 128:(h + 1) * 128], cqn[:, c, :Tt], c == 0, c == 3, [bw, b_cqn[c]], [bp])
                    x = h % 2
                    k.cp(k.act, qn[x][:, :Tt], p[:, :Tt], [bp], [b_qn[x]])
                    for half in range(2):
                        p2, bp2 = nps()
                        k.mm(p2[:, :Tt], WkT[:, h, half, :], qn[x][:, :Tt], True, True, [b_la, b_qn[x]], [bp2])
                        k.cp(k.dve if half == 0 else k.act, qp[:, h, half, :Tt], p2[:, :Tt], [bp2], [b_qp[h]])

                def evac_qr(sub, p, bp):
                    pv = p[:TS, :512].rearrange("p (h e) -> p h e", h=8)
                    x1 = pv[:, :, 0:32]; x2 = pv[:, :, 32:64]
                    cosb = cstab[:TS, sub:sub + 1, 0:32].broadcast_to([TS, 8, 32]); sinb = cstab[:TS, sub:sub + 1, 32:64].broadcast_to([TS, 8, 32])
                    k.tt(k.dve, tk[0][:TS], x1, cosb, ALU.mult, [bp, b_cs], [b_tk[0]])
                    k.tt(k.dve, tk[1][:TS], x2, sinb, ALU.mult, [bp, b_cs], [b_tk[1]])
                    k.tt(k.pool, qrt[:TS, :, 0:32], tk[0][:TS], tk[1][:TS], ALU.subtract, [b_tk[0], b_tk[1]], [b_qrt])
                    k.tt(k.dve, tk[2][:TS], x1, sinb, ALU.mult, [bp, b_cs], [b_tk[2]])
                    k.tt(k.dve, tk[3][:TS], x2, cosb, ALU.mult, [bp, b_cs], [b_tk[3]])
                    k.tt(k.pool, qrt[:TS, :, 32:64], tk[2][:TS], tk[3][:TS], ALU.add, [b_tk[2], b_tk[3]], [b_qrt])
                    p3, bp3 = nps(); pb3 = psb(p3)
                    for h in range(NH):
                        k.tr(pb3[:64, h * TS:(h + 1) * TS], qrt[:TS, h, :], ident_b[:TS, :TS], [b_qrt, b_const], [bp3])
                    k.cp(k.act, qrT[:, :, sub * TS:(sub + 1) * TS], pb3[:64, :8 * TS].rearrange("p (a b) -> p a b", a=8), [bp3], [b_qrT])
                proj_tm([lw["qb_rope"]], lambda kc, sub: cqn[:, kc, sub * TS:(sub + 1) * TS], lambda kc: [b_cqn[kc]], TS, nsub, 512, evac_qr)
                kbase = s.past + ti * Tt
                blocks = [(b * 128, 128, 0, False) for b in range(kbase // 128)]
                if Tt >= 128:
                    blocks += [(kbase + j * 128, 128, j * 128, True) for j in range(Tt // 128)]
                else:
                    blocks += [(kbase, Tt, 0, False)]
                sbl = {}
                for bi, bl in enumerate(blocks):
                    sbl.setdefault(bl[0] // 512, []).append((bi, bl))
                NKB = 3
                kvTb = [sbs(ea, [128, 2, 512], BF16) for _ in range(NKB)]; krTb = [sbs(ea, [64, 512], BF16) for _ in range(NKB)]
                kvkb = [sbs(ea, [128, 4, 256], BF16) for _ in range(NKB)]; b_kvs = bufs(NKB)
                PTb = [sbs(ea, [128, TP], BF16) for _ in range(3)]; b_PTb = bufs(3)
                recip = sbs(ea, [128, TP], F32); b_rc = Buf(); OLn = sbs(ea, [128, 2, TP], BF16); b_OLn = Buf()
                steps = []
                for h in range(NH):
                    for sbi in sorted(sbl):
                        for n_, (bi, bl) in enumerate(sbl[sbi]):
                            steps.append((h, sbi, n_ == 0, bi, bl))
                kvc = [0]; ptc = [0]
                cur = {}

                def emit_S(stp):
                    h, sbi, first_sb, bi, (ks, kn, qlo, diag) = stp
                    if first_sb:
                        w = kvc[0] % NKB; kvc[0] += 1
                        lst = sbl[sbi]
                        k0 = sbi * 512
                        nk = sum(bl[1] for _, bl in lst)
                        kbuf = s.kv_b[i][sbi]
                        k.dma(kvTb[w][:, :, :nk], s.kvT[i][:, :, k0:k0 + nk], R=[kbuf], W=[b_kvs[w]])
                        k.dma(krTb[w][:, :nk], s.krT[i][:, k0:k0 + nk], R=[kbuf], W=[b_kvs[w]])
                        if nk == 512:
                            k.dma(kvkb[w][:, :, :], s.kvtok[i][k0:k0 + 512, :].rearrange("(b p) c -> p b c", p=128), R=[kbuf], W=[b_kvs[w]])
                        else:
                            for (_, bl) in lst:
                                o_ = bl[0] - k0
                                k.dma(kvkb[w][:bl[1], o_ // 128, :], s.kvtok[i][bl[0]:bl[0] + bl[1], :], R=[kbuf], W=[b_kvs[w]])
                        cur["w"] = w
                    w = cur["w"]
                    k0 = sbi * 512
                    o = ks - k0
                    qn_ = Tt - qlo
                    pS, bS = nps()
                    k.mm(pS[:kn, :qn_], kvTb[w][:, 0, o:o + kn], qp[:, h, 0, qlo:Tt], True, False, [b_kvs[w], b_qp[h]], [bS])
                    k.mm(pS[:kn, :qn_], kvTb[w][:, 1, o:o + kn], qp[:, h, 1, qlo:Tt], False, False, [b_kvs[w], b_qp[h]], [bS])
                    k.mm(pS[:kn, :qn_], krTb[w][:, o:o + kn], qrT[:, h, qlo:Tt], False, True, [b_kvs[w], b_qrT], [bS])
                    x = ptc[0] % 3; ptc[0] += 1
                    k.actf(PTb[x][:kn, :qn_], pS[:kn, :qn_], AF.Exp, [bS], [b_PTb[x]], scale=MLA_SCALE)
                    if diag:
                        k.memset(k.pool, PTb[x][64:128, 0:64], 0.0, [], [b_PTb[x]])
                    return (w, o, x)

                acc = {}

                def emit_PV(stp, info):
                    h, sbi, first_sb, bi, (ks, kn, qlo, diag) = stp
                    w, o, x = info
                    qn_ = Tt - qlo
                    first = bi == 0; last = bi == len(blocks) - 1
                    if first:
                        acc["a"] = npa(3)
                    (pO0, bO0), (pO1, bO1), (pSm, bSm) = acc["a"]
                    k.mm(pO0[:, qlo:Tt], kvkb[w][:kn, o // 128, 0:128], PTb[x][:kn, :qn_], first, last, [b_kvs[w], b_PTb[x]], [bO0])
                    k.mm(pO1[:, qlo:Tt], kvkb[w][:kn, o // 128, 128:256], PTb[x][:kn, :qn_], first, last, [b_kvs[w], b_PTb[x]], [bO1])
                    k.mm(pSm[:, qlo:Tt], ones_b[:kn, :], PTb[x][:kn, :qn_], first, last, [b_const, b_PTb[x]], [bSm])
                    if last:
                        k.op(k.dve, lambda e: e.reciprocal(out=recip[:, :Tt], in_=pSm[:, :Tt]), [bSm], [b_rc])
                        k.tt(k.dve, OLn[:, 0, :Tt], pO0[:, :Tt], recip[:, :Tt], ALU.mult, [bO0, b_rc], [b_OLn])
                        k.tt(k.dve, OLn[:, 1, :Tt], pO1[:, :Tt], recip[:, :Tt], ALU.mult, [bO1, b_rc], [b_OLn])
                        pA, bA = nps()
                        for half in range(2):
                            k.mm(pA[:, :Tt], wkvn[:, half, h * 256 + 128:h * 256 + 256], OLn[:, half, :Tt], half == 0, half == 1, [b_la, b_OLn], [bA])
                        k.cp(k.act, mixT[:, h, :Tt], pA[:, :Tt], [bA], [b_mix[h]])

                pend = None
                for stp in steps + [None]:
                    info = emit_S(stp) if stp is not None else None
                    if pend is not None:
                        emit_PV(*pend)
                    pend = (stp, info) if stp is not None else None
                k.barrier()
            proj_fm(lw["out"], lambda kc: mixT[:, kc, :Tt], lambda kc: [b_mix[kc]], Tt, add_to_x(Tt))
            k.barrier()

    def tile_begin(l, s, ti):
        Tt = s.T
        if l == 0:
            load_x0(s, ti)
        else:
            k.dma(xt[:, :, :Tt], s.xscr[ti], R=[s.xscr_b[ti]], W=b_xt)

    def tile_end(l, s, ti):
        Tt = s.T
        if DO_MLP:
            mlp(l, Tt)
        if l == DEPTH - 1:
            final_out(s, ti)
        else:
            k.dma(s.xscr[ti], xt[:, :, :Tt], R=b_xt, W=[s.xscr_b[ti]])

    for l in range(DEPTH):
        if types[l] == "ab":
            layer_ab(l)
        elif types[l] == "c":
            layer_c(l)
        else:
            for s in seqs:
                for ti in range(s.nt):
                    tile_begin(l, s, ti)
                    tile_end(l, s, ti)
    k.finish(out_bufs)
    k.barrier()
    return nc, k


_WNAMES = ["norm_mix", "norm_mlp", "norm_final", "w_in_ab", "q_a_norm", "kv_a_norm", "w_q_b", "w_kv_b",
           "s5_lam_re", "s5_lam_im", "s5_log_dt", "s5_b_re", "s5_b_im", "s5_c_re", "s5_c_im", "s5_d", "w_glu", "b_glu",
           "w_out_ab", "w_in_c", "ret_gn", "w_out_c", "w_up", "w_down"]


def run_cfg(cfg, inputs, trace=False):
    f = lambda a: np.ascontiguousarray(np.asarray(a), dtype=np.float32)
    xp_all = f(inputs["x_prompt"]); xs_all = f(inputs["x_sample"])
    B, SEQ, _ = xp_all.shape
    DB, DS, _ = xs_all.shape
    PAST = inputs["cache_mla_ckv"].shape[2]
    cfg = dict(cfg); cfg.update(SEQ=SEQ, DS=DS, PAST=PAST)
    if not cfg.get("mlp", True):
        inputs = dict(inputs); inputs["w_up"] = np.zeros((1, 1, 1), np.float32); inputs["w_down"] = np.zeros((1, 1, 1), np.float32)
    hc = host_consts(max(SEQ, PAST + DS))
    cfg["gL"] = hc["gL"]
    nc, k = build(cfg)
    k.nc = nc
    NABd = max(1, sum(1 for t in cfg["types"] if t == "ab")); NCd = max(1, sum(1 for t in cfg["types"] if t == "c"))
    w = {n: f(inputs[n]) for n in _WNAMES}
    def pad0(a, n):
        a = f(a)
        if a.shape[0] == 0:
            return np.zeros((n,) + a.shape[1:], np.float32)
        return a
    for n in list(w):
        if w[n].shape[0] == 0:
            w[n] = np.zeros((1,) + w[n].shape[1:], np.float32)
    consts = {"h_mla_cs": hc["mla_cs"], "h_ret_cos": hc["ret_cos"], "h_ret_sin": hc["ret_sin"], "h_DTt": hc["DTt"],
              "h_dq": hc["dq_rep"], "h_kdec128": hc["kdec128"], "h_kdec64": hc["kdec64"],
              "h_ident_f": hc["ident_f"], "h_ident_b": hc["ident_b"]}
    ckv = pad0(inputs["cache_mla_ckv"], 1); ckr = pad0(inputs["cache_mla_krope"], 1)
    s5r = pad0(inputs["state_s5_re"], 1); s5i = pad0(inputs["state_s5_im"], 1); rst = pad0(inputs["state_ret"], 1)
    in_maps = []
    for c in range(8):
        m = {"xp": xp_all[c % B], "xs": xs_all[c % DB], "c_ckv": np.ascontiguousarray(ckv[:, c % DB]),
             "c_kr": np.ascontiguousarray(ckr[:, c % DB]), "s5re_in": np.ascontiguousarray(s5r[:, c % DB]),
             "s5im_in": np.ascontiguousarray(s5i[:, c % DB]), "ret_in": np.ascontiguousarray(rst[:, c % DB])}
        m.update(w); m.update(consts)
        in_maps.append(m)
    for m in in_maps:
        for n in list(m):
            shp = k.in_shapes.get(n)
            if shp is not None and tuple(m[n].shape) != tuple(shp):
                m[n] = np.zeros(shp, m[n].dtype)
    global _LAST_IN_MAPS
    _LAST_IN_MAPS = in_maps
    if cfg.get("build_only"):
        return None, None, k
    res = run_bass_kernel_spmd(nc, in_maps, core_ids=list(range(8)), trace=trace)
    R = res.results
    NAB = sum(1 for t in cfg["types"] if t == "ab"); NC_ = sum(1 for t in cfg["types"] if t == "c")
    def gat(name, n, cores, lay):
        a = np.stack([np.asarray(R[c][name], dtype=np.float32) for c in cores], axis=0)
        if lay:
            a = np.swapaxes(a, 0, 1)[:n]
        return np.ascontiguousarray(a)
    pc = list(range(B)); sc = list(range(DB))
    outs = (gat("y_p", 0, pc, False), gat("y_s", 0, sc, False),
            gat("ckv_p", NAB, pc, True), gat("kr_p", NAB, pc, True), gat("s5re_p", NAB, pc, True), gat("s5im_p", NAB, pc, True),
            gat("ret_p", NC_, pc, True),
            gat("ckv_s", NAB, sc, True), gat("kr_s", NAB, sc, True), gat("s5re_s", NAB, sc, True), gat("s5im_s", NAB, sc, True),
            gat("ret_s", NC_, sc, True))
    return outs, res, k


def kernel(**inputs):
    depth = np.asarray(inputs["norm_mix"]).shape[0]
    cfg = {"DEPTH": depth, "types": ["ab" if l % 2 == 0 else "c" for l in range(depth)]}
    outs, _, _ = run_cfg(cfg, inputs)
    return outs
```
